# Optimizing a Trainium2 kernel written in Bass

```python
import math
import jax, jax.numpy as jnp
from jax import lax
import numpy as np

D_MODEL = 1024
BATCH = 4
SEQ = 4096
DEPTH = 1
DEC_BATCH = 128
DEC_SEQ = 8
PAST_LEN = 2048
PAGE_SIZE = 128

HEAD_DIM = 64
HEADS_PER_GROUP = 4
ATTN_GROUPS = ((128, 1), (512, 4), (2048, 16))
N_GROUPS = len(ATTN_GROUPS)
GROUP_WIDTH = HEADS_PER_GROUP * HEAD_DIM
WIN_STEPS = 128
BLOCK = WIN_STEPS
D_CONV = D_MODEL // 2
CONV_W = 3
D_FF = 4 * D_MODEL
ROPE_THETA = 10000.0
ALPHA = (2.0 * DEPTH) ** 0.25
BETA = (8.0 * DEPTH) ** -0.25
LN_EPS = 1e-5
ATTN_SCALE = HEAD_DIM ** -0.5
N_IN = 3 * D_CONV + 3 * N_GROUPS * GROUP_WIDTH + 2 * D_MODEL

kernel_name = 'hybrid_shortconv_dilated_swa_decoder_step'


def layer_norm(x, g, b):
    xf = x.astype(jnp.float32)
    mu = jnp.mean(xf, axis=-1, keepdims=True)
    var = jnp.mean(jnp.square(xf - mu), axis=-1, keepdims=True)
    return ((xf - mu) * lax.rsqrt(var + LN_EPS) * g.astype(jnp.float32) + b.astype(jnp.float32)).astype(x.dtype)


def rope(x, pos):
    inv = 1.0 / (ROPE_THETA ** (jnp.arange(0, HEAD_DIM, 2, dtype=jnp.float32) / HEAD_DIM))
    ang = pos.astype(jnp.float32)[:, None] * inv[None, :]
    cos = jnp.cos(ang)[None, :, None, :]
    sin = jnp.sin(ang)[None, :, None, :]
    xf = x.astype(jnp.float32)
    x1, x2 = xf[..., :HEAD_DIM // 2], xf[..., HEAD_DIM // 2:]
    return jnp.concatenate([x1 * cos - x2 * sin, x2 * cos + x1 * sin], axis=-1).astype(x.dtype)


def masked_softmax_lse(s, valid):
    s = jnp.where(valid, s, -jnp.inf)
    m = jnp.max(s, axis=-1, keepdims=True)
    p = jnp.exp(s - m)
    denom = jnp.sum(p, axis=-1, keepdims=True)
    return p / denom, (m + jnp.log(denom))[..., 0]


def dilated_attn_prompt(q, k, v, dil):
    n, s_len, h, e = q.shape
    L = s_len // dil
    nb = -(-L // BLOCK)
    lp = nb * BLOCK

    def to_blocks(t):
        t = t.reshape(n, L, dil, h, e).transpose(0, 2, 1, 3, 4)
        t = jnp.pad(t, ((0, 0), (0, 0), (0, lp - L), (0, 0), (0, 0)))
        return t.reshape(n, dil, nb, BLOCK, h, e)

    def with_prev(t):
        prev = jnp.pad(t[:, :, :-1], ((0, 0), (0, 0), (1, 0), (0, 0), (0, 0), (0, 0)))
        return jnp.concatenate([prev, t], axis=3)

    qb = to_blocks(q)
    kk = with_prev(to_blocks(k))
    vv = with_prev(to_blocks(v))
    s = jnp.einsum('nrbqhe,nrbkhe->nrbhqk', qb, kk, preferred_element_type=jnp.float32) * ATTN_SCALE
    qi = jnp.arange(nb)[:, None] * BLOCK + jnp.arange(BLOCK)[None, :]
    ki = (jnp.arange(nb)[:, None] - 1) * BLOCK + jnp.arange(2 * BLOCK)[None, :]
    dist = qi[:, :, None] - ki[:, None, :]
    valid = (dist >= 0) & (dist <= WIN_STEPS) & (ki[:, None, :] >= 0)
    p, lse = masked_softmax_lse(s, valid[:, None])
    o = jnp.einsum('nrbhqk,nrbkhe->nrbqhe', p.astype(v.dtype), vv)
    o = o.reshape(n, dil, lp, h, e)[:, :, :L].transpose(0, 2, 1, 3, 4).reshape(n, s_len, h, e)
    lse = lse.transpose(0, 1, 2, 4, 3).reshape(n, dil, lp, h)[:, :, :L].transpose(0, 2, 1, 3).reshape(n, s_len, h)
    return o, lse


def dilated_attn_sample(q, k, v, kv_prev, dil, window):
    n, t_len, h, e = q.shape
    w = kv_prev.shape[1]
    kv_all = jnp.concatenate([kv_prev.astype(k.dtype), jnp.stack([k, v], axis=2)], axis=1)
    idx = w + jnp.arange(t_len)[:, None] - jnp.arange(WIN_STEPS + 1)[None, :] * dil
    valid = idx >= 0
    g = kv_all[:, jnp.maximum(idx, 0)]
    s = jnp.einsum('nthe,ntjhe->nthj', q, g[:, :, :, 0], preferred_element_type=jnp.float32) * ATTN_SCALE
    p, lse = masked_softmax_lse(s, valid[None, :, None, :])
    o = jnp.einsum('nthj,ntjhe->nthe', p.astype(v.dtype), g[:, :, :, 1])
    new_kv = kv_all[:, -min(window, w + t_len):]
    return o, lse, new_kv


def short_conv(gb, gc, hc, conv_prev, conv_w):
    u = gc * hc
    u_pad = jnp.concatenate([conv_prev.astype(u.dtype), u], axis=1)
    t_len = u.shape[1]
    y = conv_w[0] * u_pad[:, 0:t_len]
    for j in range(1, CONV_W):
        y = y + conv_w[j] * u_pad[:, j:j + t_len]
    return gb * y, u_pad[:, -(CONV_W - 1):]


def decoder_layer(x, pos, conv_prev, kv_prev, w_in, b_gate, conv_w, w_conv_out, w_attn_out, w_o,
                  ln1_g, ln1_b, w_up, w_down, ln2_g, ln2_b):
    n, t_len, _ = x.shape
    z = jnp.einsum('btd,dn->btn', x, w_in)
    gb = z[..., 0:D_CONV]
    gc = z[..., D_CONV:2 * D_CONV]
    hc = z[..., 2 * D_CONV:3 * D_CONV]
    yc, conv_state = short_conv(gb, gc, hc, conv_prev, conv_w)

    outs, lses, kv_new = [], [], []
    for gi, (window, dil) in enumerate(ATTN_GROUPS):
        off = 3 * D_CONV + gi * 3 * GROUP_WIDTH
        q = rope(z[..., off:off + GROUP_WIDTH].reshape(n, t_len, HEADS_PER_GROUP, HEAD_DIM), pos)
        k = rope(z[..., off + GROUP_WIDTH:off + 2 * GROUP_WIDTH].reshape(n, t_len, HEADS_PER_GROUP, HEAD_DIM), pos)
        v = z[..., off + 2 * GROUP_WIDTH:off + 3 * GROUP_WIDTH].reshape(n, t_len, HEADS_PER_GROUP, HEAD_DIM)
        if kv_prev is None:
            o, lse = dilated_attn_prompt(q, k, v, dil)
            kv = jnp.stack([k, v], axis=2)[:, -min(window, t_len):]
        else:
            o, lse, kv = dilated_attn_sample(q, k, v, kv_prev[gi], dil, window)
        outs.append(o)
        lses.append(lse)
        kv_new.append(kv)

    wts = jax.nn.softmax(jnp.stack(lses, axis=0), axis=0)
    ya = wts[0][..., None] * outs[0]
    for gi in range(1, N_GROUPS):
        ya = ya + wts[gi][..., None] * outs[gi]
    ya = ya.astype(x.dtype).reshape(n, t_len, GROUP_WIDTH)

    gates = jax.nn.sigmoid(z[..., -2 * D_MODEL:] + b_gate)
    mixed = gates[..., :D_MODEL] * (yc @ w_conv_out) + gates[..., D_MODEL:] * (ya @ w_attn_out)
    x1 = layer_norm(ALPHA * x + mixed @ w_o, ln1_g, ln1_b)
    hid = jnp.square(jax.nn.relu(x1 @ w_up))
    x2 = layer_norm(ALPHA * x1 + hid @ w_down, ln2_g, ln2_b)
    return x2, conv_state, kv_new


def setup_inputs(seed: int = 0) -> dict:
    key = jax.random.key(seed)
    ks = jax.random.split(key, 20)
    f32 = jnp.float32

    def nrm(k, shape, scale):
        return jax.random.normal(k, shape, f32) * scale

    def kv_cache(k, window):
        return nrm(k, (DEPTH, DEC_BATCH, min(window, PAST_LEN), 2, HEADS_PER_GROUP, HEAD_DIM), 1.0)

    col_scale = jnp.concatenate(
        [jnp.ones((2 * D_CONV,), f32), jnp.full((D_CONV,), BETA, f32)]
        + [jnp.concatenate([jnp.ones((2 * GROUP_WIDTH,), f32), jnp.full((GROUP_WIDTH,), BETA, f32)]) for _ in range(N_GROUPS)]
        + [jnp.ones((2 * D_MODEL,), f32)])
    return {
        'x_prompt': nrm(ks[0], (BATCH, SEQ, D_MODEL), 1.0),
        'x_sample': nrm(ks[1], (DEC_BATCH, DEC_SEQ, D_MODEL), 1.0),
        'state_conv': nrm(ks[2], (DEPTH, DEC_BATCH, CONV_W - 1, D_CONV), 1.0),
        'cache_kv_w128': kv_cache(ks[3], ATTN_GROUPS[0][0]),
        'cache_kv_w512': kv_cache(ks[4], ATTN_GROUPS[1][0]),
        'cache_kv_w2048': kv_cache(ks[5], ATTN_GROUPS[2][0]),
        'w_in': nrm(ks[6], (DEPTH, D_MODEL, N_IN), D_MODEL ** -0.5) * col_scale,
        'b_gate': nrm(ks[7], (DEPTH, 2 * D_MODEL), 0.02),
        'conv_w': nrm(ks[8], (DEPTH, CONV_W, D_CONV), CONV_W ** -0.5),
        'w_conv_out': nrm(ks[9], (DEPTH, D_CONV, D_MODEL), BETA * D_CONV ** -0.5),
        'w_attn_out': nrm(ks[10], (DEPTH, GROUP_WIDTH, D_MODEL), BETA * GROUP_WIDTH ** -0.5),
        'w_o': nrm(ks[11], (DEPTH, D_MODEL, D_MODEL), BETA * D_MODEL ** -0.5),
        'ln1_g': 1.0 + nrm(ks[12], (DEPTH, D_MODEL), 0.02),
        'ln1_b': nrm(ks[13], (DEPTH, D_MODEL), 0.02),
        'w_up': nrm(ks[14], (DEPTH, D_MODEL, D_FF), BETA * D_MODEL ** -0.5),
        'w_down': nrm(ks[15], (DEPTH, D_FF, D_MODEL), BETA * D_FF ** -0.5),
        'ln2_g': 1.0 + nrm(ks[16], (DEPTH, D_MODEL), 0.02),
        'ln2_b': nrm(ks[17], (DEPTH, D_MODEL), 0.02),
    }


def reference(x_prompt, x_sample, state_conv, cache_kv_w128, cache_kv_w512, cache_kv_w2048,
              w_in, b_gate, conv_w, w_conv_out, w_attn_out, w_o, ln1_g, ln1_b, w_up, w_down, ln2_g, ln2_b):
    n_p, s_p, _ = x_prompt.shape
    t_s = x_sample.shape[1]
    pos_p = jnp.arange(s_p, dtype=jnp.int32)
    pos_s = PAST_LEN + jnp.arange(t_s, dtype=jnp.int32)
    conv_zero = jnp.zeros((n_p, CONV_W - 1, D_CONV), x_prompt.dtype)
    hp, hs = x_prompt, x_sample
    cp_list, cs_list, kvp_list, kvs_list = [], [], [], []
    for l in range(DEPTH):
        weights = (w_in[l], b_gate[l], conv_w[l], w_conv_out[l], w_attn_out[l], w_o[l],
                   ln1_g[l], ln1_b[l], w_up[l], w_down[l], ln2_g[l], ln2_b[l])
        hp, cp, kvp = decoder_layer(hp, pos_p, conv_zero, None, *weights)
        hs, cs, kvs = decoder_layer(hs, pos_s, state_conv[l],
                                    (cache_kv_w128[l], cache_kv_w512[l], cache_kv_w2048[l]), *weights)
        cp_list.append(cp)
        cs_list.append(cs)
        kvp_list.append(kvp)
        kvs_list.append(kvs)
    conv_p = jnp.stack(cp_list, axis=0)
    conv_s = jnp.stack(cs_list, axis=0)
    kvp_g = [jnp.stack([kv[g] for kv in kvp_list], axis=0) for g in range(N_GROUPS)]
    kvs_g = [jnp.stack([kv[g] for kv in kvs_list], axis=0) for g in range(N_GROUPS)]
    return (hp, hs, conv_p, kvp_g[0], kvp_g[1], kvp_g[2], conv_s, kvs_g[0], kvs_g[1], kvs_g[2])
```

```python
import numpy as np
from contextlib import ExitStack
import concourse.bass as bass
import concourse.mybir as mybir
from concourse.bass_utils import run_bass_kernel_spmd

F32 = mybir.dt.float32
BF16 = mybir.dt.bfloat16
ALU = mybir.AluOpType
AF = mybir.ActivationFunctionType
AX = mybir.AxisListType

NCORES = 8
D = 1024
KC = 8
TP = 2048
TS = 128
T = TP + TS
PF = 2048
NSEQ = 16
DIL = (1, 4, 16)
WIN = (128, 512, 2048)
ALPHA = 2.0 ** 0.25
LN_EPS = 1e-5
SCALE = 0.125
def _sl(lo, n, step):
    return slice(lo, lo + (n - 1) * step + 1, step)


BLOCKS = [(0, 512), (512, 512), (1024, 512), (1536, 512), (2048, 128)]
STOP_AFTER = None


class Buf:
    __slots__ = ("name", "w", "r")

    def __init__(self, name, inherit=None):
        self.name = name
        self.w = dict(inherit) if inherit else {}
        self.r = dict(inherit) if inherit else {}


def _merge(d, s):
    for k, v in s.items():
        if d.get(k, 0) < v:
            d[k] = v


class Tracker:
    def __init__(self, nc, es):
        self.nc = nc
        self.es = es
        self.eng = {"pe": nc.tensor, "act": nc.scalar, "dve": nc.vector, "pool": nc.gpsimd, "sp": nc.sync}
        self.sems = {}
        self.cnt = {}
        self.seen = {e: {} for e in self.eng}
        for e in ("pe", "act", "dve", "pool"):
            self.new_sem(e)
        self.grave = {}
        self.phase_bufs = [[]]
        self.n_wait = 0

    def new_sem(self, key):
        self.sems[key] = self.es.enter_context(self.nc.semaphore("s_" + key))
        self.cnt[key] = 0
        return key

    def buf(self, name):
        b = Buf(name, self.grave)
        self.phase_bufs[-1].append(b)
        return b

    def bufs(self, name, n):
        return [self.buf("%s%d" % (name, i)) for i in range(n)]

    def push(self):
        self.phase_bufs.append([])

    def pop(self):
        for b in self.phase_bufs.pop():
            _merge(self.grave, b.w)
            _merge(self.grave, b.r)

    def retire(self, bufs):
        for b in bufs:
            _merge(self.grave, b.w)
            _merge(self.grave, b.r)

    def _wait(self, e, deps):
        seen = self.seen[e]
        for k, v in deps.items():
            if v <= 0 or seen.get(k, 0) >= v:
                continue
            if e == "pe" and k == "pe":
                continue
            self.eng[e].wait_ge(self.sems[k], v)
            self.n_wait += 1
            seen[k] = v

    def _deps(self, reads, writes):
        d = {}
        for b in reads:
            _merge(d, b.w)
        for b in writes:
            _merge(d, b.w)
            _merge(d, b.r)
        return d

    def _record(self, key, val, reads, writes):
        for b in reads:
            if b.r.get(key, 0) < val:
                b.r[key] = val
        for b in writes:
            b.w = {key: val}
            b.r = {}

    def op(self, e, fn, reads=(), writes=()):
        self._wait(e, self._deps(reads, writes))
        ins = fn(self.eng[e])
        self.cnt[e] += 1
        ins.then_inc(self.sems[e], 1)
        self._record(e, self.cnt[e], reads, writes)

    def group(self, e, fns, reads=(), writes=()):
        self._wait(e, self._deps(reads, writes))
        ins = None
        for fn in fns:
            ins = fn(self.eng[e])
        self.cnt[e] += 1
        ins.then_inc(self.sems[e], 1)
        self._record(e, self.cnt[e], reads, writes)

    def dma(self, e, semkey, out, in_, reads=(), writes=()):
        self._wait(e, self._deps(reads, writes))
        ins = self.eng[e].dma_start(out=out, in_=in_)
        self.cnt[semkey] += 16
        ins.then_inc(self.sems[semkey], 16)
        self._record(semkey, self.cnt[semkey], reads, writes)

    def wait_all(self, e):
        d = {k: v for k, v in self.cnt.items()}
        self._wait(e, d)


class Arena:
    def __init__(self, nc, es, nbytes):
        self.cap = nbytes
        self.t = es.enter_context(nc.sbuf_tensor("arena", [128, nbytes // 4], F32))
        self.free_list = [(0, nbytes)]
        self.scopes = []
        self.used = 0
        self.peak = 0
        self.handles = {}

    def push(self):
        self.scopes.append([])

    def pop(self):
        for h in self.scopes.pop():
            self._free(h)

    def _free(self, h):
        off, n = h
        self.used -= n
        fl = self.free_list + [(off, n)]
        fl.sort()
        out = []
        for o, l in fl:
            if out and out[-1][0] + out[-1][1] == o:
                out[-1] = (out[-1][0], out[-1][1] + l)
            else:
                out.append((o, l))
        self.free_list = out

    def free(self, ap):
        self._free(self.handles.pop(id(ap)))

    def alloc(self, dtype, *dims, keep=False, high=False):
        n = 1
        for x in dims:
            n *= x
        esz = 4 if dtype == F32 else 2
        nbytes = (n * esz + 63) // 64 * 64
        cands = [i for i, (o, l) in enumerate(self.free_list) if l >= nbytes]
        assert cands, "SBUF arena overflow: need %d, free=%s" % (nbytes, self.free_list)
        i = cands[-1] if high else cands[0]
        o, l = self.free_list[i]
        if high:
            off = o + l - nbytes
            self.free_list[i] = (o, l - nbytes)
        else:
            off = o
            self.free_list[i] = (o + nbytes, l - nbytes)
        self.free_list = [(a, b) for a, b in self.free_list if b > 0]
        self.used += nbytes
        self.peak = max(self.peak, self.used)
        ap = self.t[:, off // 4: off // 4 + nbytes // 4]
        if dtype != F32:
            ap = ap.bitcast(dtype)
        ap = ap[:, 0:n]
        if len(dims) == 2:
            ap = ap.rearrange("p (a b) -> p a b", a=dims[0])
        elif len(dims) == 3:
            ap = ap.rearrange("p (a b c) -> p a b c", a=dims[0], b=dims[1])
        elif len(dims) == 4:
            ap = ap.rearrange("p (a b c d) -> p a b c d", a=dims[0], b=dims[1], c=dims[2])
        if keep or not self.scopes:
            self.handles[id(ap)] = (off, nbytes)
            self._keepalive = getattr(self, "_keepalive", []) + [ap]
        else:
            self.scopes[-1].append((off, nbytes))
        return ap


def build_program(stop_after=None):
    nc = bass.Bass("TRN2", target_bir_lowering=False)

    def din(name, shape):
        return nc.dram_tensor(name, list(shape), F32, kind="ExternalInput").ap()

    def dout(name, shape):
        return nc.dram_tensor(name, list(shape), F32, kind="ExternalOutput").ap()

    xm = din("xm", [T, D])
    xp = din("xp", [PF, D])
    wqk = din("wqk", [3, 4, D, 128])
    wv = din("wv", [3, D, 256])
    wcv = din("wcv", [4, 3, D, 128])
    wg = din("wg", [8, 2, D, 128])
    wco = din("wco", [512, D])
    wao = din("wao", [256, D])
    wo = din("wo", [D, D])
    wup = din("wup", [D, 4096])
    wdn = din("wdn", [4096, D])
    bgt = din("bgt", [128, 16])
    cwt = din("cwt", [128, 12])
    lnp = din("lnp", [4, D])
    stc = din("stc", [32, 512])
    caches = [din("c128", [NSEQ, 128, 512]), din("c512", [NSEQ, 512, 512]), din("c2048", [NSEQ, 2048, 512])]
    cosm = din("cosm", [128, T])
    sinm = din("sinm", [128, T])
    cosp = din("cosp", [128, PF])
    sinp = din("sinp", [128, PF])
    identf_d = din("identf", [128, 128])
    masks_d = din("masks", [128, 384])
    smc_d = din("smc", [128, 13 * 32])
    smn_d = din("smn", [128, NSEQ * 3 * 32])
    maskd_d = din("maskd", [128, 256])

    y_o = dout("y", [T, D])
    convp_o = dout("convp", [2, 512])
    kvp_o = [dout("kv128p", [128, 512]), dout("kv512p", [512, 512]), dout("kv2048p", [2048, 512])]
    convs_o = dout("convs", [32, 512])
    kvs_o = [dout("kv128s", [NSEQ, 128, 512]), dout("kv512s", [NSEQ, 512, 512]), dout("kv2048s", [NSEQ, 2048, 512])]

    dbg_o = dout("dbg", [128, 8 * T]) if stop_after in ("attn", "sattn", "conv", "mix") else None
    es = ExitStack()
    with es:
        tr = Tracker(nc, es)
        ar = Arena(nc, es, 206 * 1024)
        ps = [es.enter_context(nc.psum_tensor("ps%d" % i, [128, 512], F32)) for i in range(8)]
        psb = tr.bufs("ps", 8)
        rr = [0]
        bank_limit = [8]

        def next_bank():
            i = rr[0] % bank_limit[0]
            rr[0] += 1
            return i

        def stop(name):
            return stop_after is not None and stop_after == name

        class Slot:
            def __init__(self, name, ap):
                self.ap = ap
                self.b = tr.buf(name)
                self.sem = tr.new_sem("d_" + name)

        def load(slot, src, eng="pool", dst=None, extra_w=()):
            tr.dma(eng, slot.sem, dst if dst is not None else slot.ap, src, reads=(), writes=(slot.b,) + tuple(extra_w))

        cp_sem = tr.new_sem("cpy")
        cp_pending = []
        import os as _os
        for g in range(3):
            W = WIN[g]
            rows_per = min(W - 8, 512)
            for n in range(NSEQ):
                r0 = 0
                while r0 < W - 8:
                    nr = min(rows_per, W - 8 - r0)
                    cp_pending.append((kvs_o[g][n, r0:r0 + nr, :], caches[g][n, 8 + r0:8 + r0 + nr, :]))
                    r0 += nr
        cp_pending.reverse()

        def copy_some(k, dep=None):
            for _ in range(k):
                if not cp_pending:
                    return
                o_, i_ = cp_pending.pop()
                tr.dma("sp", cp_sem, o_, i_, reads=(dep,) if dep is not None else ())
                if dep is not None:
                    dep.r.pop(cp_sem, None)

        csem = tr.new_sem("const")
        csem2 = tr.new_sem("const2")
        cb = tr.buf("const")
        identf = ar.alloc(F32, 128)
        identb = ar.alloc(BF16, 128)
        masks = ar.alloc(BF16, 384)
        smc = ar.alloc(BF16, 13, 32)
        smn = ar.alloc(BF16, NSEQ, 3, 32)
        maskd = ar.alloc(F32, 4, 64)
        bg = ar.alloc(F32, 16)
        cw = ar.alloc(F32, 12)
        epsb = ar.alloc(F32, 1)
        tr.dma("sp", csem, identf, identf_d)
        tr.dma("sp", csem, bg, bgt)
        tr.dma("sp", csem, cw, cwt)
        tr.dma("sp", csem, maskd, maskd_d.rearrange("p (a b) -> p a b", a=4))
        tr.dma("pool", csem2, identb, identf_d)
        tr.dma("pool", csem2, masks, masks_d)
        tr.dma("pool", csem2, smc, smc_d.rearrange("p (a b) -> p a b", a=13))
        tr.dma("pool", csem2, smn, smn_d.rearrange("p (a b c) -> p a b c", a=NSEQ, b=3))
        cb.w = {csem: tr.cnt[csem], csem2: tr.cnt[csem2]}
        ones64 = ar.alloc(BF16, 64)
        tr.op("dve", lambda e: e.memset(ones64, 1.0), writes=(cb,))
        cb.w = {csem: tr.cnt[csem], csem2: tr.cnt[csem2], "dve": tr.cnt["dve"]}
        epsb_b = tr.buf("epsb")
        tr.op("dve", lambda e: e.memset(epsb, LN_EPS), writes=(epsb_b,))

        xT = ar.alloc(BF16, KC, T)
        xTb = [[tr.buf("xT%d_%d" % (k, j)) for j in range(len(BLOCKS))] for k in range(1)]
        xT_b = tr.bufs("xTblk", T // 128)
        xpl2 = ar.alloc(BF16, KC, 2)
        xpl2_b = tr.buf("xpl2")
        yaT = ar.alloc(BF16, 2, T)
        yaT_b = tr.buf("yaT")
        qTs = [ar.alloc(BF16, 2, TS) for _ in range(3)]
        kTs = [ar.alloc(BF16, 2, TS) for _ in range(3)]
        Vs = [ar.alloc(BF16, 320) for _ in range(3)]
        st_b = [tr.buf("stash%d" % g) for g in range(3)]

        def xtiles(t0, n):
            return xT_b[t0 // 128:(t0 + n + 127) // 128]

        ar.push(); tr.push()
        kTp = [ar.alloc(BF16, 2, 128 * DIL[g]) for g in range(3)]
        Vp = [ar.alloc(BF16, DIL[g], 256) for g in range(3)]
        kTp_b = [tr.buf("kTp%d" % g) for g in range(3)]
        Vp_b = [tr.buf("Vp%d" % g) for g in range(3)]
        for g in range(3):
            tr.op("dve", lambda e, g=g: e.memset(Vs[g][:, 256:320], 1.0), writes=(st_b[g],))
        wqk_s = [Slot("wqk%d" % i, ar.alloc(BF16, 4, KC, 128)) for i in range(2)]
        wv_s = [Slot("wv%d" % i, ar.alloc(BF16, KC, 256)) for i in range(2)]
        rt = [ar.alloc(F32, 512) for _ in range(4)]
        rt_b = tr.bufs("rt", 4)
        ok = [ar.alloc(F32, 512) for _ in range(2)]
        ok_b = tr.bufs("ok", 2)

        ar.push(); tr.push()
        xpT = ar.alloc(BF16, KC, PF)
        xpT_b = tr.bufs("xpT", PF // 128)
        xin = [Slot("xin%d" % i, ar.alloc(BF16, D)) for i in range(6)]
        x_ti = [0]

        def x_tile(kind, i):
            ti = x_ti[0]
            x_ti[0] += 1
            sl = xin[ti % 6]
            src = (xm if kind == "m" else xp)[i * 128:(i + 1) * 128, :]
            load(sl, src)
            bk = next_bank()
            pbf = ps[bk][:].bitcast(BF16)
            tr.group("pe", [
                (lambda e, kc=kc: e.transpose(pbf[:, kc * 128:(kc + 1) * 128], sl.ap[:, kc * 128:(kc + 1) * 128], identb))
                for kc in range(KC)], reads=(sl.b, cb), writes=(psb[bk],))
            dstT, dstb = (xT, xT_b[i]) if kind == "m" else (xpT, xpT_b[i])
            dst = dstT[:, :, i * 128:(i + 1) * 128]
            srcp = pbf.rearrange("p (a b) -> p a b", a=KC)
            if ti % 2 == 0:
                tr.op("act", lambda e: e.copy(dst, srcp), reads=(psb[bk],), writes=(dstb,))
            else:
                tr.op("dve", lambda e: e.tensor_copy(dst, srcp), reads=(psb[bk],), writes=(dstb,))

        for i in range(PF // 128):
            x_tile("p", i)
        pending_main = [("m", i) for i in range(T // 128)]

        def x_main_step():
            if pending_main:
                x_tile(*pending_main.pop(0))

        tr.op("act", lambda e: e.copy(xpl2, xpT[:, :, PF - 2:PF]), reads=(xpT_b[-1],), writes=(xpl2_b,))
        if stop("a0"):
            tr.wait_all("sp")
            return nc

        def rope(bkA, bkB, n, cs, sn, tabb, dstA, dstB, dstbufs):
            zA = ps[bkA][:, 0:n]
            zB = ps[bkB][:, 0:n]
            tr.op("dve", lambda e: e.tensor_tensor(rt[0][:, 0:n], zA, cs, ALU.mult), reads=(psb[bkA], tabb), writes=(rt_b[0],))
            tr.op("dve", lambda e: e.tensor_tensor(rt[1][:, 0:n], zB, sn, ALU.mult), reads=(psb[bkB], tabb), writes=(rt_b[1],))
            tr.op("dve", lambda e: e.tensor_tensor(rt[2][:, 0:n], zB, cs, ALU.mult), reads=(psb[bkB], tabb), writes=(rt_b[2],))
            tr.op("dve", lambda e: e.tensor_tensor(rt[3][:, 0:n], zA, sn, ALU.mult), reads=(psb[bkA], tabb), writes=(rt_b[3],))
            tr.op("pool", lambda e: e.tensor_tensor(dstA, rt[0][:, 0:n], rt[1][:, 0:n], ALU.subtract), reads=(rt_b[0], rt_b[1]), writes=dstbufs)
            tr.op("pool", lambda e: e.tensor_tensor(dstB, rt[2][:, 0:n], rt[3][:, 0:n], ALU.add), reads=(rt_b[2], rt_b[3]), writes=dstbufs)

        pf_calls = [0]
        pf_every = [4]

        def proj_fm(wslot, j, xsrc, xbufs, t0, n):
            bk = next_bank()
            pf_calls[0] += 1
            if pf_calls[0] % pf_every[0] == 0:
                copy_some(1, dep=psb[bk])
            tr.group("pe", [
                (lambda e, kc=kc: e.matmul(ps[bk][:, 0:n], wslot.ap[:, j, kc, :], xsrc[:, kc, t0:t0 + n], start=(kc == 0), stop=(kc == KC - 1)))
                for kc in range(KC)], reads=(wslot.b,) + tuple(xbufs), writes=(psb[bk],))
            return bk

        tabp = Slot("tabp", ar.alloc(F32, 2, PF))
        tr.dma("sp", tabp.sem, tabp.ap[:, 0, :], cosp, writes=(tabp.b,))
        tr.dma("sp", tabp.sem, tabp.ap[:, 1, :], sinp, writes=(tabp.b,))
        for g in range(3):
            d = DIL[g]
            npg = 128 * d
            p0 = PF - npg
            load(wqk_s[g % 2], wqk[g].rearrange("j (kc p) c -> p j kc c", p=128))
            load(wv_s[g % 2], wv[g].rearrange("(kc p) c -> p kc c", p=128))
            wq = wqk_s[g % 2]
            t0 = p0
            while t0 < PF:
                n = min(512, PF - t0)
                xb = xpT_b[t0 // 128:(t0 + n) // 128]
                bA = proj_fm(wq, 2, xpT, xb, t0, n)
                bB = proj_fm(wq, 3, xpT, xb, t0, n)
                rope(bA, bB, n, tabp.ap[:, 0, t0:t0 + n], tabp.ap[:, 1, t0:t0 + n], tabp.b,
                     kTp[g][:, 0, t0 - p0:t0 - p0 + n], kTp[g][:, 1, t0 - p0:t0 - p0 + n], (kTp_b[g],))
                t0 += n
                x_main_step()
            for r in range(d):
                bk = next_bank()
                tr.group("pe", [
                    (lambda e, kc=kc, bk=bk, r=r: e.matmul(ps[bk][:, 0:256], xpT[:, kc, p0 + r:PF:d], wv_s[g % 2].ap[:, kc, :], start=(kc == 0), stop=(kc == KC - 1)))
                    for kc in range(KC)], reads=(wv_s[g % 2].b,) + tuple(xpT_b[p0 // 128:]), writes=(psb[bk],))
                tr.op("act", lambda e, bk=bk, r=r: e.copy(Vp[g][:, r, 0:256], ps[bk][:, 0:256]), reads=(psb[bk],), writes=(Vp_b[g],))
                x_main_step()
        while pending_main:
            x_main_step()
        tr.pop(); ar.pop()
        if stop("a1p"):
            tr.wait_all("sp")
            return nc

        ar.push(); tr.push()
        tabm = Slot("tabm", ar.alloc(F32, 2, T))
        tr.dma("sp", tabm.sem, tabm.ap[:, 0, :], cosm, writes=(tabm.b,))
        tr.dma("sp", tabm.sem, tabm.ap[:, 1, :], sinm, writes=(tabm.b,))
        UD = ar.alloc(F32, 4, TP)
        UD_b = tr.bufs("UD", TP // 128)
        qT = ar.alloc(BF16, 2, TP)
        kT = ar.alloc(BF16, 2, TP)
        qT_b = tr.bufs("qT", 4)
        kT_b = tr.bufs("kT", 4)
        Vm = ar.alloc(BF16, 16, 256)
        Vm_b = tr.bufs("Vm", 16)
        kst = [Slot("kst%d" % i, ar.alloc(F32, 4, 64)) for i in range(2)]
        vst = [Slot("vst%d" % i, ar.alloc(F32, 256)) for i in range(2)]
        Pt = [ar.alloc(BF16, 4, 256) for _ in range(2)]
        Pt_b = tr.bufs("Pt", 2)
        kst_i = [0]
        vst_i = [0]

        def k_out_rows(g, t0):
            W = WIN[g]
            if t0 >= TP:
                return kvs_o[g][:, W - 8:W, 0:256]
            lo = TP - W
            if t0 >= lo:
                return kvp_o[g][t0 - lo:t0 - lo + 128, 0:256]
            return None

        for g in range(3):
            d = DIL[g]
            J = 16 // d
            if g > 0:
                load(wqk_s[(g + 1) % 2], wqk[g].rearrange("j (kc p) c -> p j kc c", p=128))
                load(wv_s[(g + 1) % 2], wv[g].rearrange("(kc p) c -> p kc c", p=128))
                wq, wvs = wqk_s[(g + 1) % 2], wv_s[(g + 1) % 2]
            else:
                load(wqk_s[1], wqk[0].rearrange("j (kc p) c -> p j kc c", p=128))
                load(wv_s[1], wv[0].rearrange("(kc p) c -> p kc c", p=128))
                wq, wvs = wqk_s[1], wv_s[1]
            for bi, (t0, n) in enumerate(BLOCKS):
                xb = xtiles(t0, n)
                cs = tabm.ap[:, 0, t0:t0 + n]
                sn = tabm.ap[:, 1, t0:t0 + n]
                sample = t0 >= TP
                bA = proj_fm(wq, 0, xT, xb, t0, n)
                bB = proj_fm(wq, 1, xT, xb, t0, n)
                if sample:
                    rope(bA, bB, n, cs, sn, tabm.b, qTs[g][:, 0, :], qTs[g][:, 1, :], (st_b[g],))
                else:
                    rope(bA, bB, n, cs, sn, tabm.b, qT[:, 0, t0:t0 + n], qT[:, 1, t0:t0 + n], (qT_b[bi],))
                bA = proj_fm(wq, 2, xT, xb, t0, n)
                bB = proj_fm(wq, 3, xT, xb, t0, n)
                rope(bA, bB, n, cs, sn, tabm.b, ok[0][:, 0:n], ok[1][:, 0:n], (ok_b[0], ok_b[1]))
                if sample:
                    tr.op("act", lambda e: e.copy(kTs[g][:, 0, :], ok[0][:, 0:n]), reads=(ok_b[0],), writes=(st_b[g],))
                    tr.op("act", lambda e: e.copy(kTs[g][:, 1, :], ok[1][:, 0:n]), reads=(ok_b[1],), writes=(st_b[g],))
                else:
                    tr.op("act", lambda e: e.copy(kT[:, 0, t0:t0 + n], ok[0][:, 0:n]), reads=(ok_b[0],), writes=(kT_b[bi],))
                    tr.op("act", lambda e: e.copy(kT[:, 1, t0:t0 + n], ok[1][:, 0:n]), reads=(ok_b[1],), writes=(kT_b[bi],))
                for jt in range(n // 128):
                    dst = k_out_rows(g, t0 + jt * 128)
                    if dst is None or _os.environ.get("NO_KOUT"):
                        continue
                    bk = next_bank()
                    tr.group("pe", [
                        (lambda e, hf=hf: e.transpose(ps[bk][:, hf * 128:(hf + 1) * 128], ok[hf][:, jt * 128:(jt + 1) * 128], identf))
                        for hf in range(2)], reads=(ok_b[0], ok_b[1], cb), writes=(psb[bk],))
                    ks = kst[kst_i[0] % 2]
                    kst_i[0] += 1
                    for hf in range(2):
                        tr.op("act", lambda e, hf=hf: e.copy(ks.ap[:, :, hf * 32:(hf + 1) * 32],
                                                             ps[bk][:, hf * 128:(hf + 1) * 128].rearrange("p (h f) -> p h f", h=4)),
                              reads=(psb[bk],), writes=(ks.b,))
                    tr.dma("sp", ks.sem, dst, ks.ap.rearrange("p h f -> p (h f)"), reads=(ks.b,), writes=())
                    ks.b.r[ks.sem] = tr.cnt[ks.sem]
            vt = []
            for j in range(J):
                for r in range(d):
                    vt.append((j * d + r, 128 * d * j + r, d))
            vt.append((16, TP, 1))
            for (idx, tstart, step) in vt:
                bk = next_bank()
                tok = _sl(tstart, 128, step)
                if idx < 16:
                    xb = xT_b[(tstart // 128):(tstart + 128 * step + 127) // 128]
                else:
                    xb = xT_b[16:17]
                tr.group("pe", [
                    (lambda e, kc=kc: e.matmul(ps[bk][:, 0:256], xT[:, kc, tok], wvs.ap[:, kc, :], start=(kc == 0), stop=(kc == KC - 1)))
                    for kc in range(KC)], reads=(wvs.b,) + tuple(xb), writes=(psb[bk],))
                if idx < 16:
                    tr.op("act", lambda e: e.copy(Vm[:, idx, 0:256], ps[bk][:, 0:256]), reads=(psb[bk],), writes=(Vm_b[idx],))
                else:
                    tr.op("act", lambda e: e.copy(Vs[g][:, 0:256], ps[bk][:, 0:256]), reads=(psb[bk],), writes=(st_b[g],))
                W = WIN[g]
                dst = None
                if idx == 16:
                    dst = kvs_o[g][:, W - 8:W, 256:512]
                else:
                    lo = TP - W
                    if tstart >= lo:
                        rows = kvp_o[g].rearrange("(a s) c -> s a c", s=step) if step > 1 else None
                        if step == 1:
                            dst = kvp_o[g][tstart - lo:tstart - lo + 128, 256:512]
                        else:
                            base = tstart - lo
                            dst = rows[base % step, base // step:base // step + 128, 256:512]
                if dst is not None and not _os.environ.get("NO_VOUT"):
                    vs_ = vst[vst_i[0] % 2]
                    vst_i[0] += 1
                    tr.op("act", lambda e: e.copy(vs_.ap, ps[bk][:, 0:256]), reads=(psb[bk],), writes=(vs_.b,))
                    tr.dma("sp", vs_.sem, dst, vs_.ap, reads=(vs_.b,), writes=())
                    vs_.b.r[vs_.sem] = tr.cnt[vs_.sem]
            if stop("proj"):
                continue
            units = [(r, jj, hp) for r in range(d) for jj in range(-1, J) for hp in range(2)]

            def unit_info(r, jj):
                qbs = ([jj] if jj >= 0 else []) + ([jj + 1] if jj + 1 < J else [])
                ncols = 128 * len(qbs)
                qlo = 128 * d * qbs[0] + r
                return qbs, ncols, qlo

            def a_scores(k):
                r, jj, hp = units[k]
                qbs, ncols, qlo = unit_info(r, jj)
                qsl = _sl(qlo, ncols, d)
                qbufs = qT_b[(qlo // 512):((qlo + (ncols - 1) * d) // 512) + 1]
                if jj >= 0:
                    klo = 128 * d * jj + r
                    ksrc, ksl = kT, _sl(klo, 128, d)
                    kbufs = kT_b[(klo // 512):((klo + 127 * d) // 512) + 1]
                else:
                    ksrc, ksl = kTp[g], _sl(r, 128, d)
                    kbufs = [kTp_b[g]]
                for hh in range(2):
                    h = 2 * hp + hh
                    bk = (k % 2) * 2 + hh
                    tr.group("pe", [
                        (lambda e, hf=hf: e.matmul(ps[bk][:, 0:ncols], ksrc[32 * h:32 * h + 32, hf, ksl], qT[32 * h:32 * h + 32, hf, qsl],
                                                   start=(hf == 0), stop=(hf == 1), tile_position=(32 * h, 0)))
                        for hf in range(2)], reads=tuple(kbufs) + tuple(qbufs), writes=(psb[bk],))

            def a_rest(k):
                r, jj, hp = units[k]
                qbs, ncols, qlo = unit_info(r, jj)
                if jj >= 0:
                    vsrc, vbuf = Vm[:, jj * d + r, :], Vm_b[jj * d + r]
                else:
                    vsrc, vbuf = Vp[g][:, r, :], Vp_b[g]
                P, P_b = Pt[k % 2], Pt_b[k % 2]
                for hh in range(2):
                    bk = (k % 2) * 2 + hh
                    tr.op("act", lambda e: e.activation(P[:, hh, 0:ncols], ps[bk][:, 0:ncols], AF.Exp, scale=SCALE),
                          reads=(psb[bk],), writes=(P_b,))
                if jj == -1:
                    m = masks[:, 256:384]
                elif len(qbs) == 2:
                    m = masks[:, 0:256]
                else:
                    m = masks[:, 0:128]
                mb = m.unsqueeze(1).broadcast_to([128, 2, ncols])
                tr.op("dve", lambda e: e.tensor_tensor(P[:, 0:2, 0:ncols], P[:, 0:2, 0:ncols], mb, ALU.mult), reads=(P_b, cb), writes=(P_b,))
                for qi, qb in enumerate(qbs):
                    bk = 4 + (qb % 4)
                    is_prev = (qb != jj)
                    fns = []
                    for hh in range(2):
                        h = 2 * hp + hh
                        first = is_prev and h == 0
                        fns.append(lambda e, h=h, hh=hh, first=first: e.matmul(ps[bk][0:64, h * 128:(h + 1) * 128], vsrc[:, h * 64:(h + 1) * 64],
                                                                               P[:, hh, qi * 128:(qi + 1) * 128],
                                                                               start=first, stop=(not is_prev), skip_group_check=True, tile_position=(0, 0)))
                        fns.append(lambda e, h=h, hh=hh, first=first: e.matmul(ps[bk][64:128, h * 128:(h + 1) * 128], ones64,
                                                                               P[:, hh, qi * 128:(qi + 1) * 128],
                                                                               start=first, stop=(not is_prev), skip_group_check=True, tile_position=(0, 64)))
                    tr.group("pe", fns, reads=(vbuf, P_b, cb), writes=(psb[bk],))
                    if (not is_prev) and hp == 1:
                        tlo = 128 * d * qb + r
                        tsl = _sl(tlo, 128, d)
                        ub = UD_b[(tlo // 128):((tlo + 127 * d) // 128) + 1]
                        src = ps[bk][:].rearrange("p (h q) -> p h q", h=4)
                        if g == 0:
                            tr.op("act", lambda e: e.copy(UD[:, :, tsl], src), reads=(psb[bk],), writes=tuple(ub))
                        else:
                            tr.op("dve", lambda e: e.tensor_tensor(UD[:, :, tsl], src, UD[:, :, tsl], ALU.add), reads=(psb[bk],) + tuple(ub), writes=tuple(ub))

            a_scores(0)
            for k in range(len(units)):
                if k + 1 < len(units):
                    a_scores(k + 1)
                a_rest(k)
                if k % 8 == 7:
                    copy_some(1, dep=psb[(k % 2) * 2])
        if not stop("proj"):
            for h in range(4):
                for c0 in range(0, TP, 512):
                    rec, rec_b = rt[(c0 // 512) % 2], rt_b[(c0 // 512) % 2]
                    ub = tuple(UD_b[c0 // 128:c0 // 128 + 4])
                    tr.op("act", lambda e: e.activation(rec[0:64, :], UD[64:128, h, c0:c0 + 512], AF.Ln), reads=ub, writes=(rec_b,))
                    tr.op("act", lambda e: e.activation(rec[0:64, :], rec[0:64, :], AF.Exp, scale=-1.0), reads=(rec_b,), writes=(rec_b,))
                    ph = (h % 2) * 64
                    tr.op("dve", lambda e: e.tensor_tensor(yaT[ph:ph + 64, h // 2, c0:c0 + 512], UD[0:64, h, c0:c0 + 512], rec[0:64, :], ALU.mult),
                          reads=ub + (rec_b,), writes=(yaT_b,))
        tr.pop(); ar.pop()
        tr.pop(); ar.pop()
        if stop("attn"):
            dsem = tr.new_sem("dbg")
            tr.dma("pool", dsem, dbg_o[:, 0:2 * T].rearrange("p (a b) -> p a b", a=2)[:, :, 0:TP], yaT[:, :, 0:TP], reads=(yaT_b,))
            tr.wait_all("sp")
            return nc

        ar.push(); tr.push()
        NBK = (1, 4, 8)
        BOFF = (0, 1, 5)
        KV = [Slot("kvg%d" % i, ar.alloc(BF16, 13, 520)) for i in range(4)]
        for i in range(4):
            tr.op("dve", lambda e, i=i: e.memset(KV[i].ap[:, :, 512:513], 1.0), writes=(KV[i].b,))
        KTu2 = [ar.alloc(BF16, 26, 128) for _ in range(2)]
        KTu2_b = tr.bufs("KTu", 2)
        Qbd = ar.alloc(BF16, 3, 2, NSEQ, 32)
        Qbd_b = tr.buf("Qbd")
        kTn = ar.alloc(BF16, 3, 2, TS)
        tr.op("dve", lambda e: e.memset(Qbd, 0.0), writes=(Qbd_b,))
        for g in range(3):
            for hf in range(2):
                for h in range(4):
                    c_, hh = h // 2, h % 2
                    pd = hh * 64 + hf * 32
                    tr.op("act", lambda e: e.copy(Qbd[pd:pd + 32, g, c_, :, 8 * h:8 * h + 8],
                                                  qTs[g][32 * h:32 * h + 32, hf, :].rearrange("p (n t) -> p n t", t=8)),
                          reads=(st_b[g],), writes=(Qbd_b,))
                    tr.op("act", lambda e: e.copy(kTn[pd:pd + 32, g, c_, :], kTs[g][32 * h:32 * h + 32, hf, :]),
                          reads=(st_b[g],), writes=(Qbd_b,))
        Ps = [ar.alloc(BF16, 16, 32) for _ in range(2)]
        Ps_b = tr.bufs("Ps", 2)
        tmpd = ar.alloc(F32, 4, 64)
        usum = ar.alloc(F32, 64)
        recs = ar.alloc(F32, 1)
        yas = ar.alloc(F32, 64)
        ep_b = tr.buf("sep")
        yas_b = tr.buf("yas")
        rr5 = [0]

        def nb5():
            return next_bank()

        s_bank = {}

        def sample_D(n):
            kv = KV[n % 4]
            srcs = [caches[0][n].rearrange("(p b) c -> p b c", b=1),
                    caches[1][n].rearrange("(p b) c -> p b c", b=4),
                    caches[2][n].rearrange("(p b) c -> p b c", b=16)[:, 0:8, :]]
            for g in range(3):
                dstv = kv.ap[:, BOFF[g]:BOFF[g] + NBK[g], 0:512]
                if g == 0:
                    tr.dma("pool", kv.sem, dstv, srcs[g], writes=(kv.b,))
                else:
                    tr.dma("pool", kv.sem, dstv, srcs[g])
            kv.b.w = {kv.sem: tr.cnt[kv.sem]}
            kv.b.r = {}

        def sample_A(n):
            kv = KV[n % 4]
            KTu, KTu_b = KTu2[n % 2], KTu2_b[n % 2]
            for m in range(7):
                units = list(range(4 * m, min(4 * m + 4, 26)))
                bk = nb5()
                pbf = ps[bk][:].bitcast(BF16)
                fns = []
                for qi, u in enumerate(units):
                    b, hf = u // 2, u % 2
                    src = kv.ap[:, b, hf * 128:(hf + 1) * 128]
                    fns.append(lambda e, qi=qi, src=src: e.transpose(pbf[:, qi * 128:(qi + 1) * 128], src, identb))
                tr.group("pe", fns, reads=(kv.b, cb), writes=(psb[bk],))
                nu = len(units)
                dst = KTu[:, 4 * m:4 * m + nu, :]
                srcp = pbf[:, 0:nu * 128].rearrange("p (a b) -> p a b", a=nu)
                if m % 2 == 0:
                    tr.op("act", lambda e: e.copy(dst, srcp), reads=(psb[bk],), writes=(KTu_b,))
                else:
                    tr.op("dve", lambda e: e.tensor_copy(dst, srcp), reads=(psb[bk],), writes=(KTu_b,))
        def sample_B(n):
            KTu, KTu_b = KTu2[n % 2], KTu2_b[n % 2]
            sb_ = nb5()
            fns = []
            for u in range(16):
                if u < 13:
                    g = 0 if u == 0 else (1 if u < 5 else 2)
                else:
                    g = u - 13
                for hf in range(2):
                    lhs = KTu[:, 2 * u + hf, :] if u < 13 else kTn[:, g, hf, :]
                    fns.append(lambda e, u=u, hf=hf, lhs=lhs, g=g: e.matmul(ps[sb_][:, u * 32:(u + 1) * 32], lhs, Qbd[:, g, hf, n, :],
                                                                         start=(hf == 0), stop=(hf == 1)))
            tr.group("pe", fns, reads=(KTu_b, Qbd_b, st_b[0], st_b[1], st_b[2]), writes=(psb[sb_],))
            P = Ps[n % 2]
            tr.op("act", lambda e: e.activation(P, ps[sb_][:].rearrange("p (u c) -> p u c", u=16), AF.Exp, scale=SCALE),
                  reads=(psb[sb_],), writes=(Ps_b[n % 2],))
            tr.op("dve", lambda e: e.tensor_tensor(P[:, 0:13, :], P[:, 0:13, :], smc, ALU.mult), reads=(Ps_b[n % 2], cb), writes=(Ps_b[n % 2],))
            tr.op("dve", lambda e: e.tensor_tensor(P[:, 13:16, :], P[:, 13:16, :], smn[:, n, :, :], ALU.mult), reads=(Ps_b[n % 2], cb), writes=(Ps_b[n % 2],))
        def sample_C(n):
            kv = KV[n % 4]
            nl = n % 4
            ob = 7
            P = Ps[n % 2]
            fns = []
            for u in range(16):
                if u < 13:
                    rhs = kv.ap[:, u, 256:513]
                else:
                    rhs = Vs[u - 13][:, 0:257]
                fns.append(lambda e, u=u, rhs=rhs: e.matmul(ps[ob][32 * nl:32 * nl + 32, 0:257], P[:, u, :], rhs,
                                                          start=(u == 0), stop=(u == 15), tile_position=(0, 32 * nl)))
            tr.group("pe", fns, reads=(Ps_b[n % 2], kv.b, st_b[0], st_b[1], st_b[2]), writes=(psb[ob],))
            if nl == 3:
                bt = n // 4
                tr.op("dve", lambda e: e.tensor_tensor(tmpd, ps[ob][:, 0:256].rearrange("p (h e) -> p h e", h=4), maskd, ALU.mult),
                      reads=(psb[ob], cb), writes=(ep_b,))
                tr.op("dve", lambda e: e.tensor_reduce(usum, tmpd.rearrange("p h e -> p e h"), AX.X, ALU.add), reads=(ep_b,), writes=(ep_b,))
                tr.op("dve", lambda e: e.reciprocal(recs, ps[ob][:, 256:257]), reads=(psb[ob],), writes=(ep_b,))
                tr.op("dve", lambda e: e.tensor_scalar(yas, usum, recs[:, 0:1], None, ALU.mult), reads=(ep_b,), writes=(yas_b,))
                tb_ = next_bank()
                tr.group("pe", [lambda e: e.transpose(ps[tb_][0:64, 0:128], yas, identf)], reads=(yas_b, cb), writes=(psb[tb_],))
                for h in range(4):
                    ph = (h % 2) * 64
                    c0 = TP + 32 * bt
                    tr.op("act", lambda e: e.copy(yaT[ph:ph + 64, h // 2, c0:c0 + 32].rearrange("p (a b) -> p a b", a=4),
                                                  ps[tb_][0:64, 0:128].rearrange("p (a h b) -> p a h b", a=4, h=4)[:, :, h, :]),
                          reads=(psb[tb_],), writes=(yaT_b,))
        def dbg_dump(src, nch, bufs):
            dsem = tr.new_sem("dbg")
            tr.dma("pool", dsem, dbg_o[:, 0:nch * T].rearrange("p (a b) -> p a b", a=nch), src, reads=tuple(bufs))
            tr.wait_all("sp")

        ycT = ar.alloc(BF16, 4, T, keep=True)
        ycT_b = tr.bufs("ycT", 4)
        ss_next = [0]

        def sample_step_pre():
            pass

        def sample_step():
            i = ss_next[0]
            ss_next[0] += 1
            if i < NSEQ:
                sample_A(i)
            if 0 <= i - 1 < NSEQ:
                sample_B(i - 1)
            if 0 <= i - 2 < NSEQ:
                sample_C(i - 2)
            if i + 2 < NSEQ:
                sample_D(i + 2)

        sample_D(0)
        sample_D(1)
        pf_every[0] = 6
        bank_limit[0] = 7
        ar.push(); tr.push()
        wcv_s = [Slot("wcv%d" % i, ar.alloc(BF16, 3, KC, 128)) for i in range(2)]
        uext = ar.alloc(F32, 2 + TP)
        us = ar.alloc(F32, NSEQ, 10)
        u_b = tr.buf("u")
        hsb = [ar.alloc(F32, 512) for _ in range(2)]
        hsb_b = tr.bufs("hsb", 2)
        ct = [ar.alloc(F32, 512) for _ in range(2)]
        ct_b = tr.bufs("ct", 2)
        ust = ar.alloc(F32, 32)
        ust_b = tr.buf("ust")
        stsb = Slot("stsb", ar.alloc(F32, 512))
        tr.dma("sp", stsb.sem, stsb.ap[0:32, :], stc, writes=(stsb.b,))
        cstp = Slot("cstp", ar.alloc(F32, 512))
        csts = Slot("csts", ar.alloc(F32, 512))
        hk = [0]
        for i in range(4):
            w = wcv_s[i % 2]
            load(w, wcv[i].rearrange("s (kc p) c -> p s kc c", p=128))
            bC = next_bank()
            tr.group("pe", [(lambda e, kc=kc: e.matmul(ps[bC][:, 0:2], w.ap[:, 1, kc, :], xpl2[:, kc, :], start=(kc == 0), stop=(kc == KC - 1)))
                            for kc in range(KC)], reads=(w.b, xpl2_b), writes=(psb[bC],))
            bH = next_bank()
            tr.group("pe", [(lambda e, kc=kc: e.matmul(ps[bH][:, 0:2], w.ap[:, 2, kc, :], xpl2[:, kc, :], start=(kc == 0), stop=(kc == KC - 1)))
                            for kc in range(KC)], reads=(w.b, xpl2_b), writes=(psb[bH],))
            hs, hs_b = hsb[hk[0] % 2], hsb_b[hk[0] % 2]
            hk[0] += 1
            tr.op("act", lambda e: e.copy(hs[:, 0:2], ps[bH][:, 0:2]), reads=(psb[bH],), writes=(hs_b,))
            tr.op("dve", lambda e: e.tensor_tensor(uext[:, 0:2], ps[bC][:, 0:2], hs[:, 0:2], ALU.mult), reads=(psb[bC], hs_b), writes=(u_b,))
            bk = next_bank()
            tr.group("pe", [lambda e: e.transpose(ps[bk][:, 0:32], stsb.ap[0:32, i * 128:(i + 1) * 128], identf[0:32, 0:32])],
                     reads=(stsb.b, cb), writes=(psb[bk],))
            tr.op("act", lambda e: e.copy(us[:, :, 0:2], ps[bk][:, 0:32].rearrange("p (n j) -> p n j", j=2)), reads=(psb[bk],), writes=(u_b,))
            for bi, (t0, n) in enumerate(BLOCKS):
                sample_step_pre()
                xb = xtiles(t0, n)
                sample = t0 >= TP
                bC = proj_fm(w, 1, xT, xb, t0, n)
                bH = proj_fm(w, 2, xT, xb, t0, n)
                hs, hs_b = hsb[hk[0] % 2], hsb_b[hk[0] % 2]
                hk[0] += 1
                tr.op("act", lambda e: e.copy(hs[:, 0:n], ps[bH][:, 0:n]), reads=(psb[bH],), writes=(hs_b,))
                if sample:
                    v3 = lambda a: a.rearrange("p (n t) -> p n t", t=8)
                    tr.op("dve", lambda e: e.tensor_tensor(us[:, :, 2:10], v3(ps[bC][:, 0:n]), v3(hs[:, 0:n]), ALU.mult),
                          reads=(psb[bC], hs_b), writes=(u_b,))
                    s0, s1, s2 = us[:, :, 0:8], us[:, :, 1:9], us[:, :, 2:10]
                    c0, c1 = v3(ct[0][:, 0:n]), v3(ct[1][:, 0:n])
                else:
                    tr.op("dve", lambda e: e.tensor_tensor(uext[:, 2 + t0:2 + t0 + n], ps[bC][:, 0:n], hs[:, 0:n], ALU.mult),
                          reads=(psb[bC], hs_b), writes=(u_b,))
                    s0, s1, s2 = uext[:, t0:t0 + n], uext[:, t0 + 1:t0 + 1 + n], uext[:, t0 + 2:t0 + 2 + n]
                    c0, c1 = ct[0][:, 0:n], ct[1][:, 0:n]
                tr.op("act", lambda e: e.activation(c0, s0, AF.Copy, scale=cw[:, 3 * i:3 * i + 1]), reads=(u_b, cb), writes=(ct_b[0],))
                tr.op("dve", lambda e: e.scalar_tensor_tensor(c1, s1, cw[:, 3 * i + 1:3 * i + 2], c0, ALU.mult, ALU.add),
                      reads=(u_b, cb, ct_b[0]), writes=(ct_b[1],))
                tr.op("dve", lambda e: e.scalar_tensor_tensor(c0, s2, cw[:, 3 * i + 2:3 * i + 3], c1, ALU.mult, ALU.add),
                      reads=(u_b, cb, ct_b[1]), writes=(ct_b[0],))
                bB = proj_fm(w, 0, xT, xb, t0, n)
                tr.op("dve", lambda e: e.tensor_tensor(ycT[:, i, t0:t0 + n], ps[bB][:, 0:n], ct[0][:, 0:n], ALU.mult),
                      reads=(psb[bB], ct_b[0]), writes=(ycT_b[i],))
                sample_step()
            bk = next_bank()
            tr.group("pe", [lambda e: e.transpose(ps[bk][0:2, 0:128], uext[:, TP:TP + 2], identf)], reads=(u_b, cb), writes=(psb[bk],))
            tr.op("act", lambda e: e.copy(cstp.ap[0:2, i * 128:(i + 1) * 128], ps[bk][0:2, 0:128]), reads=(psb[bk],), writes=(cstp.b,))
            bk2 = next_bank()
            tr.op("act", lambda e: e.copy(ust.rearrange("p (n j) -> p n j", j=2), us[:, :, 8:10]), reads=(u_b,), writes=(ust_b,))
            tr.group("pe", [lambda e: e.transpose(ps[bk2][0:32, 0:128], ust, identf)], reads=(ust_b, cb), writes=(psb[bk2],))
            tr.op("act", lambda e: e.copy(csts.ap[0:32, i * 128:(i + 1) * 128], ps[bk2][0:32, 0:128]), reads=(psb[bk2],), writes=(csts.b,))
        tr.dma("sp", cstp.sem, convp_o, cstp.ap[0:2, :], reads=(cstp.b,))
        tr.dma("sp", csts.sem, convs_o, csts.ap[0:32, :], reads=(csts.b,))
        while ss_next[0] < NSEQ + 2:
            sample_step_pre()
            sample_step()
        tr.pop(); ar.pop()
        tr.pop(); ar.pop()
        bank_limit[0] = 8
        pf_every[0] = 8
        if stop("conv"):
            dbg_dump(ycT, 4, ycT_b)
            return nc

        mixT = ar.alloc(BF16, 8, T, keep=True, high=True)
        mixT_b = tr.bufs("mixT", len(BLOCKS))
        ar.push(); tr.push()
        Wc = Slot("wco", ar.alloc(BF16, 4, D))
        Wa = Slot("wao", ar.alloc(BF16, 2, D))
        load(Wc, wco.rearrange("(kc p) c -> p kc c", p=128))
        load(Wa, wao.rearrange("(kc p) c -> p kc c", p=128))
        wg_s = [Slot("wg%d" % i, ar.alloc(BF16, 2, KC, 128)) for i in range(2)]
        gt = [ar.alloc(F32, 512) for _ in range(4)]
        gt_b = tr.bufs("gt", 4)
        for f in range(8):
            wgs = wg_s[f % 2]
            load(wgs, wg[f].rearrange("s (kc p) c -> p s kc c", p=128))
            fs = slice(f * 128, (f + 1) * 128)
            for bi, (t0, n) in enumerate(BLOCKS):
                xb = xtiles(t0, n)
                b1 = next_bank()
                tr.group("pe", [(lambda e, c=c: e.matmul(ps[b1][:, 0:n], Wc.ap[:, c, fs], ycT[:, c, t0:t0 + n], start=(c == 0), stop=(c == 3)))
                                for c in range(4)], reads=(Wc.b,) + tuple(ycT_b), writes=(psb[b1],))
                b2 = next_bank()
                tr.group("pe", [(lambda e, c=c: e.matmul(ps[b2][:, 0:n], Wa.ap[:, c, fs], yaT[:, c, t0:t0 + n], start=(c == 0), stop=(c == 1)))
                                for c in range(2)], reads=(Wa.b, yaT_b), writes=(psb[b2],))
                b3 = proj_fm(wgs, 0, xT, xb, t0, n)
                b4 = proj_fm(wgs, 1, xT, xb, t0, n)
                tr.op("act", lambda e: e.activation(gt[0][:, 0:n], ps[b3][:, 0:n], AF.Sigmoid, bias=bg[:, f:f + 1]), reads=(psb[b3], cb), writes=(gt_b[0],))
                tr.op("act", lambda e: e.activation(gt[1][:, 0:n], ps[b4][:, 0:n], AF.Sigmoid, bias=bg[:, 8 + f:9 + f]), reads=(psb[b4], cb), writes=(gt_b[1],))
                tr.op("dve", lambda e: e.tensor_tensor(gt[2][:, 0:n], ps[b1][:, 0:n], gt[0][:, 0:n], ALU.mult), reads=(psb[b1], gt_b[0]), writes=(gt_b[2],))
                tr.op("dve", lambda e: e.tensor_tensor(gt[3][:, 0:n], ps[b2][:, 0:n], gt[1][:, 0:n], ALU.mult), reads=(psb[b2], gt_b[1]), writes=(gt_b[3],))
                tr.op("dve", lambda e: e.tensor_tensor(mixT[:, f, t0:t0 + n], gt[2][:, 0:n], gt[3][:, 0:n], ALU.add), reads=(gt_b[2], gt_b[3]), writes=(mixT_b[bi],))
        tr.pop(); ar.pop()
        if stop("mix"):
            dbg_dump(mixT, 8, mixT_b)
            return nc
        tr.retire(xT_b + [xpl2_b, yaT_b] + st_b + ycT_b)
        for a_ in [xT, xpl2, yaT, ycT] + qTs + kTs + Vs:
            ar.free(a_)

        X1 = ar.alloc(F32, T // 128, D, keep=True)
        X1_b = tr.bufs("X1", T // 128)
        x1T = ar.alloc(BF16, KC, T, keep=True)
        x1T_b = tr.bufs("x1T", T // 128)
        lnt = Slot("lnt", ar.alloc(F32, 2, D, keep=True))

        def load_ln(j0):
            for j in range(2):
                tr.dma("sp", lnt.sem, lnt.ap[:, j, :], lnp[j0 + j].partition_broadcast(128), writes=(lnt.b,) if j == 0 else ())
            lnt.b.w = {lnt.sem: tr.cnt[lnt.sem]}
            lnt.b.r = {}

        load_ln(0)
        lns3 = [ar.alloc(F32, 16, keep=True) for _ in range(3)]
        lns3_b = tr.bufs("lns", 3)

        def ln_stats(src, src_bufs, k):
            lns, lns_b = lns3[k % 3], lns3_b[k % 3]
            stv = lns[:, 0:12].rearrange("p (a b) -> p a b", a=2)
            tr.op("dve", lambda e: e.bn_stats(stv[:, 0, :], src[:, 0:512]), reads=tuple(src_bufs), writes=(lns_b,))
            tr.op("dve", lambda e: e.bn_stats(stv[:, 1, :], src[:, 512:1024]), reads=tuple(src_bufs), writes=(lns_b,))
            tr.op("dve", lambda e: e.bn_aggr(lns[:, 12:14], lns[:, 0:12]), reads=(lns_b,), writes=(lns_b,))
            tr.op("act", lambda e: e.activation(lns[:, 14:15], lns[:, 13:14], AF.Sqrt, bias=epsb[:, 0:1]), reads=(lns_b, epsb_b), writes=(lns_b,))
            tr.op("dve", lambda e: e.reciprocal(lns[:, 14:15], lns[:, 14:15]), reads=(lns_b,), writes=(lns_b,))
            tr.op("dve", lambda e: e.scalar_tensor_tensor(lns[:, 15:16], lns[:, 12:13], -1.0, lns[:, 14:15], ALU.mult, ALU.mult), reads=(lns_b,), writes=(lns_b,))

        def ln_apply(src, src_bufs, dst, dst_bufs, k):
            lns, lns_b = lns3[k % 3], lns3_b[k % 3]
            tr.op("act", lambda e: e.activation(dst, src, AF.Identity, bias=lns[:, 15:16], scale=lns[:, 14:15]),
                  reads=tuple(src_bufs) + (lns_b,), writes=tuple(dst_bufs))
            tr.op("pool", lambda e: e.tensor_tensor(dst, dst, lnt.ap[:, 0, :], ALU.mult), reads=tuple(dst_bufs) + (lnt.b,), writes=tuple(dst_bufs))
            tr.op("pool", lambda e: e.tensor_tensor(dst, dst, lnt.ap[:, 1, :], ALU.add), reads=tuple(dst_bufs) + (lnt.b,), writes=tuple(dst_bufs))

        ar.push(); tr.push()
        Wo = Slot("wo", ar.alloc(BF16, KC, D))
        load(Wo, wo.rearrange("(kc p) c -> p kc c", p=128))
        xf = [Slot("xf%d" % i, ar.alloc(F32, D)) for i in range(3)]
        pre3 = [ar.alloc(F32, D) for _ in range(3)]
        pre3_b = tr.bufs("pre", 3)
        x1b = [ar.alloc(BF16, D) for _ in range(3)]
        x1b_b = tr.bufs("x1b", 3)
        NT = T // 128

        def b_tr(tt):
            tsl = slice(tt * 128, (tt + 1) * 128)
            xb_, xbb = x1b[tt % 3], x1b_b[tt % 3]
            bk = next_bank()
            pbf = ps[bk][:].bitcast(BF16)
            tr.group("pe", [(lambda e, kc=kc: e.transpose(pbf[:, kc * 128:(kc + 1) * 128], xb_[:, kc * 128:(kc + 1) * 128], identb))
                            for kc in range(KC)], reads=(xbb, cb), writes=(psb[bk],))
            tr.op("act", lambda e: e.copy(x1T[:, :, tsl], pbf.rearrange("p (a b) -> p a b", a=KC)), reads=(psb[bk],), writes=(x1T_b[tt],))

        def b_gb(tt):
            dst, dst_bufs = X1[:, tt, :], (X1_b[tt],)
            tr.op("dve", lambda e: e.tensor_tensor(dst, dst, lnt.ap[:, 0, :], ALU.mult), reads=dst_bufs + (lnt.b,), writes=dst_bufs)
            tr.op("pool", lambda e: e.tensor_tensor(dst, dst, lnt.ap[:, 1, :], ALU.add), reads=dst_bufs + (lnt.b,), writes=dst_bufs)

        def b_mm(tt):
            if tt % 2 == 0:
                copy_some(1, dep=X1_b[max(tt - 3, 0)])
            xs_ = xf[tt % 3]
            pre, pre_b = pre3[tt % 3], pre3_b[tt % 3]
            tsl = slice(tt * 128, (tt + 1) * 128)
            tr.dma("sp", xs_.sem, xs_.ap, xm[tsl, :], writes=(xs_.b,))
            mb = mixT_b[min(tt // 4, 4)]
            for hf in range(2):
                cs_ = slice(hf * 512, (hf + 1) * 512)
                bk = next_bank()
                tr.group("pe", [(lambda e, f=f: e.matmul(ps[bk][:, :], mixT[:, f, tsl], Wo.ap[:, f, cs_], start=(f == 0), stop=(f == 7)))
                                for f in range(8)], reads=(mb, Wo.b), writes=(psb[bk],))
                tr.op("dve", lambda e: e.scalar_tensor_tensor(pre[:, cs_], xs_.ap[:, cs_], ALPHA, ps[bk][:, :], ALU.mult, ALU.add),
                      reads=(xs_.b, psb[bk]), writes=(pre_b,))
            ln_stats(pre, (pre_b,), tt)
            lns, lns_b = lns3[tt % 3], lns3_b[tt % 3]
            tr.op("act", lambda e: e.activation(X1[:, tt, :], pre, AF.Identity, bias=lns[:, 15:16], scale=lns[:, 14:15]),
                  reads=(pre_b, lns_b), writes=(X1_b[tt],))

        def b_cast(tt):
            xb_, xbb = x1b[tt % 3], x1b_b[tt % 3]
            tr.op("act", lambda e: e.copy(xb_, X1[:, tt, :]), reads=(X1_b[tt],), writes=(xbb,))

        for j in range(NT + 4):
            if 0 <= j - 4 < NT:
                b_tr(j - 4)
            if 0 <= j - 2 < NT:
                b_cast(j - 2)
            if 0 <= j - 1 < NT:
                b_gb(j - 1)
            if j < NT:
                b_mm(j)
        tr.pop(); ar.pop()
        tr.retire(mixT_b)
        ar.free(mixT)
        load_ln(2)

        ar.push(); tr.push()
        yst = [Slot("yst%d" % i, ar.alloc(F32, D)) for i in range(2)]
        hidT = ar.alloc(BF16, 4, T)
        hid_b = tr.bufs("hid", len(BLOCKS))
        hidT2 = ar.alloc(BF16, 4, T)
        hid2_b = tr.bufs("hid2", len(BLOCKS))
        wup_s = [Slot("wup%d" % i, ar.alloc(BF16, KC, 512)) for i in range(2)]
        wdn_s = [Slot("wdn%d" % i, ar.alloc(BF16, 4, D)) for i in range(2)]
        rl = [ar.alloc(F32, 512) for _ in range(2)]
        rl_b = tr.bufs("rl", 2)
        rk = [0]
        wup_v = wup.rearrange("(kc p) c -> p kc c", p=128)

        def c1(G, wu, fi, bi, t0, n, hidT=hidT, hid_b=hid_b):
            bk = next_bank()
            if rk[0] % 5 == 0:
                copy_some(1, dep=psb[bk])
            tr.group("pe", [(lambda e, kc=kc: e.matmul(ps[bk][:, 0:n], wu.ap[:, kc, fi * 128:(fi + 1) * 128], x1T[:, kc, t0:t0 + n],
                                                       start=(kc == 0), stop=(kc == KC - 1)))
                            for kc in range(KC)], reads=(wu.b,) + tuple(x1T_b[t0 // 128:(t0 + n) // 128]), writes=(psb[bk],))
            r_, r_b = rl[rk[0] % 2], rl_b[rk[0] % 2]
            rk[0] += 1
            tr.op("act", lambda e: e.activation(r_[:, 0:n], ps[bk][:, 0:n], AF.Relu), reads=(psb[bk],), writes=(r_b,))
            tr.op("dve", lambda e: e.tensor_tensor(hidT[:, fi, t0:t0 + n], r_[:, 0:n], r_[:, 0:n], ALU.mult), reads=(r_b,), writes=(hid_b[bi],))

        def c2(G, wd, tt, hidT=hidT, hid_b=hid_b):
            tsl = slice(tt * 128, (tt + 1) * 128)
            hb = hid_b[min(tt // 4, 4)]
            for hf in range(2):
                cs_ = slice(hf * 512, (hf + 1) * 512)
                bk = next_bank()
                tr.group("pe", [(lambda e, fi=fi: e.matmul(ps[bk][:, :], hidT[:, fi, tsl], wd.ap[:, fi, cs_], start=(fi == 0), stop=(fi == 3)))
                                for fi in range(4)], reads=(hb, wd.b), writes=(psb[bk],))
                if G == 0:
                    tr.op("dve", lambda e: e.scalar_tensor_tensor(X1[:, tt, cs_], X1[:, tt, cs_], ALPHA, ps[bk][:, :], ALU.mult, ALU.add),
                          reads=(X1_b[tt], psb[bk]), writes=(X1_b[tt],))
                else:
                    tr.op("dve", lambda e: e.tensor_tensor(X1[:, tt, cs_], ps[bk][:, :], X1[:, tt, cs_], ALU.add),
                          reads=(X1_b[tt], psb[bk]), writes=(X1_b[tt],))

        def y_out(tt):
            ys = yst[tt % 2]
            ln_apply(X1[:, tt, :], (X1_b[tt],), ys.ap, (ys.b,), tt)
            tr.dma("sp", ys.sem, y_o[tt * 128:(tt + 1) * 128, :], ys.ap, reads=(ys.b,))

        for G in range(8):
            wu, wd = wup_s[G % 2], wdn_s[G % 2]
            load(wu, wup_v[:, :, G * 512:(G + 1) * 512])
            load(wd, wdn[G * 512:(G + 1) * 512, :].rearrange("(fc p) c -> p fc c", p=128))
            if G < 6:
                for fi in range(4):
                    for bi, (t0, n) in enumerate(BLOCKS):
                        c1(G, wu, fi, bi, t0, n)
                for tt in range(NT):
                    c2(G, wd, tt)
            elif G == 6:
                continue
            else:
                wu6, wd6 = wup_s[0], wdn_s[0]

                def c1_both(bi):
                    t0, n = BLOCKS[bi]
                    for fi in range(4):
                        c1(6, wu6, fi, bi, t0, n)
                    for fi in range(4):
                        c1(7, wu, fi, bi, t0, n, hidT2, hid2_b)

                prev = None
                c1_both(0)
                for bi, (t0, n) in enumerate(BLOCKS):
                    if bi + 1 < len(BLOCKS):
                        c1_both(bi + 1)
                    for tt in range(t0 // 128, (t0 + n) // 128):
                        c2(6, wd6, tt)
                        c2(7, wd, tt, hidT2, hid2_b)
                        ln_stats(X1[:, tt, :], (X1_b[tt],), tt)
                        if prev is not None:
                            y_out(prev)
                        prev = tt
                y_out(prev)
        tr.pop(); ar.pop()

        copy_some(len(cp_pending))
        tr.wait_all("sp")
        print("build: waits=%d pe=%d act=%d dve=%d arena_peak=%d" % (tr.n_wait, tr.cnt["pe"], tr.cnt["act"], tr.cnt["dve"], ar.peak))
    return nc


def _rope_tables(pos):
    inv = (1.0 / (10000.0 ** (np.arange(0, 64, 2, dtype=np.float32) / np.float32(64)))).astype(np.float32)
    ang = pos.astype(np.float32)[None, :] * inv[:, None]
    c = np.cos(ang).astype(np.float32)
    s = np.sin(ang).astype(np.float32)
    return np.ascontiguousarray(np.tile(c, (4, 1))), np.ascontiguousarray(np.tile(s, (4, 1)))


def _host_constants():
    ident = np.eye(128, dtype=np.float32)
    k = np.arange(128)[:, None]
    q = np.arange(128)[None, :]
    m_cur = (k <= q).astype(np.float32)
    m_prev = (k >= q).astype(np.float32)
    t = np.tile(np.arange(8), 4)[None, :]
    p = np.arange(128)[:, None]
    smc = np.zeros((128, 13, 32), np.float32)
    smc[:, 0, :] = (p >= t)
    for b in range(4):
        smc[:, 1 + b, :] = ((t % 4) == b) & ((4 * p + b) >= t)
    for b in range(8):
        smc[:, 5 + b, :] = (t == b)
    smn = np.zeros((128, NSEQ, 3, 32), np.float32)
    n_ = (np.arange(128) // 8)[:, None]
    t_ = (np.arange(128) % 8)[:, None]
    for n in range(NSEQ):
        same = (n_ == n)
        for g, dl in enumerate(DIL):
            smn[:, n, g, :] = same & (t_ <= t) & (((t - t_) % dl) == 0)
    hp = (np.arange(128) % 32) // 8
    maskd = np.zeros((128, 4, 64), np.float32)
    for h in range(4):
        maskd[hp == h, h, :] = 1.0
    return ident, m_cur, m_prev, smc.reshape(128, -1), smn.reshape(128, -1), maskd.reshape(128, -1)


def _prep_inputs(inp):
    f = np.float32
    w_in = np.asarray(inp["w_in"][0], f)
    wqk = np.empty((3, 4, D, 128), f)
    wv = np.empty((3, D, 256), f)
    for g in range(3):
        off = 1536 + g * 768
        for j in range(4):
            base = off + (256 if j >= 2 else 0) + (32 if j % 2 else 0)
            cols = (base + np.arange(4)[:, None] * 64 + np.arange(32)[None, :]).reshape(-1)
            wqk[g, j] = w_in[:, cols]
        wv[g] = w_in[:, off + 512:off + 768]
    wcv = np.empty((4, 3, D, 128), f)
    for i in range(4):
        for s in range(3):
            wcv[i, s] = w_in[:, s * 512 + i * 128:s * 512 + (i + 1) * 128]
    wg = np.empty((8, 2, D, 128), f)
    for i in range(8):
        for s in range(2):
            wg[i, s] = w_in[:, 3840 + s * 1024 + i * 128:3840 + s * 1024 + (i + 1) * 128]
    bgt = np.ascontiguousarray(np.asarray(inp["b_gate"][0], f).reshape(16, 128).T)
    cwt = np.ascontiguousarray(np.asarray(inp["conv_w"][0], f).reshape(3, 4, 128).transpose(2, 1, 0).reshape(128, 12))
    lnp = np.stack([np.asarray(inp[k][0], f) for k in ("ln1_g", "ln1_b", "ln2_g", "ln2_b")])
    ident, m_cur, m_prev, smc, smn, maskd = _host_constants()
    shared = dict(wqk=wqk, wv=wv, wcv=wcv, wg=wg, wco=np.asarray(inp["w_conv_out"][0], f), wao=np.asarray(inp["w_attn_out"][0], f),
                  wo=np.asarray(inp["w_o"][0], f), wup=np.asarray(inp["w_up"][0], f), wdn=np.asarray(inp["w_down"][0], f),
                  bgt=bgt, cwt=cwt, lnp=lnp, identf=ident, smc=smc, smn=smn, maskd=maskd)
    xpr = np.asarray(inp["x_prompt"], f)
    xs = np.asarray(inp["x_sample"], f)
    maps = []
    for c in range(NCORES):
        s, half = c // 2, c % 2
        p0 = half * TP
        m = dict(shared)
        m["xm"] = np.ascontiguousarray(np.concatenate([xpr[s, p0:p0 + TP], xs[c * NSEQ:(c + 1) * NSEQ].reshape(TS, D)], axis=0))
        m["xp"] = np.ascontiguousarray(xpr[s, 0:PF]) if half == 1 else np.zeros((PF, D), f)
        pos_m = np.concatenate([p0 + np.arange(TP), np.tile(2048 + np.arange(8), NSEQ)])
        pos_p = (np.arange(PF) if half == 1 else np.zeros(PF))
        m["cosm"], m["sinm"] = _rope_tables(pos_m)
        m["cosp"], m["sinp"] = _rope_tables(pos_p)
        m["masks"] = np.ascontiguousarray(np.concatenate([m_cur, m_prev, m_prev * float(half)], axis=1))
        m["stc"] = np.ascontiguousarray(np.asarray(inp["state_conv"][0, c * NSEQ:(c + 1) * NSEQ], f).reshape(32, 512))
        m["c128"] = np.ascontiguousarray(np.asarray(inp["cache_kv_w128"][0, c * NSEQ:(c + 1) * NSEQ], f).reshape(NSEQ, 128, 512))
        m["c512"] = np.ascontiguousarray(np.asarray(inp["cache_kv_w512"][0, c * NSEQ:(c + 1) * NSEQ], f).reshape(NSEQ, 512, 512))
        m["c2048"] = np.ascontiguousarray(np.asarray(inp["cache_kv_w2048"][0, c * NSEQ:(c + 1) * NSEQ], f).reshape(NSEQ, 2048, 512))
        maps.append(m)
    return maps


def _assemble(res):
    f = np.float32
    y_p = np.empty((4, 4096, D), f)
    y_s = np.empty((128, 8, D), f)
    conv_p = np.empty((1, 4, 2, 512), f)
    kvp = [np.empty((1, 4, w, 2, 4, 64), f) for w in WIN]
    conv_s = np.empty((1, 128, 2, 512), f)
    kvs = [np.empty((1, 128, w, 2, 4, 64), f) for w in WIN]
    names_p = ("kv128p", "kv512p", "kv2048p")
    names_s = ("kv128s", "kv512s", "kv2048s")
    for c in range(NCORES):
        r = res[c]
        s, half = c // 2, c % 2
        y = np.asarray(r["y"])
        y_p[s, half * TP:(half + 1) * TP] = y[:TP]
        y_s[c * NSEQ:(c + 1) * NSEQ] = y[TP:].reshape(NSEQ, 8, D)
        conv_s[0, c * NSEQ:(c + 1) * NSEQ] = np.asarray(r["convs"]).reshape(NSEQ, 2, 512)
        for g in range(3):
            kvs[g][0, c * NSEQ:(c + 1) * NSEQ] = np.asarray(r[names_s[g]]).reshape(NSEQ, WIN[g], 2, 4, 64)
        if half == 1:
            conv_p[0, s] = np.asarray(r["convp"])
            for g in range(3):
                kvp[g][0, s] = np.asarray(r[names_p[g]]).reshape(WIN[g], 2, 4, 64)
    return (y_p, y_s, conv_p, kvp[0], kvp[1], kvp[2], conv_s, kvs[0], kvs[1], kvs[2])


_PROGRAM = {}


DEV_CORES = None


def kernel(**inputs):
    maps = _prep_inputs(inputs)
    key = STOP_AFTER
    if key not in _PROGRAM:
        _PROGRAM[key] = build_program(STOP_AFTER)
    nc = _PROGRAM[key]
    if DEV_CORES:
        res = run_bass_kernel_spmd(nc, maps[:DEV_CORES], core_ids=list(range(DEV_CORES)))
        results = list(res.results) + [res.results[0]] * (NCORES - DEV_CORES)
        return _assemble(results)
    res = run_bass_kernel_spmd(nc, maps, core_ids=list(range(NCORES)))
    return _assemble(res.results)
```

```python
import numpy as np
from contextlib import ExitStack
import concourse.bass as bass
import concourse.mybir as mybir
from concourse.bass_utils import run_bass_kernel_spmd

F32 = mybir.dt.float32
BF16 = mybir.dt.bfloat16
ALU = mybir.AluOpType
AF = mybir.ActivationFunctionType
AX = mybir.AxisListType

NCORES = 8
D = 1024
KC = 8
TP = 2048
TS = 128
T = TP + TS
PF = 2048
NSEQ = 16
DIL = (1, 4, 16)
WIN = (128, 512, 2048)
ALPHA = 2.0 ** 0.25
LN_EPS = 1e-5
SCALE = 0.125
def _sl(lo, n, step):
    return slice(lo, lo + (n - 1) * step + 1, step)


BLOCKS = [(0, 512), (512, 512), (1024, 512), (1536, 512), (2048, 128)]
STOP_AFTER = None


class Buf:
    __slots__ = ("name", "w", "r")

    def __init__(self, name, inherit=None):
        self.name = name
        self.w = dict(inherit) if inherit else {}
        self.r = dict(inherit) if inherit else {}


def _merge(d, s):
    for k, v in s.items():
        if d.get(k, 0) < v:
            d[k] = v


class Tracker:
    def __init__(self, nc, es):
        self.nc = nc
        self.es = es
        self.eng = {"pe": nc.tensor, "act": nc.scalar, "dve": nc.vector, "pool": nc.gpsimd, "sp": nc.sync}
        self.sems = {}
        self.cnt = {}
        self.seen = {e: {} for e in self.eng}
        for e in ("pe", "act", "dve", "pool"):
            self.new_sem(e)
        self.grave = {}
        self.phase_bufs = [[]]
        self.n_wait = 0

    def new_sem(self, key):
        self.sems[key] = self.es.enter_context(self.nc.semaphore("s_" + key))
        self.cnt[key] = 0
        return key

    def buf(self, name):
        b = Buf(name, self.grave)
        self.phase_bufs[-1].append(b)
        return b

    def bufs(self, name, n):
        return [self.buf("%s%d" % (name, i)) for i in range(n)]

    def push(self):
        self.phase_bufs.append([])

    def pop(self):
        for b in self.phase_bufs.pop():
            _merge(self.grave, b.w)
            _merge(self.grave, b.r)

    def retire(self, bufs):
        for b in bufs:
            _merge(self.grave, b.w)
            _merge(self.grave, b.r)

    def _wait(self, e, deps):
        seen = self.seen[e]
        for k, v in deps.items():
            if v <= 0 or seen.get(k, 0) >= v:
                continue
            if e == "pe" and k == "pe":
                continue
            self.eng[e].wait_ge(self.sems[k], v)
            self.n_wait += 1
            seen[k] = v

    def _deps(self, reads, writes):
        d = {}
        for b in reads:
            _merge(d, b.w)
        for b in writes:
            _merge(d, b.w)
            _merge(d, b.r)
        return d

    def _record(self, key, val, reads, writes):
        for b in reads:
            if b.r.get(key, 0) < val:
                b.r[key] = val
        for b in writes:
            b.w = {key: val}
            b.r = {}

    def op(self, e, fn, reads=(), writes=()):
        self._wait(e, self._deps(reads, writes))
        ins = fn(self.eng[e])
        self.cnt[e] += 1
        ins.then_inc(self.sems[e], 1)
        self._record(e, self.cnt[e], reads, writes)

    def group(self, e, fns, reads=(), writes=()):
        self._wait(e, self._deps(reads, writes))
        ins = None
        for fn in fns:
            ins = fn(self.eng[e])
        self.cnt[e] += 1
        ins.then_inc(self.sems[e], 1)
        self._record(e, self.cnt[e], reads, writes)

    def dma(self, e, semkey, out, in_, reads=(), writes=()):
        self._wait(e, self._deps(reads, writes))
        ins = self.eng[e].dma_start(out=out, in_=in_)
        self.cnt[semkey] += 16
        ins.then_inc(self.sems[semkey], 16)
        self._record(semkey, self.cnt[semkey], reads, writes)

    def wait_all(self, e):
        d = {k: v for k, v in self.cnt.items()}
        self._wait(e, d)


class Arena:
    def __init__(self, nc, es, nbytes):
        self.cap = nbytes
        self.t = es.enter_context(nc.sbuf_tensor("arena", [128, nbytes // 4], F32))
        self.free_list = [(0, nbytes)]
        self.scopes = []
        self.used = 0
        self.peak = 0
        self.handles = {}

    def push(self):
        self.scopes.append([])

    def pop(self):
        for h in self.scopes.pop():
            self._free(h)

    def _free(self, h):
        off, n = h
        self.used -= n
        fl = self.free_list + [(off, n)]
        fl.sort()
        out = []
        for o, l in fl:
            if out and out[-1][0] + out[-1][1] == o:
                out[-1] = (out[-1][0], out[-1][1] + l)
            else:
                out.append((o, l))
        self.free_list = out

    def free(self, ap):
        self._free(self.handles.pop(id(ap)))

    def alloc(self, dtype, *dims, keep=False, high=False):
        n = 1
        for x in dims:
            n *= x
        esz = 4 if dtype == F32 else 2
        nbytes = (n * esz + 63) // 64 * 64
        cands = [i for i, (o, l) in enumerate(self.free_list) if l >= nbytes]
        assert cands, "SBUF arena overflow: need %d, free=%s" % (nbytes, self.free_list)
        i = cands[-1] if high else cands[0]
        o, l = self.free_list[i]
        if high:
            off = o + l - nbytes
            self.free_list[i] = (o, l - nbytes)
        else:
            off = o
            self.free_list[i] = (o + nbytes, l - nbytes)
        self.free_list = [(a, b) for a, b in self.free_list if b > 0]
        self.used += nbytes
        self.peak = max(self.peak, self.used)
        ap = self.t[:, off // 4: off // 4 + nbytes // 4]
        if dtype != F32:
            ap = ap.bitcast(dtype)
        ap = ap[:, 0:n]
        if len(dims) == 2:
            ap = ap.rearrange("p (a b) -> p a b", a=dims[0])
        elif len(dims) == 3:
            ap = ap.rearrange("p (a b c) -> p a b c", a=dims[0], b=dims[1])
        elif len(dims) == 4:
            ap = ap.rearrange("p (a b c d) -> p a b c d", a=dims[0], b=dims[1], c=dims[2])
        if keep or not self.scopes:
            self.handles[id(ap)] = (off, nbytes)
            self._keepalive = getattr(self, "_keepalive", []) + [ap]
        else:
            self.scopes[-1].append((off, nbytes))
        return ap


def build_program(stop_after=None):
    nc = bass.Bass("TRN2", target_bir_lowering=False)

    def din(name, shape):
        return nc.dram_tensor(name, list(shape), F32, kind="ExternalInput").ap()

    def dout(name, shape):
        return nc.dram_tensor(name, list(shape), F32, kind="ExternalOutput").ap()

    xm = din("xm", [T, D])
    xp = din("xp", [PF, D])
    wqk = din("wqk", [3, 4, D, 128])
    wv = din("wv", [3, D, 256])
    wcv = din("wcv", [4, 3, D, 128])
    wg = din("wg", [8, 2, D, 128])
    wco = din("wco", [512, D])
    wao = din("wao", [256, D])
    wo = din("wo", [D, D])
    wup = din("wup", [D, 4096])
    wdn = din("wdn", [4096, D])
    bgt = din("bgt", [128, 16])
    cwt = din("cwt", [128, 12])
    lnp = din("lnp", [4, D])
    stc = din("stc", [32, 512])
    caches = [din("c128", [NSEQ, 128, 512]), din("c512", [NSEQ, 512, 512]), din("c2048", [NSEQ, 2048, 512])]
    cosm = din("cosm", [128, T])
    sinm = din("sinm", [128, T])
    cosp = din("cosp", [128, PF])
    sinp = din("sinp", [128, PF])
    identf_d = din("identf", [128, 128])
    masks_d = din("masks", [128, 384])
    smc_d = din("smc", [128, 13 * 32])
    smn_d = din("smn", [128, NSEQ * 3 * 32])
    maskd_d = din("maskd", [128, 256])

    y_o = dout("y", [T, D])
    convp_o = dout("convp", [2, 512])
    kvp_o = [dout("kv128p", [128, 512]), dout("kv512p", [512, 512]), dout("kv2048p", [2048, 512])]
    convs_o = dout("convs", [32, 512])
    kvs_o = [dout("kv128s", [NSEQ, 128, 512]), dout("kv512s", [NSEQ, 512, 512]), dout("kv2048s", [NSEQ, 2048, 512])]

    dbg_o = dout("dbg", [128, 8 * T]) if stop_after in ("attn", "sattn", "conv", "mix") else None
    es = ExitStack()
    with es:
        tr = Tracker(nc, es)
        ar = Arena(nc, es, 206 * 1024)
        ps = [es.enter_context(nc.psum_tensor("ps%d" % i, [128, 512], F32)) for i in range(8)]
        psb = tr.bufs("ps", 8)
        rr = [0]
        bank_limit = [8]

        def next_bank():
            i = rr[0] % bank_limit[0]
            rr[0] += 1
            return i

        def stop(name):
            return stop_after is not None and stop_after == name

        class Slot:
            def __init__(self, name, ap):
                self.ap = ap
                self.b = tr.buf(name)
                self.sem = tr.new_sem("d_" + name)

        def load(slot, src, eng="pool", dst=None, extra_w=()):
            tr.dma(eng, slot.sem, dst if dst is not None else slot.ap, src, reads=(), writes=(slot.b,) + tuple(extra_w))

        cp_sem = tr.new_sem("cpy")
        cp_pending = []
        import os as _os
        for g in range(3):
            W = WIN[g]
            rows_per = min(W - 8, 512)
            for n in range(NSEQ):
                r0 = 0
                while r0 < W - 8:
                    nr = min(rows_per, W - 8 - r0)
                    cp_pending.append((kvs_o[g][n, r0:r0 + nr, :], caches[g][n, 8 + r0:8 + r0 + nr, :]))
                    r0 += nr
        cp_pending.reverse()

        def copy_some(k, dep=None):
            for _ in range(k):
                if not cp_pending:
                    return
                o_, i_ = cp_pending.pop()
                tr.dma("sp", cp_sem, o_, i_, reads=(dep,) if dep is not None else ())
                if dep is not None:
                    dep.r.pop(cp_sem, None)

        csem = tr.new_sem("const")
        csem2 = tr.new_sem("const2")
        cb = tr.buf("const")
        identf = ar.alloc(F32, 128)
        identb = ar.alloc(BF16, 128)
        masks = ar.alloc(BF16, 384)
        smc = ar.alloc(BF16, 13, 32)
        smn = ar.alloc(BF16, NSEQ, 3, 32)
        maskd = ar.alloc(F32, 4, 64)
        bg = ar.alloc(F32, 16)
        cw = ar.alloc(F32, 12)
        epsb = ar.alloc(F32, 1)
        tr.dma("sp", csem, identf, identf_d)
        tr.dma("sp", csem, bg, bgt)
        tr.dma("sp", csem, cw, cwt)
        tr.dma("sp", csem, maskd, maskd_d.rearrange("p (a b) -> p a b", a=4))
        tr.dma("pool", csem2, identb, identf_d)
        tr.dma("pool", csem2, masks, masks_d)
        tr.dma("pool", csem2, smc, smc_d.rearrange("p (a b) -> p a b", a=13))
        tr.dma("pool", csem2, smn, smn_d.rearrange("p (a b c) -> p a b c", a=NSEQ, b=3))
        cb.w = {csem: tr.cnt[csem], csem2: tr.cnt[csem2]}
        ones64 = ar.alloc(BF16, 64)
        tr.op("dve", lambda e: e.memset(ones64, 1.0), writes=(cb,))
        cb.w = {csem: tr.cnt[csem], csem2: tr.cnt[csem2], "dve": tr.cnt["dve"]}
        epsb_b = tr.buf("epsb")
        tr.op("dve", lambda e: e.memset(epsb, LN_EPS), writes=(epsb_b,))

        xT = ar.alloc(BF16, KC, T)
        xTb = [[tr.buf("xT%d_%d" % (k, j)) for j in range(len(BLOCKS))] for k in range(1)]
        xT_b = tr.bufs("xTblk", T // 128)
        xpl2 = ar.alloc(BF16, KC, 2)
        xpl2_b = tr.buf("xpl2")
        yaT = ar.alloc(BF16, 2, T)
        yaT_b = tr.buf("yaT")
        qTs = [ar.alloc(BF16, 2, TS) for _ in range(3)]
        kTs = [ar.alloc(BF16, 2, TS) for _ in range(3)]
        Vs = [ar.alloc(BF16, 320) for _ in range(3)]
        st_b = [tr.buf("stash%d" % g) for g in range(3)]

        def xtiles(t0, n):
            return xT_b[t0 // 128:(t0 + n + 127) // 128]

        ar.push(); tr.push()
        kTp = [ar.alloc(BF16, 2, 128 * DIL[g]) for g in range(3)]
        Vp = [ar.alloc(BF16, DIL[g], 256) for g in range(3)]
        kTp_b = [tr.buf("kTp%d" % g) for g in range(3)]
        Vp_b = [tr.buf("Vp%d" % g) for g in range(3)]
        for g in range(3):
            tr.op("dve", lambda e, g=g: e.memset(Vs[g][:, 256:320], 1.0), writes=(st_b[g],))
        wqk_s = [Slot("wqk%d" % i, ar.alloc(BF16, 4, KC, 128)) for i in range(2)]
        wv_s = [Slot("wv%d" % i, ar.alloc(BF16, KC, 256)) for i in range(2)]
        rt = [ar.alloc(F32, 512) for _ in range(4)]
        rt_b = tr.bufs("rt", 4)
        ok = [ar.alloc(F32, 512) for _ in range(2)]
        ok_b = tr.bufs("ok", 2)

        ar.push(); tr.push()
        xpT = ar.alloc(BF16, KC, PF)
        xpT_b = tr.bufs("xpT", PF // 128)
        xin = [Slot("xin%d" % i, ar.alloc(BF16, D)) for i in range(6)]
        x_ti = [0]

        def x_tile(kind, i):
            ti = x_ti[0]
            x_ti[0] += 1
            sl = xin[ti % 6]
            src = (xm if kind == "m" else xp)[i * 128:(i + 1) * 128, :]
            load(sl, src)
            bk = next_bank()
            pbf = ps[bk][:].bitcast(BF16)
            tr.group("pe", [
                (lambda e, kc=kc: e.transpose(pbf[:, kc * 128:(kc + 1) * 128], sl.ap[:, kc * 128:(kc + 1) * 128], identb))
                for kc in range(KC)], reads=(sl.b, cb), writes=(psb[bk],))
            dstT, dstb = (xT, xT_b[i]) if kind == "m" else (xpT, xpT_b[i])
            dst = dstT[:, :, i * 128:(i + 1) * 128]
            srcp = pbf.rearrange("p (a b) -> p a b", a=KC)
            tr.op("act", lambda e: e.copy(dst, srcp), reads=(psb[bk],), writes=(dstb,))

        for i in range(PF // 128):
            x_tile("p", i)
        pending_main = [("m", i) for i in range(T // 128)]

        def x_main_step():
            if pending_main:
                x_tile(*pending_main.pop(0))

        tr.op("act", lambda e: e.copy(xpl2, xpT[:, :, PF - 2:PF]), reads=(xpT_b[-1],), writes=(xpl2_b,))
        if stop("a0"):
            tr.wait_all("sp")
            return nc

        def rope(bkA, bkB, n, cs, sn, tabb, dstA, dstB, dstbufs):
            zA = ps[bkA][:, 0:n]
            zB = ps[bkB][:, 0:n]
            tr.op("dve", lambda e: e.tensor_tensor(rt[0][:, 0:n], zA, cs, ALU.mult), reads=(psb[bkA], tabb), writes=(rt_b[0],))
            tr.op("dve", lambda e: e.tensor_tensor(rt[1][:, 0:n], zB, sn, ALU.mult), reads=(psb[bkB], tabb), writes=(rt_b[1],))
            tr.op("dve", lambda e: e.tensor_tensor(rt[2][:, 0:n], zB, cs, ALU.mult), reads=(psb[bkB], tabb), writes=(rt_b[2],))
            tr.op("dve", lambda e: e.tensor_tensor(rt[3][:, 0:n], zA, sn, ALU.mult), reads=(psb[bkA], tabb), writes=(rt_b[3],))
            tr.op("pool", lambda e: e.tensor_tensor(dstA, rt[0][:, 0:n], rt[1][:, 0:n], ALU.subtract), reads=(rt_b[0], rt_b[1]), writes=dstbufs)
            tr.op("pool", lambda e: e.tensor_tensor(dstB, rt[2][:, 0:n], rt[3][:, 0:n], ALU.add), reads=(rt_b[2], rt_b[3]), writes=dstbufs)

        pf_calls = [0]
        pf_every = [4]

        def proj_fm(wslot, j, xsrc, xbufs, t0, n):
            bk = next_bank()
            pf_calls[0] += 1
            if pf_calls[0] % pf_every[0] == 0:
                copy_some(1, dep=psb[bk])
            tr.group("pe", [
                (lambda e, kc=kc: e.matmul(ps[bk][:, 0:n], wslot.ap[:, j, kc, :], xsrc[:, kc, t0:t0 + n], start=(kc == 0), stop=(kc == KC - 1)))
                for kc in range(KC)], reads=(wslot.b,) + tuple(xbufs), writes=(psb[bk],))
            return bk

        tabp = Slot("tabp", ar.alloc(F32, 2, PF))
        tr.dma("sp", tabp.sem, tabp.ap[:, 0, :], cosp, writes=(tabp.b,))
        tr.dma("sp", tabp.sem, tabp.ap[:, 1, :], sinp, writes=(tabp.b,))
        for g in range(3):
            d = DIL[g]
            npg = 128 * d
            p0 = PF - npg
            load(wqk_s[g % 2], wqk[g].rearrange("j (kc p) c -> p j kc c", p=128))
            load(wv_s[g % 2], wv[g].rearrange("(kc p) c -> p kc c", p=128))
            wq = wqk_s[g % 2]
            t0 = p0
            while t0 < PF:
                n = min(512, PF - t0)
                xb = xpT_b[t0 // 128:(t0 + n) // 128]
                bA = proj_fm(wq, 2, xpT, xb, t0, n)
                bB = proj_fm(wq, 3, xpT, xb, t0, n)
                rope(bA, bB, n, tabp.ap[:, 0, t0:t0 + n], tabp.ap[:, 1, t0:t0 + n], tabp.b,
                     kTp[g][:, 0, t0 - p0:t0 - p0 + n], kTp[g][:, 1, t0 - p0:t0 - p0 + n], (kTp_b[g],))
                t0 += n
                x_main_step()
            for r in range(d):
                bk = next_bank()
                tr.group("pe", [
                    (lambda e, kc=kc, bk=bk, r=r: e.matmul(ps[bk][:, 0:256], xpT[:, kc, p0 + r:PF:d], wv_s[g % 2].ap[:, kc, :], start=(kc == 0), stop=(kc == KC - 1)))
                    for kc in range(KC)], reads=(wv_s[g % 2].b,) + tuple(xpT_b[p0 // 128:]), writes=(psb[bk],))
                tr.op("act", lambda e, bk=bk, r=r: e.copy(Vp[g][:, r, 0:256], ps[bk][:, 0:256]), reads=(psb[bk],), writes=(Vp_b[g],))
                x_main_step()
        while pending_main:
            x_main_step()
        tr.pop(); ar.pop()
        if stop("a1p"):
            tr.wait_all("sp")
            return nc

        ar.push(); tr.push()
        tabm = Slot("tabm", ar.alloc(F32, 2, T))
        tr.dma("sp", tabm.sem, tabm.ap[:, 0, :], cosm, writes=(tabm.b,))
        tr.dma("sp", tabm.sem, tabm.ap[:, 1, :], sinm, writes=(tabm.b,))
        UD = ar.alloc(F32, 4, TP)
        UD_b = tr.bufs("UD", TP // 128)
        qT = ar.alloc(BF16, 2, TP)
        kT = ar.alloc(BF16, 2, TP)
        qT_b = tr.bufs("qT", 4)
        kT_b = tr.bufs("kT", 4)
        Vm = ar.alloc(BF16, 16, 256)
        Vm_b = tr.bufs("Vm", 16)
        kst = [Slot("kst%d" % i, ar.alloc(F32, 4, 64)) for i in range(2)]
        vst = [Slot("vst%d" % i, ar.alloc(F32, 256)) for i in range(2)]
        Pt = [ar.alloc(BF16, 4, 256) for _ in range(2)]
        Pt_b = tr.bufs("Pt", 2)
        kst_i = [0]
        vst_i = [0]

        def k_out_rows(g, t0):
            W = WIN[g]
            if t0 >= TP:
                return kvs_o[g][:, W - 8:W, 0:256]
            lo = TP - W
            if t0 >= lo:
                return kvp_o[g][t0 - lo:t0 - lo + 128, 0:256]
            return None

        for g in range(3):
            d = DIL[g]
            J = 16 // d
            if g > 0:
                load(wqk_s[(g + 1) % 2], wqk[g].rearrange("j (kc p) c -> p j kc c", p=128))
                load(wv_s[(g + 1) % 2], wv[g].rearrange("(kc p) c -> p kc c", p=128))
                wq, wvs = wqk_s[(g + 1) % 2], wv_s[(g + 1) % 2]
            else:
                load(wqk_s[1], wqk[0].rearrange("j (kc p) c -> p j kc c", p=128))
                load(wv_s[1], wv[0].rearrange("(kc p) c -> p kc c", p=128))
                wq, wvs = wqk_s[1], wv_s[1]
            for bi, (t0, n) in enumerate(BLOCKS):
                xb = xtiles(t0, n)
                cs = tabm.ap[:, 0, t0:t0 + n]
                sn = tabm.ap[:, 1, t0:t0 + n]
                sample = t0 >= TP
                bA = proj_fm(wq, 0, xT, xb, t0, n)
                bB = proj_fm(wq, 1, xT, xb, t0, n)
                if sample:
                    rope(bA, bB, n, cs, sn, tabm.b, qTs[g][:, 0, :], qTs[g][:, 1, :], (st_b[g],))
                else:
                    rope(bA, bB, n, cs, sn, tabm.b, qT[:, 0, t0:t0 + n], qT[:, 1, t0:t0 + n], (qT_b[bi],))
                bA = proj_fm(wq, 2, xT, xb, t0, n)
                bB = proj_fm(wq, 3, xT, xb, t0, n)
                rope(bA, bB, n, cs, sn, tabm.b, ok[0][:, 0:n], ok[1][:, 0:n], (ok_b[0], ok_b[1]))
                if sample:
                    tr.op("act", lambda e: e.copy(kTs[g][:, 0, :], ok[0][:, 0:n]), reads=(ok_b[0],), writes=(st_b[g],))
                    tr.op("act", lambda e: e.copy(kTs[g][:, 1, :], ok[1][:, 0:n]), reads=(ok_b[1],), writes=(st_b[g],))
                else:
                    tr.op("act", lambda e: e.copy(kT[:, 0, t0:t0 + n], ok[0][:, 0:n]), reads=(ok_b[0],), writes=(kT_b[bi],))
                    tr.op("act", lambda e: e.copy(kT[:, 1, t0:t0 + n], ok[1][:, 0:n]), reads=(ok_b[1],), writes=(kT_b[bi],))
                for jt in range(n // 128):
                    dst = k_out_rows(g, t0 + jt * 128)
                    if dst is None or _os.environ.get("NO_KOUT"):
                        continue
                    bk = next_bank()
                    tr.group("pe", [
                        (lambda e, hf=hf: e.transpose(ps[bk][:, hf * 128:(hf + 1) * 128], ok[hf][:, jt * 128:(jt + 1) * 128], identf))
                        for hf in range(2)], reads=(ok_b[0], ok_b[1], cb), writes=(psb[bk],))
                    ks = kst[kst_i[0] % 2]
                    kst_i[0] += 1
                    for hf in range(2):
                        tr.op("act", lambda e, hf=hf: e.copy(ks.ap[:, :, hf * 32:(hf + 1) * 32],
                                                             ps[bk][:, hf * 128:(hf + 1) * 128].rearrange("p (h f) -> p h f", h=4)),
                              reads=(psb[bk],), writes=(ks.b,))
                    tr.dma("sp", ks.sem, dst, ks.ap.rearrange("p h f -> p (h f)"), reads=(ks.b,), writes=())
                    ks.b.r[ks.sem] = tr.cnt[ks.sem]
            vt = []
            for j in range(J):
                for r in range(d):
                    vt.append((j * d + r, 128 * d * j + r, d))
            vt.append((16, TP, 1))
            for (idx, tstart, step) in vt:
                bk = next_bank()
                tok = _sl(tstart, 128, step)
                if idx < 16:
                    xb = xT_b[(tstart // 128):(tstart + 128 * step + 127) // 128]
                else:
                    xb = xT_b[16:17]
                tr.group("pe", [
                    (lambda e, kc=kc: e.matmul(ps[bk][:, 0:256], xT[:, kc, tok], wvs.ap[:, kc, :], start=(kc == 0), stop=(kc == KC - 1)))
                    for kc in range(KC)], reads=(wvs.b,) + tuple(xb), writes=(psb[bk],))
                if idx < 16:
                    tr.op("act", lambda e: e.copy(Vm[:, idx, 0:256], ps[bk][:, 0:256]), reads=(psb[bk],), writes=(Vm_b[idx],))
                else:
                    tr.op("act", lambda e: e.copy(Vs[g][:, 0:256], ps[bk][:, 0:256]), reads=(psb[bk],), writes=(st_b[g],))
                W = WIN[g]
                dst = None
                if idx == 16:
                    dst = kvs_o[g][:, W - 8:W, 256:512]
                else:
                    lo = TP - W
                    if tstart >= lo:
                        rows = kvp_o[g].rearrange("(a s) c -> s a c", s=step) if step > 1 else None
                        if step == 1:
                            dst = kvp_o[g][tstart - lo:tstart - lo + 128, 256:512]
                        else:
                            base = tstart - lo
                            dst = rows[base % step, base // step:base // step + 128, 256:512]
                if dst is not None and not _os.environ.get("NO_VOUT"):
                    vs_ = vst[vst_i[0] % 2]
                    vst_i[0] += 1
                    tr.op("act", lambda e: e.copy(vs_.ap, ps[bk][:, 0:256]), reads=(psb[bk],), writes=(vs_.b,))
                    tr.dma("sp", vs_.sem, dst, vs_.ap, reads=(vs_.b,), writes=())
                    vs_.b.r[vs_.sem] = tr.cnt[vs_.sem]
            if stop("proj"):
                continue
            units = [(r, jj, hp) for r in range(d) for jj in range(-1, J) for hp in range(2)]

            def unit_info(r, jj):
                qbs = ([jj] if jj >= 0 else []) + ([jj + 1] if jj + 1 < J else [])
                ncols = 128 * len(qbs)
                qlo = 128 * d * qbs[0] + r
                return qbs, ncols, qlo

            def a_scores(k):
                r, jj, hp = units[k]
                qbs, ncols, qlo = unit_info(r, jj)
                qsl = _sl(qlo, ncols, d)
                qbufs = qT_b[(qlo // 512):((qlo + (ncols - 1) * d) // 512) + 1]
                if jj >= 0:
                    klo = 128 * d * jj + r
                    ksrc, ksl = kT, _sl(klo, 128, d)
                    kbufs = kT_b[(klo // 512):((klo + 127 * d) // 512) + 1]
                else:
                    ksrc, ksl = kTp[g], _sl(r, 128, d)
                    kbufs = [kTp_b[g]]
                for hh in range(2):
                    h = 2 * hp + hh
                    bk = (k % 2) * 2 + hh
                    tr.group("pe", [
                        (lambda e, hf=hf: e.matmul(ps[bk][:, 0:ncols], ksrc[32 * h:32 * h + 32, hf, ksl], qT[32 * h:32 * h + 32, hf, qsl],
                                                   start=(hf == 0), stop=(hf == 1), tile_position=(32 * h, 0)))
                        for hf in range(2)], reads=tuple(kbufs) + tuple(qbufs), writes=(psb[bk],))

            def a_rest(k):
                r, jj, hp = units[k]
                qbs, ncols, qlo = unit_info(r, jj)
                if jj >= 0:
                    vsrc, vbuf = Vm[:, jj * d + r, :], Vm_b[jj * d + r]
                else:
                    vsrc, vbuf = Vp[g][:, r, :], Vp_b[g]
                P, P_b = Pt[k % 2], Pt_b[k % 2]
                for hh in range(2):
                    bk = (k % 2) * 2 + hh
                    tr.op("act", lambda e: e.activation(P[:, hh, 0:ncols], ps[bk][:, 0:ncols], AF.Exp, scale=SCALE),
                          reads=(psb[bk],), writes=(P_b,))
                if jj == -1:
                    m = masks[:, 256:384]
                elif len(qbs) == 2:
                    m = masks[:, 0:256]
                else:
                    m = masks[:, 0:128]
                mb = m.unsqueeze(1).broadcast_to([128, 2, ncols])
                tr.op("dve", lambda e: e.tensor_tensor(P[:, 0:2, 0:ncols], P[:, 0:2, 0:ncols], mb, ALU.mult), reads=(P_b, cb), writes=(P_b,))
                for qi, qb in enumerate(qbs):
                    bk = 4 + (qb % 4)
                    is_prev = (qb != jj)
                    fns = []
                    for hh in range(2):
                        h = 2 * hp + hh
                        first = is_prev and h == 0
                        fns.append(lambda e, h=h, hh=hh, first=first: e.matmul(ps[bk][0:64, h * 128:(h + 1) * 128], vsrc[:, h * 64:(h + 1) * 64],
                                                                               P[:, hh, qi * 128:(qi + 1) * 128],
                                                                               start=first, stop=(not is_prev), skip_group_check=True, tile_position=(0, 0)))
                        fns.append(lambda e, h=h, hh=hh, first=first: e.matmul(ps[bk][64:128, h * 128:(h + 1) * 128], ones64,
                                                                               P[:, hh, qi * 128:(qi + 1) * 128],
                                                                               start=first, stop=(not is_prev), skip_group_check=True, tile_position=(0, 64)))
                    tr.group("pe", fns, reads=(vbuf, P_b, cb), writes=(psb[bk],))
                    if (not is_prev) and hp == 1:
                        tlo = 128 * d * qb + r
                        tsl = _sl(tlo, 128, d)
                        ub = UD_b[(tlo // 128):((tlo + 127 * d) // 128) + 1]
                        src = ps[bk][:].rearrange("p (h q) -> p h q", h=4)
                        if g == 0:
                            tr.op("act", lambda e: e.copy(UD[:, :, tsl], src), reads=(psb[bk],), writes=tuple(ub))
                        else:
                            tr.op("dve", lambda e: e.tensor_tensor(UD[:, :, tsl], src, UD[:, :, tsl], ALU.add), reads=(psb[bk],) + tuple(ub), writes=tuple(ub))

            a_scores(0)
            for k in range(len(units)):
                if k + 1 < len(units):
                    a_scores(k + 1)
                a_rest(k)
                if k % 8 == 7:
                    copy_some(1, dep=psb[(k % 2) * 2])
        if not stop("proj"):
            for c0 in range(0, TP, 512):
                ub = tuple(UD_b[c0 // 128:c0 // 128 + 4])
                for h in range(4):
                    tr.op("act", lambda e: e.activation(rt[h][0:64, :], UD[64:128, h, c0:c0 + 512], AF.Ln), reads=ub, writes=(rt_b[h],))
                for h in range(4):
                    tr.op("act", lambda e: e.activation(rt[h][0:64, :], rt[h][0:64, :], AF.Exp, scale=-1.0), reads=(rt_b[h],), writes=(rt_b[h],))
                for h in range(4):
                    ph = (h % 2) * 64
                    tr.op("dve", lambda e: e.tensor_tensor(yaT[ph:ph + 64, h // 2, c0:c0 + 512], UD[0:64, h, c0:c0 + 512], rt[h][0:64, :], ALU.mult),
                          reads=ub + (rt_b[h],), writes=(yaT_b,))
        tr.pop(); ar.pop()
        tr.pop(); ar.pop()
        if stop("attn"):
            dsem = tr.new_sem("dbg")
            tr.dma("pool", dsem, dbg_o[:, 0:2 * T].rearrange("p (a b) -> p a b", a=2)[:, :, 0:TP], yaT[:, :, 0:TP], reads=(yaT_b,))
            tr.wait_all("sp")
            return nc

        ar.push(); tr.push()
        NBK = (1, 4, 8)
        BOFF = (0, 1, 5)
        KV = [Slot("kvg%d" % i, ar.alloc(BF16, 13, 520)) for i in range(4)]
        for i in range(4):
            tr.op("dve", lambda e, i=i: e.memset(KV[i].ap[:, :, 512:513], 1.0), writes=(KV[i].b,))
        KTu2 = [ar.alloc(BF16, 26, 128) for _ in range(2)]
        KTu2_b = tr.bufs("KTu", 2)
        Qbd = ar.alloc(BF16, 3, 2, NSEQ, 32)
        Qbd_b = tr.buf("Qbd")
        kTn = ar.alloc(BF16, 3, 2, TS)
        tr.op("dve", lambda e: e.memset(Qbd, 0.0), writes=(Qbd_b,))
        for g in range(3):
            for hf in range(2):
                for h in range(4):
                    c_, hh = h // 2, h % 2
                    pd = hh * 64 + hf * 32
                    tr.op("act", lambda e: e.copy(Qbd[pd:pd + 32, g, c_, :, 8 * h:8 * h + 8],
                                                  qTs[g][32 * h:32 * h + 32, hf, :].rearrange("p (n t) -> p n t", t=8)),
                          reads=(st_b[g],), writes=(Qbd_b,))
                    tr.op("act", lambda e: e.copy(kTn[pd:pd + 32, g, c_, :], kTs[g][32 * h:32 * h + 32, hf, :]),
                          reads=(st_b[g],), writes=(Qbd_b,))
        Ps = [ar.alloc(BF16, 16, 32) for _ in range(2)]
        Ps_b = tr.bufs("Ps", 2)
        tmpd = ar.alloc(F32, 4, 64)
        usum = ar.alloc(F32, 64)
        recs = ar.alloc(F32, 1)
        yas = ar.alloc(F32, 64)
        ep_b = tr.buf("sep")
        yas_b = tr.buf("yas")
        rr5 = [0]

        def nb5():
            return next_bank()

        s_bank = {}

        def sample_D(n):
            kv = KV[n % 4]
            srcs = [caches[0][n].rearrange("(p b) c -> p b c", b=1),
                    caches[1][n].rearrange("(p b) c -> p b c", b=4),
                    caches[2][n].rearrange("(p b) c -> p b c", b=16)[:, 0:8, :]]
            for g in range(3):
                dstv = kv.ap[:, BOFF[g]:BOFF[g] + NBK[g], 0:512]
                if g == 0:
                    tr.dma("pool", kv.sem, dstv, srcs[g], writes=(kv.b,))
                else:
                    tr.dma("pool", kv.sem, dstv, srcs[g])
            kv.b.w = {kv.sem: tr.cnt[kv.sem]}
            kv.b.r = {}

        def sample_A(n):
            kv = KV[n % 4]
            KTu, KTu_b = KTu2[n % 2], KTu2_b[n % 2]
            for m in range(7):
                units = list(range(4 * m, min(4 * m + 4, 26)))
                bk = nb5()
                pbf = ps[bk][:].bitcast(BF16)
                fns = []
                for qi, u in enumerate(units):
                    b, hf = u // 2, u % 2
                    src = kv.ap[:, b, hf * 128:(hf + 1) * 128]
                    fns.append(lambda e, qi=qi, src=src: e.transpose(pbf[:, qi * 128:(qi + 1) * 128], src, identb))
                tr.group("pe", fns, reads=(kv.b, cb), writes=(psb[bk],))
                nu = len(units)
                dst = KTu[:, 4 * m:4 * m + nu, :]
                srcp = pbf[:, 0:nu * 128].rearrange("p (a b) -> p a b", a=nu)
                tr.op("act", lambda e: e.copy(dst, srcp), reads=(psb[bk],), writes=(KTu_b,))
        def sample_B(n):
            KTu, KTu_b = KTu2[n % 2], KTu2_b[n % 2]
            sb_ = nb5()
            fns = []
            for u in range(16):
                if u < 13:
                    g = 0 if u == 0 else (1 if u < 5 else 2)
                else:
                    g = u - 13
                for hf in range(2):
                    lhs = KTu[:, 2 * u + hf, :] if u < 13 else kTn[:, g, hf, :]
                    fns.append(lambda e, u=u, hf=hf, lhs=lhs, g=g: e.matmul(ps[sb_][:, u * 32:(u + 1) * 32], lhs, Qbd[:, g, hf, n, :],
                                                                         start=(hf == 0), stop=(hf == 1)))
            tr.group("pe", fns, reads=(KTu_b, Qbd_b, st_b[0], st_b[1], st_b[2]), writes=(psb[sb_],))
            P = Ps[n % 2]
            tr.op("act", lambda e: e.activation(P, ps[sb_][:].rearrange("p (u c) -> p u c", u=16), AF.Exp, scale=SCALE),
                  reads=(psb[sb_],), writes=(Ps_b[n % 2],))
            tr.op("dve", lambda e: e.tensor_tensor(P[:, 0:13, :], P[:, 0:13, :], smc, ALU.mult), reads=(Ps_b[n % 2], cb), writes=(Ps_b[n % 2],))
            tr.op("dve", lambda e: e.tensor_tensor(P[:, 13:16, :], P[:, 13:16, :], smn[:, n, :, :], ALU.mult), reads=(Ps_b[n % 2], cb), writes=(Ps_b[n % 2],))
        def sample_C(n):
            kv = KV[n % 4]
            nl = n % 4
            ob = 7
            P = Ps[n % 2]
            fns = []
            for u in range(16):
                if u < 13:
                    rhs = kv.ap[:, u, 256:513]
                else:
                    rhs = Vs[u - 13][:, 0:257]
                fns.append(lambda e, u=u, rhs=rhs: e.matmul(ps[ob][32 * nl:32 * nl + 32, 0:257], P[:, u, :], rhs,
                                                          start=(u == 0), stop=(u == 15), tile_position=(0, 32 * nl)))
            tr.group("pe", fns, reads=(Ps_b[n % 2], kv.b, st_b[0], st_b[1], st_b[2]), writes=(psb[ob],))
            if nl == 3:
                bt = n // 4
                tr.op("dve", lambda e: e.tensor_tensor(tmpd, ps[ob][:, 0:256].rearrange("p (h e) -> p h e", h=4), maskd, ALU.mult),
                      reads=(psb[ob], cb), writes=(ep_b,))
                tr.op("dve", lambda e: e.tensor_reduce(usum, tmpd.rearrange("p h e -> p e h"), AX.X, ALU.add), reads=(ep_b,), writes=(ep_b,))
                tr.op("dve", lambda e: e.reciprocal(recs, ps[ob][:, 256:257]), reads=(psb[ob],), writes=(ep_b,))
                tr.op("dve", lambda e: e.tensor_scalar(yas, usum, recs[:, 0:1], None, ALU.mult), reads=(ep_b,), writes=(yas_b,))
                tb_ = next_bank()
                tr.group("pe", [lambda e: e.transpose(ps[tb_][0:64, 0:128], yas, identf)], reads=(yas_b, cb), writes=(psb[tb_],))
                for h in range(4):
                    ph = (h % 2) * 64
                    c0 = TP + 32 * bt
                    tr.op("act", lambda e: e.copy(yaT[ph:ph + 64, h // 2, c0:c0 + 32].rearrange("p (a b) -> p a b", a=4),
                                                  ps[tb_][0:64, 0:128].rearrange("p (a h b) -> p a h b", a=4, h=4)[:, :, h, :]),
                          reads=(psb[tb_],), writes=(yaT_b,))
        def dbg_dump(src, nch, bufs):
            dsem = tr.new_sem("dbg")
            tr.dma("pool", dsem, dbg_o[:, 0:nch * T].rearrange("p (a b) -> p a b", a=nch), src, reads=tuple(bufs))
            tr.wait_all("sp")

        ycT = ar.alloc(BF16, 4, T, keep=True)
        ycT_b = tr.bufs("ycT", 4)
        ss_next = [0]

        def sample_step_pre():
            pass

        def sample_step():
            i = ss_next[0]
            ss_next[0] += 1
            if i < NSEQ:
                sample_A(i)
            if 0 <= i - 1 < NSEQ:
                sample_B(i - 1)
            if 0 <= i - 2 < NSEQ:
                sample_C(i - 2)
            if i + 2 < NSEQ:
                sample_D(i + 2)

        sample_D(0)
        sample_D(1)
        pf_every[0] = 6
        bank_limit[0] = 7
        ar.push(); tr.push()
        wcv_s = [Slot("wcv%d" % i, ar.alloc(BF16, 3, KC, 128)) for i in range(2)]
        uext = ar.alloc(F32, 2 + TP)
        us = ar.alloc(F32, NSEQ, 10)
        u_b = tr.buf("u")
        hsb = [ar.alloc(F32, 512) for _ in range(2)]
        hsb_b = tr.bufs("hsb", 2)
        ct = [ar.alloc(F32, 512) for _ in range(2)]
        ct_b = tr.bufs("ct", 2)
        ust = ar.alloc(F32, 32)
        ust_b = tr.buf("ust")
        stsb = Slot("stsb", ar.alloc(F32, 512))
        tr.dma("sp", stsb.sem, stsb.ap[0:32, :], stc, writes=(stsb.b,))
        cstp = Slot("cstp", ar.alloc(F32, 512))
        csts = Slot("csts", ar.alloc(F32, 512))
        hk = [0]
        for i in range(4):
            w = wcv_s[i % 2]
            load(w, wcv[i].rearrange("s (kc p) c -> p s kc c", p=128))
            bC = next_bank()
            tr.group("pe", [(lambda e, kc=kc: e.matmul(ps[bC][:, 0:2], w.ap[:, 1, kc, :], xpl2[:, kc, :], start=(kc == 0), stop=(kc == KC - 1)))
                            for kc in range(KC)], reads=(w.b, xpl2_b), writes=(psb[bC],))
            bH = next_bank()
            tr.group("pe", [(lambda e, kc=kc: e.matmul(ps[bH][:, 0:2], w.ap[:, 2, kc, :], xpl2[:, kc, :], start=(kc == 0), stop=(kc == KC - 1)))
                            for kc in range(KC)], reads=(w.b, xpl2_b), writes=(psb[bH],))
            hs, hs_b = hsb[hk[0] % 2], hsb_b[hk[0] % 2]
            hk[0] += 1
            tr.op("act", lambda e: e.copy(hs[:, 0:2], ps[bH][:, 0:2]), reads=(psb[bH],), writes=(hs_b,))
            tr.op("dve", lambda e: e.tensor_tensor(uext[:, 0:2], ps[bC][:, 0:2], hs[:, 0:2], ALU.mult), reads=(psb[bC], hs_b), writes=(u_b,))
            bk = next_bank()
            tr.group("pe", [lambda e: e.transpose(ps[bk][:, 0:32], stsb.ap[0:32, i * 128:(i + 1) * 128], identf[0:32, 0:32])],
                     reads=(stsb.b, cb), writes=(psb[bk],))
            tr.op("act", lambda e: e.copy(us[:, :, 0:2], ps[bk][:, 0:32].rearrange("p (n j) -> p n j", j=2)), reads=(psb[bk],), writes=(u_b,))
            for bi, (t0, n) in enumerate(BLOCKS):
                sample_step_pre()
                xb = xtiles(t0, n)
                sample = t0 >= TP
                bC = proj_fm(w, 1, xT, xb, t0, n)
                bH = proj_fm(w, 2, xT, xb, t0, n)
                hs, hs_b = hsb[hk[0] % 2], hsb_b[hk[0] % 2]
                hk[0] += 1
                tr.op("act", lambda e: e.copy(hs[:, 0:n], ps[bH][:, 0:n]), reads=(psb[bH],), writes=(hs_b,))
                if sample:
                    v3 = lambda a: a.rearrange("p (n t) -> p n t", t=8)
                    tr.op("dve", lambda e: e.tensor_tensor(us[:, :, 2:10], v3(ps[bC][:, 0:n]), v3(hs[:, 0:n]), ALU.mult),
                          reads=(psb[bC], hs_b), writes=(u_b,))
                    s0, s1, s2 = us[:, :, 0:8], us[:, :, 1:9], us[:, :, 2:10]
                    c0, c1 = v3(ct[0][:, 0:n]), v3(ct[1][:, 0:n])
                else:
                    tr.op("dve", lambda e: e.tensor_tensor(uext[:, 2 + t0:2 + t0 + n], ps[bC][:, 0:n], hs[:, 0:n], ALU.mult),
                          reads=(psb[bC], hs_b), writes=(u_b,))
                    s0, s1, s2 = uext[:, t0:t0 + n], uext[:, t0 + 1:t0 + 1 + n], uext[:, t0 + 2:t0 + 2 + n]
                    c0, c1 = ct[0][:, 0:n], ct[1][:, 0:n]
                tr.op("act", lambda e: e.activation(c0, s0, AF.Copy, scale=cw[:, 3 * i:3 * i + 1]), reads=(u_b, cb), writes=(ct_b[0],))
                tr.op("dve", lambda e: e.scalar_tensor_tensor(c1, s1, cw[:, 3 * i + 1:3 * i + 2], c0, ALU.mult, ALU.add),
                      reads=(u_b, cb, ct_b[0]), writes=(ct_b[1],))
                tr.op("dve", lambda e: e.scalar_tensor_tensor(c0, s2, cw[:, 3 * i + 2:3 * i + 3], c1, ALU.mult, ALU.add),
                      reads=(u_b, cb, ct_b[1]), writes=(ct_b[0],))
                bB = proj_fm(w, 0, xT, xb, t0, n)
                tr.op("dve", lambda e: e.tensor_tensor(ycT[:, i, t0:t0 + n], ps[bB][:, 0:n], ct[0][:, 0:n], ALU.mult),
                      reads=(psb[bB], ct_b[0]), writes=(ycT_b[i],))
                sample_step()
            bk = next_bank()
            tr.group("pe", [lambda e: e.transpose(ps[bk][0:2, 0:128], uext[:, TP:TP + 2], identf)], reads=(u_b, cb), writes=(psb[bk],))
            tr.op("act", lambda e: e.copy(cstp.ap[0:2, i * 128:(i + 1) * 128], ps[bk][0:2, 0:128]), reads=(psb[bk],), writes=(cstp.b,))
            bk2 = next_bank()
            tr.op("act", lambda e: e.copy(ust.rearrange("p (n j) -> p n j", j=2), us[:, :, 8:10]), reads=(u_b,), writes=(ust_b,))
            tr.group("pe", [lambda e: e.transpose(ps[bk2][0:32, 0:128], ust, identf)], reads=(ust_b, cb), writes=(psb[bk2],))
            tr.op("act", lambda e: e.copy(csts.ap[0:32, i * 128:(i + 1) * 128], ps[bk2][0:32, 0:128]), reads=(psb[bk2],), writes=(csts.b,))
        tr.dma("sp", cstp.sem, convp_o, cstp.ap[0:2, :], reads=(cstp.b,))
        tr.dma("sp", csts.sem, convs_o, csts.ap[0:32, :], reads=(csts.b,))
        while ss_next[0] < NSEQ + 2:
            sample_step_pre()
            sample_step()
        tr.pop(); ar.pop()
        tr.pop(); ar.pop()
        bank_limit[0] = 8
        pf_every[0] = 8
        if stop("conv"):
            dbg_dump(ycT, 4, ycT_b)
            return nc

        mixT = ar.alloc(BF16, 8, T, keep=True, high=True)
        mixT_b = tr.bufs("mixT", len(BLOCKS))
        ar.push(); tr.push()
        Wc = Slot("wco", ar.alloc(BF16, 4, D))
        Wa = Slot("wao", ar.alloc(BF16, 2, D))
        load(Wc, wco.rearrange("(kc p) c -> p kc c", p=128))
        load(Wa, wao.rearrange("(kc p) c -> p kc c", p=128))
        wg_s = [Slot("wg%d" % i, ar.alloc(BF16, 2, KC, 128)) for i in range(2)]
        gt = [ar.alloc(F32, 512) for _ in range(4)]
        gt_b = tr.bufs("gt", 4)
        for f in range(8):
            wgs = wg_s[f % 2]
            load(wgs, wg[f].rearrange("s (kc p) c -> p s kc c", p=128))
            fs = slice(f * 128, (f + 1) * 128)
            for bi, (t0, n) in enumerate(BLOCKS):
                xb = xtiles(t0, n)
                b1 = next_bank()
                tr.group("pe", [(lambda e, c=c: e.matmul(ps[b1][:, 0:n], Wc.ap[:, c, fs], ycT[:, c, t0:t0 + n], start=(c == 0), stop=(c == 3)))
                                for c in range(4)], reads=(Wc.b,) + tuple(ycT_b), writes=(psb[b1],))
                b2 = next_bank()
                tr.group("pe", [(lambda e, c=c: e.matmul(ps[b2][:, 0:n], Wa.ap[:, c, fs], yaT[:, c, t0:t0 + n], start=(c == 0), stop=(c == 1)))
                                for c in range(2)], reads=(Wa.b, yaT_b), writes=(psb[b2],))
                b3 = proj_fm(wgs, 0, xT, xb, t0, n)
                b4 = proj_fm(wgs, 1, xT, xb, t0, n)
                tr.op("act", lambda e: e.activation(gt[0][:, 0:n], ps[b3][:, 0:n], AF.Sigmoid, bias=bg[:, f:f + 1]), reads=(psb[b3], cb), writes=(gt_b[0],))
                tr.op("act", lambda e: e.activation(gt[1][:, 0:n], ps[b4][:, 0:n], AF.Sigmoid, bias=bg[:, 8 + f:9 + f]), reads=(psb[b4], cb), writes=(gt_b[1],))
                tr.op("dve", lambda e: e.tensor_tensor(gt[2][:, 0:n], ps[b1][:, 0:n], gt[0][:, 0:n], ALU.mult), reads=(psb[b1], gt_b[0]), writes=(gt_b[2],))
                tr.op("dve", lambda e: e.tensor_tensor(gt[3][:, 0:n], ps[b2][:, 0:n], gt[1][:, 0:n], ALU.mult), reads=(psb[b2], gt_b[1]), writes=(gt_b[3],))
                tr.op("dve", lambda e: e.tensor_tensor(mixT[:, f, t0:t0 + n], gt[2][:, 0:n], gt[3][:, 0:n], ALU.add), reads=(gt_b[2], gt_b[3]), writes=(mixT_b[bi],))
        tr.pop(); ar.pop()
        if stop("mix"):
            dbg_dump(mixT, 8, mixT_b)
            return nc
        tr.retire(xT_b + [xpl2_b, yaT_b] + st_b + ycT_b)
        for a_ in [xT, xpl2, yaT, ycT] + qTs + kTs + Vs:
            ar.free(a_)

        X1 = ar.alloc(F32, T // 128, D, keep=True)
        X1_b = tr.bufs("X1", T // 128)
        x1T = ar.alloc(BF16, KC, T, keep=True)
        x1T_b = tr.bufs("x1T", T // 128)
        lnt = Slot("lnt", ar.alloc(F32, 2, D, keep=True))

        def load_ln(j0):
            for j in range(2):
                tr.dma("sp", lnt.sem, lnt.ap[:, j, :], lnp[j0 + j].partition_broadcast(128), writes=(lnt.b,) if j == 0 else ())
            lnt.b.w = {lnt.sem: tr.cnt[lnt.sem]}
            lnt.b.r = {}

        load_ln(0)
        lns3 = [ar.alloc(F32, 16, keep=True) for _ in range(3)]
        lns3_b = tr.bufs("lns", 3)

        def ln_stats(src, src_bufs, k):
            lns, lns_b = lns3[k % 3], lns3_b[k % 3]
            stv = lns[:, 0:12].rearrange("p (a b) -> p a b", a=2)
            tr.op("dve", lambda e: e.bn_stats(stv[:, 0, :], src[:, 0:512]), reads=tuple(src_bufs), writes=(lns_b,))
            tr.op("dve", lambda e: e.bn_stats(stv[:, 1, :], src[:, 512:1024]), reads=tuple(src_bufs), writes=(lns_b,))
            tr.op("dve", lambda e: e.bn_aggr(lns[:, 12:14], lns[:, 0:12]), reads=(lns_b,), writes=(lns_b,))
            tr.op("act", lambda e: e.activation(lns[:, 14:15], lns[:, 13:14], AF.Sqrt, bias=epsb[:, 0:1]), reads=(lns_b, epsb_b), writes=(lns_b,))
            tr.op("dve", lambda e: e.reciprocal(lns[:, 14:15], lns[:, 14:15]), reads=(lns_b,), writes=(lns_b,))
            tr.op("dve", lambda e: e.scalar_tensor_tensor(lns[:, 15:16], lns[:, 12:13], -1.0, lns[:, 14:15], ALU.mult, ALU.mult), reads=(lns_b,), writes=(lns_b,))

        def ln_apply(src, src_bufs, dst, dst_bufs, k):
            lns, lns_b = lns3[k % 3], lns3_b[k % 3]
            tr.op("act", lambda e: e.activation(dst, src, AF.Identity, bias=lns[:, 15:16], scale=lns[:, 14:15]),
                  reads=tuple(src_bufs) + (lns_b,), writes=tuple(dst_bufs))
            tr.op("pool", lambda e: e.tensor_tensor(dst, dst, lnt.ap[:, 0, :], ALU.mult), reads=tuple(dst_bufs) + (lnt.b,), writes=tuple(dst_bufs))
            tr.op("pool", lambda e: e.tensor_tensor(dst, dst, lnt.ap[:, 1, :], ALU.add), reads=tuple(dst_bufs) + (lnt.b,), writes=tuple(dst_bufs))

        ar.push(); tr.push()
        Wo = Slot("wo", ar.alloc(BF16, KC, D))
        load(Wo, wo.rearrange("(kc p) c -> p kc c", p=128))
        xf = [Slot("xf%d" % i, ar.alloc(F32, D)) for i in range(3)]
        pre3 = [ar.alloc(F32, D) for _ in range(3)]
        pre3_b = tr.bufs("pre", 3)
        x1b = [ar.alloc(BF16, D) for _ in range(3)]
        x1b_b = tr.bufs("x1b", 3)
        NT = T // 128

        def b_tr(tt):
            tsl = slice(tt * 128, (tt + 1) * 128)
            xb_, xbb = x1b[tt % 3], x1b_b[tt % 3]
            bk = next_bank()
            pbf = ps[bk][:].bitcast(BF16)
            tr.group("pe", [(lambda e, kc=kc: e.transpose(pbf[:, kc * 128:(kc + 1) * 128], xb_[:, kc * 128:(kc + 1) * 128], identb))
                            for kc in range(KC)], reads=(xbb, cb), writes=(psb[bk],))
            tr.op("act", lambda e: e.copy(x1T[:, :, tsl], pbf.rearrange("p (a b) -> p a b", a=KC)), reads=(psb[bk],), writes=(x1T_b[tt],))

        def b_gb(tt):
            dst, dst_bufs = X1[:, tt, :], (X1_b[tt],)
            tr.op("dve", lambda e: e.tensor_tensor(dst, dst, lnt.ap[:, 0, :], ALU.mult), reads=dst_bufs + (lnt.b,), writes=dst_bufs)
            tr.op("pool", lambda e: e.tensor_tensor(dst, dst, lnt.ap[:, 1, :], ALU.add), reads=dst_bufs + (lnt.b,), writes=dst_bufs)

        def b_mm(tt):
            if tt % 2 == 0:
                copy_some(1, dep=X1_b[max(tt - 3, 0)])
            xs_ = xf[tt % 3]
            pre, pre_b = pre3[tt % 3], pre3_b[tt % 3]
            tsl = slice(tt * 128, (tt + 1) * 128)
            tr.dma("sp", xs_.sem, xs_.ap, xm[tsl, :], writes=(xs_.b,))
            mb = mixT_b[min(tt // 4, 4)]
            for hf in range(2):
                cs_ = slice(hf * 512, (hf + 1) * 512)
                bk = next_bank()
                tr.group("pe", [(lambda e, f=f: e.matmul(ps[bk][:, :], mixT[:, f, tsl], Wo.ap[:, f, cs_], start=(f == 0), stop=(f == 7)))
                                for f in range(8)], reads=(mb, Wo.b), writes=(psb[bk],))
                tr.op("dve", lambda e: e.scalar_tensor_tensor(pre[:, cs_], xs_.ap[:, cs_], ALPHA, ps[bk][:, :], ALU.mult, ALU.add),
                      reads=(xs_.b, psb[bk]), writes=(pre_b,))
            ln_stats(pre, (pre_b,), tt)
            lns, lns_b = lns3[tt % 3], lns3_b[tt % 3]
            tr.op("act", lambda e: e.activation(X1[:, tt, :], pre, AF.Identity, bias=lns[:, 15:16], scale=lns[:, 14:15]),
                  reads=(pre_b, lns_b), writes=(X1_b[tt],))

        def b_cast(tt):
            xb_, xbb = x1b[tt % 3], x1b_b[tt % 3]
            tr.op("act", lambda e: e.copy(xb_, X1[:, tt, :]), reads=(X1_b[tt],), writes=(xbb,))

        for j in range(NT + 4):
            if 0 <= j - 4 < NT:
                b_tr(j - 4)
            if 0 <= j - 2 < NT:
                b_cast(j - 2)
            if 0 <= j - 1 < NT:
                b_gb(j - 1)
            if j < NT:
                b_mm(j)
        tr.pop(); ar.pop()
        tr.retire(mixT_b)
        ar.free(mixT)
        load_ln(2)

        ar.push(); tr.push()
        yst = [Slot("yst%d" % i, ar.alloc(F32, D)) for i in range(2)]
        hidT = ar.alloc(BF16, 4, T)
        hid_b = tr.bufs("hid", len(BLOCKS))
        hidT2 = ar.alloc(BF16, 4, T)
        hid2_b = tr.bufs("hid2", len(BLOCKS))
        wup_s = [Slot("wup%d" % i, ar.alloc(BF16, KC, 512)) for i in range(2)]
        wdn_s = [Slot("wdn%d" % i, ar.alloc(BF16, 4, D)) for i in range(2)]
        rl = [ar.alloc(F32, 512) for _ in range(2)]
        rl_b = tr.bufs("rl", 2)
        rk = [0]
        wup_v = wup.rearrange("(kc p) c -> p kc c", p=128)

        def c1(G, wu, fi, bi, t0, n, hidT=hidT, hid_b=hid_b):
            bk = next_bank()
            if rk[0] % 5 == 0:
                copy_some(1, dep=psb[bk])
            tr.group("pe", [(lambda e, kc=kc: e.matmul(ps[bk][:, 0:n], wu.ap[:, kc, fi * 128:(fi + 1) * 128], x1T[:, kc, t0:t0 + n],
                                                       start=(kc == 0), stop=(kc == KC - 1)))
                            for kc in range(KC)], reads=(wu.b,) + tuple(x1T_b[t0 // 128:(t0 + n) // 128]), writes=(psb[bk],))
            r_, r_b = rl[rk[0] % 2], rl_b[rk[0] % 2]
            rk[0] += 1
            tr.op("act", lambda e: e.activation(r_[:, 0:n], ps[bk][:, 0:n], AF.Relu), reads=(psb[bk],), writes=(r_b,))
            tr.op("dve", lambda e: e.tensor_tensor(hidT[:, fi, t0:t0 + n], r_[:, 0:n], r_[:, 0:n], ALU.mult), reads=(r_b,), writes=(hid_b[bi],))

        def c2(G, wd, tt, hidT=hidT, hid_b=hid_b):
            tsl = slice(tt * 128, (tt + 1) * 128)
            hb = hid_b[min(tt // 4, 4)]
            for hf in range(2):
                cs_ = slice(hf * 512, (hf + 1) * 512)
                bk = next_bank()
                tr.group("pe", [(lambda e, fi=fi: e.matmul(ps[bk][:, :], hidT[:, fi, tsl], wd.ap[:, fi, cs_], start=(fi == 0), stop=(fi == 3)))
                                for fi in range(4)], reads=(hb, wd.b), writes=(psb[bk],))
                if G == 0:
                    tr.op("dve", lambda e: e.scalar_tensor_tensor(X1[:, tt, cs_], X1[:, tt, cs_], ALPHA, ps[bk][:, :], ALU.mult, ALU.add),
                          reads=(X1_b[tt], psb[bk]), writes=(X1_b[tt],))
                else:
                    tr.op("dve", lambda e: e.tensor_tensor(X1[:, tt, cs_], ps[bk][:, :], X1[:, tt, cs_], ALU.add),
                          reads=(X1_b[tt], psb[bk]), writes=(X1_b[tt],))

        def y_out(tt):
            ys = yst[tt % 2]
            ln_apply(X1[:, tt, :], (X1_b[tt],), ys.ap, (ys.b,), tt)
            tr.dma("sp", ys.sem, y_o[tt * 128:(tt + 1) * 128, :], ys.ap, reads=(ys.b,))

        for G in range(8):
            wu, wd = wup_s[G % 2], wdn_s[G % 2]
            load(wu, wup_v[:, :, G * 512:(G + 1) * 512])
            load(wd, wdn[G * 512:(G + 1) * 512, :].rearrange("(fc p) c -> p fc c", p=128))
            if G < 6:
                for fi in range(4):
                    for bi, (t0, n) in enumerate(BLOCKS):
                        c1(G, wu, fi, bi, t0, n)
                for tt in range(NT):
                    c2(G, wd, tt)
            elif G == 6:
                continue
            else:
                wu6, wd6 = wup_s[0], wdn_s[0]

                def c1_both(bi):
                    t0, n = BLOCKS[bi]
                    for fi in range(4):
                        c1(6, wu6, fi, bi, t0, n)
                    for fi in range(4):
                        c1(7, wu, fi, bi, t0, n, hidT2, hid2_b)

                prev = None
                c1_both(0)
                for bi, (t0, n) in enumerate(BLOCKS):
                    if bi + 1 < len(BLOCKS):
                        c1_both(bi + 1)
                    for tt in range(t0 // 128, (t0 + n) // 128):
                        c2(6, wd6, tt)
                        c2(7, wd, tt, hidT2, hid2_b)
                        ln_stats(X1[:, tt, :], (X1_b[tt],), tt)
                        if prev is not None:
                            y_out(prev)
                        prev = tt
                y_out(prev)
        tr.pop(); ar.pop()

        copy_some(len(cp_pending))
        tr.wait_all("sp")
        print("build: waits=%d pe=%d act=%d dve=%d arena_peak=%d" % (tr.n_wait, tr.cnt["pe"], tr.cnt["act"], tr.cnt["dve"], ar.peak))
    return nc


def _rope_tables(pos):
    inv = 1.0 / (10000.0 ** (np.arange(0, 64, 2, dtype=np.float64) / 64.0))
    ang = np.asarray(pos, np.float64)[None, :] * inv[:, None]
    c = np.cos(ang).astype(np.float32)
    s = np.sin(ang).astype(np.float32)
    return np.ascontiguousarray(np.tile(c, (4, 1))), np.ascontiguousarray(np.tile(s, (4, 1)))


def _host_constants():
    ident = np.eye(128, dtype=np.float32)
    k = np.arange(128)[:, None]
    q = np.arange(128)[None, :]
    m_cur = (k <= q).astype(np.float32)
    m_prev = (k >= q).astype(np.float32)
    t = np.tile(np.arange(8), 4)[None, :]
    p = np.arange(128)[:, None]
    smc = np.zeros((128, 13, 32), np.float32)
    smc[:, 0, :] = (p >= t)
    for b in range(4):
        smc[:, 1 + b, :] = ((t % 4) == b) & ((4 * p + b) >= t)
    for b in range(8):
        smc[:, 5 + b, :] = (t == b)
    smn = np.zeros((128, NSEQ, 3, 32), np.float32)
    n_ = (np.arange(128) // 8)[:, None]
    t_ = (np.arange(128) % 8)[:, None]
    for n in range(NSEQ):
        same = (n_ == n)
        for g, dl in enumerate(DIL):
            smn[:, n, g, :] = same & (t_ <= t) & (((t - t_) % dl) == 0)
    hp = (np.arange(128) % 32) // 8
    maskd = np.zeros((128, 4, 64), np.float32)
    for h in range(4):
        maskd[hp == h, h, :] = 1.0
    return ident, m_cur, m_prev, smc.reshape(128, -1), smn.reshape(128, -1), maskd.reshape(128, -1)


def _prep_inputs(inp):
    f = np.float32
    w_in = np.asarray(inp["w_in"][0], f)
    wqk = np.empty((3, 4, D, 128), f)
    wv = np.empty((3, D, 256), f)
    for g in range(3):
        off = 1536 + g * 768
        for j in range(4):
            base = off + (256 if j >= 2 else 0) + (32 if j % 2 else 0)
            cols = (base + np.arange(4)[:, None] * 64 + np.arange(32)[None, :]).reshape(-1)
            wqk[g, j] = w_in[:, cols]
        wv[g] = w_in[:, off + 512:off + 768]
    wcv = np.empty((4, 3, D, 128), f)
    for i in range(4):
        for s in range(3):
            wcv[i, s] = w_in[:, s * 512 + i * 128:s * 512 + (i + 1) * 128]
    wg = np.empty((8, 2, D, 128), f)
    for i in range(8):
        for s in range(2):
            wg[i, s] = w_in[:, 3840 + s * 1024 + i * 128:3840 + s * 1024 + (i + 1) * 128]
    bgt = np.ascontiguousarray(np.asarray(inp["b_gate"][0], f).reshape(16, 128).T)
    cwt = np.ascontiguousarray(np.asarray(inp["conv_w"][0], f).reshape(3, 4, 128).transpose(2, 1, 0).reshape(128, 12))
    lnp = np.stack([np.asarray(inp[k][0], f) for k in ("ln1_g", "ln1_b", "ln2_g", "ln2_b")])
    ident, m_cur, m_prev, smc, smn, maskd = _host_constants()
    shared = dict(wqk=wqk, wv=wv, wcv=wcv, wg=wg, wco=np.asarray(inp["w_conv_out"][0], f), wao=np.asarray(inp["w_attn_out"][0], f),
                  wo=np.asarray(inp["w_o"][0], f), wup=np.asarray(inp["w_up"][0], f), wdn=np.asarray(inp["w_down"][0], f),
                  bgt=bgt, cwt=cwt, lnp=lnp, identf=ident, smc=smc, smn=smn, maskd=maskd)
    xpr = np.asarray(inp["x_prompt"], f)
    xs = np.asarray(inp["x_sample"], f)
    maps = []
    for c in range(NCORES):
        s, half = c // 2, c % 2
        p0 = half * TP
        m = dict(shared)
        m["xm"] = np.ascontiguousarray(np.concatenate([xpr[s, p0:p0 + TP], xs[c * NSEQ:(c + 1) * NSEQ].reshape(TS, D)], axis=0))
        m["xp"] = np.ascontiguousarray(xpr[s, 0:PF]) if half == 1 else np.zeros((PF, D), f)
        pos_m = np.concatenate([p0 + np.arange(TP), np.tile(2048 + np.arange(8), NSEQ)])
        pos_p = (np.arange(PF) if half == 1 else np.zeros(PF))
        m["cosm"], m["sinm"] = _rope_tables(pos_m)
        m["cosp"], m["sinp"] = _rope_tables(pos_p)
        m["masks"] = np.ascontiguousarray(np.concatenate([m_cur, m_prev, m_prev * float(half)], axis=1))
        m["stc"] = np.ascontiguousarray(np.asarray(inp["state_conv"][0, c * NSEQ:(c + 1) * NSEQ], f).reshape(32, 512))
        m["c128"] = np.ascontiguousarray(np.asarray(inp["cache_kv_w128"][0, c * NSEQ:(c + 1) * NSEQ], f).reshape(NSEQ, 128, 512))
        m["c512"] = np.ascontiguousarray(np.asarray(inp["cache_kv_w512"][0, c * NSEQ:(c + 1) * NSEQ], f).reshape(NSEQ, 512, 512))
        m["c2048"] = np.ascontiguousarray(np.asarray(inp["cache_kv_w2048"][0, c * NSEQ:(c + 1) * NSEQ], f).reshape(NSEQ, 2048, 512))
        maps.append(m)
    return maps


def _assemble(res):
    f = np.float32
    y_p = np.empty((4, 4096, D), f)
    y_s = np.empty((128, 8, D), f)
    conv_p = np.empty((1, 4, 2, 512), f)
    kvp = [np.empty((1, 4, w, 2, 4, 64), f) for w in WIN]
    conv_s = np.empty((1, 128, 2, 512), f)
    kvs = [np.empty((1, 128, w, 2, 4, 64), f) for w in WIN]
    names_p = ("kv128p", "kv512p", "kv2048p")
    names_s = ("kv128s", "kv512s", "kv2048s")
    for c in range(NCORES):
        r = res[c]
        s, half = c // 2, c % 2
        y = np.asarray(r["y"])
        y_p[s, half * TP:(half + 1) * TP] = y[:TP]
        y_s[c * NSEQ:(c + 1) * NSEQ] = y[TP:].reshape(NSEQ, 8, D)
        conv_s[0, c * NSEQ:(c + 1) * NSEQ] = np.asarray(r["convs"]).reshape(NSEQ, 2, 512)
        for g in range(3):
            kvs[g][0, c * NSEQ:(c + 1) * NSEQ] = np.asarray(r[names_s[g]]).reshape(NSEQ, WIN[g], 2, 4, 64)
        if half == 1:
            conv_p[0, s] = np.asarray(r["convp"])
            for g in range(3):
                kvp[g][0, s] = np.asarray(r[names_p[g]]).reshape(WIN[g], 2, 4, 64)
    return (y_p, y_s, conv_p, kvp[0], kvp[1], kvp[2], conv_s, kvs[0], kvs[1], kvs[2])


_PROGRAM = {}


DEV_CORES = None


def kernel(**inputs):
    maps = _prep_inputs(inputs)
    key = STOP_AFTER
    if key not in _PROGRAM:
        _PROGRAM[key] = build_program(STOP_AFTER)
    nc = _PROGRAM[key]
    if DEV_CORES:
        res = run_bass_kernel_spmd(nc, maps[:DEV_CORES], core_ids=list(range(DEV_CORES)))
        results = list(res.results) + [res.results[0]] * (NCORES - DEV_CORES)
        return _assemble(results)
    res = run_bass_kernel_spmd(nc, maps, core_ids=list(range(NCORES)))
    return _assemble(res.results)
```

```python
import numpy as np
from contextlib import ExitStack
import concourse.bass as bass
import concourse.mybir as mybir
from concourse.bass_utils import run_bass_kernel_spmd

F32 = mybir.dt.float32
BF16 = mybir.dt.bfloat16
ALU = mybir.AluOpType
AF = mybir.ActivationFunctionType
AX = mybir.AxisListType

NCORES = 8
D = 1024
KC = 8
TP = 2048
TS = 128
T = TP + TS
PF = 2048
NSEQ = 16
DIL = (1, 4, 16)
WIN = (128, 512, 2048)
ALPHA = 2.0 ** 0.25
LN_EPS = 1e-5
SCALE = 0.125
def _sl(lo, n, step):
    return slice(lo, lo + (n - 1) * step + 1, step)


BLOCKS = [(0, 512), (512, 512), (1024, 512), (1536, 512), (2048, 128)]
STOP_AFTER = None


class Buf:
    __slots__ = ("name", "w", "r")

    def __init__(self, name, inherit=None):
        self.name = name
        self.w = dict(inherit) if inherit else {}
        self.r = dict(inherit) if inherit else {}


def _merge(d, s):
    for k, v in s.items():
        if d.get(k, 0) < v:
            d[k] = v


class Tracker:
    def __init__(self, nc, es):
        self.nc = nc
        self.es = es
        self.eng = {"pe": nc.tensor, "act": nc.scalar, "dve": nc.vector, "pool": nc.gpsimd, "sp": nc.sync}
        self.sems = {}
        self.cnt = {}
        self.seen = {e: {} for e in self.eng}
        for e in ("pe", "act", "dve", "pool"):
            self.new_sem(e)
        self.grave = {}
        self.phase_bufs = [[]]
        self.n_wait = 0

    def new_sem(self, key):
        self.sems[key] = self.es.enter_context(self.nc.semaphore("s_" + key))
        self.cnt[key] = 0
        return key

    def buf(self, name):
        b = Buf(name, self.grave)
        self.phase_bufs[-1].append(b)
        return b

    def bufs(self, name, n):
        return [self.buf("%s%d" % (name, i)) for i in range(n)]

    def push(self):
        self.phase_bufs.append([])

    def pop(self):
        for b in self.phase_bufs.pop():
            _merge(self.grave, b.w)
            _merge(self.grave, b.r)

    def retire(self, bufs):
        for b in bufs:
            _merge(self.grave, b.w)
            _merge(self.grave, b.r)

    def _wait(self, e, deps):
        seen = self.seen[e]
        for k, v in deps.items():
            if v <= 0 or seen.get(k, 0) >= v:
                continue
            if e == "pe" and k == "pe":
                continue
            self.eng[e].wait_ge(self.sems[k], v)
            self.n_wait += 1
            seen[k] = v

    def _deps(self, reads, writes):
        d = {}
        for b in reads:
            _merge(d, b.w)
        for b in writes:
            _merge(d, b.w)
            _merge(d, b.r)
        return d

    def _record(self, key, val, reads, writes):
        for b in reads:
            if b.r.get(key, 0) < val:
                b.r[key] = val
        for b in writes:
            b.w = {key: val}
            b.r = {}

    def op(self, e, fn, reads=(), writes=()):
        self._wait(e, self._deps(reads, writes))
        ins = fn(self.eng[e])
        self.cnt[e] += 1
        ins.then_inc(self.sems[e], 1)
        self._record(e, self.cnt[e], reads, writes)

    def group(self, e, fns, reads=(), writes=()):
        self._wait(e, self._deps(reads, writes))
        ins = None
        for fn in fns:
            ins = fn(self.eng[e])
        self.cnt[e] += 1
        ins.then_inc(self.sems[e], 1)
        self._record(e, self.cnt[e], reads, writes)

    def dma(self, e, semkey, out, in_, reads=(), writes=()):
        self._wait(e, self._deps(reads, writes))
        ins = self.eng[e].dma_start(out=out, in_=in_)
        self.cnt[semkey] += 16
        ins.then_inc(self.sems[semkey], 16)
        self._record(semkey, self.cnt[semkey], reads, writes)

    def wait_all(self, e):
        d = {k: v for k, v in self.cnt.items()}
        self._wait(e, d)


class Arena:
    def __init__(self, nc, es, nbytes):
        self.cap = nbytes
        self.t = es.enter_context(nc.sbuf_tensor("arena", [128, nbytes // 4], F32))
        self.free_list = [(0, nbytes)]
        self.scopes = []
        self.used = 0
        self.peak = 0
        self.handles = {}

    def push(self):
        self.scopes.append([])

    def pop(self):
        for h in self.scopes.pop():
            self._free(h)

    def _free(self, h):
        off, n = h
        self.used -= n
        fl = self.free_list + [(off, n)]
        fl.sort()
        out = []
        for o, l in fl:
            if out and out[-1][0] + out[-1][1] == o:
                out[-1] = (out[-1][0], out[-1][1] + l)
            else:
                out.append((o, l))
        self.free_list = out

    def free(self, ap):
        self._free(self.handles.pop(id(ap)))

    def alloc(self, dtype, *dims, keep=False, high=False):
        n = 1
        for x in dims:
            n *= x
        esz = 4 if dtype == F32 else 2
        nbytes = (n * esz + 63) // 64 * 64
        cands = [i for i, (o, l) in enumerate(self.free_list) if l >= nbytes]
        assert cands, "SBUF arena overflow: need %d, free=%s" % (nbytes, self.free_list)
        i = cands[-1] if high else cands[0]
        o, l = self.free_list[i]
        if high:
            off = o + l - nbytes
            self.free_list[i] = (o, l - nbytes)
        else:
            off = o
            self.free_list[i] = (o + nbytes, l - nbytes)
        self.free_list = [(a, b) for a, b in self.free_list if b > 0]
        self.used += nbytes
        self.peak = max(self.peak, self.used)
        ap = self.t[:, off // 4: off // 4 + nbytes // 4]
        if dtype != F32:
            ap = ap.bitcast(dtype)
        ap = ap[:, 0:n]
        if len(dims) == 2:
            ap = ap.rearrange("p (a b) -> p a b", a=dims[0])
        elif len(dims) == 3:
            ap = ap.rearrange("p (a b c) -> p a b c", a=dims[0], b=dims[1])
        elif len(dims) == 4:
            ap = ap.rearrange("p (a b c d) -> p a b c d", a=dims[0], b=dims[1], c=dims[2])
        if keep or not self.scopes:
            self.handles[id(ap)] = (off, nbytes)
            self._keepalive = getattr(self, "_keepalive", []) + [ap]
        else:
            self.scopes[-1].append((off, nbytes))
        return ap


def build_program(stop_after=None):
    nc = bass.Bass("TRN2", target_bir_lowering=False)

    def din(name, shape):
        return nc.dram_tensor(name, list(shape), F32, kind="ExternalInput").ap()

    def dout(name, shape):
        return nc.dram_tensor(name, list(shape), F32, kind="ExternalOutput").ap()

    xm = din("xm", [T, D])
    xp = din("xp", [PF, D])
    wqk = din("wqk", [3, 4, D, 128])
    wv = din("wv", [3, D, 256])
    wcv = din("wcv", [4, 3, D, 128])
    wg = din("wg", [8, 2, D, 128])
    wco = din("wco", [512, D])
    wao = din("wao", [256, D])
    wo = din("wo", [D, D])
    wup = din("wup", [D, 4096])
    wdn = din("wdn", [4096, D])
    bgt = din("bgt", [128, 16])
    cwt = din("cwt", [128, 12])
    lnp = din("lnp", [4, D])
    stc = din("stc", [32, 512])
    caches = [din("c128", [NSEQ, 128, 512]), din("c512", [NSEQ, 512, 512]), din("c2048", [NSEQ, 2048, 512])]
    cosm = din("cosm", [128, T])
    sinm = din("sinm", [128, T])
    cosp = din("cosp", [128, PF])
    sinp = din("sinp", [128, PF])
    identf_d = din("identf", [128, 128])
    masks_d = din("masks", [128, 384])
    smc_d = din("smc", [128, 13 * 32])
    smn_d = din("smn", [128, NSEQ * 3 * 32])
    maskd_d = din("maskd", [128, 256])

    y_o = dout("y", [T, D])
    convp_o = dout("convp", [2, 512])
    kvp_o = [dout("kv128p", [128, 512]), dout("kv512p", [512, 512]), dout("kv2048p", [2048, 512])]
    convs_o = dout("convs", [32, 512])
    kvs_o = [dout("kv128s", [NSEQ, 128, 512]), dout("kv512s", [NSEQ, 512, 512]), dout("kv2048s", [NSEQ, 2048, 512])]

    dbg_o = dout("dbg", [128, 8 * T]) if stop_after in ("attn", "sattn", "conv", "mix") else None
    es = ExitStack()
    with es:
        tr = Tracker(nc, es)
        ar = Arena(nc, es, 206 * 1024)
        ps = [es.enter_context(nc.psum_tensor("ps%d" % i, [128, 512], F32)) for i in range(8)]
        psb = tr.bufs("ps", 8)
        rr = [0]
        bank_limit = [8]

        def next_bank():
            i = rr[0] % bank_limit[0]
            rr[0] += 1
            return i

        def stop(name):
            return stop_after is not None and stop_after == name

        class Slot:
            def __init__(self, name, ap):
                self.ap = ap
                self.b = tr.buf(name)
                self.sem = tr.new_sem("d_" + name)

        def load(slot, src, eng="pool", dst=None, extra_w=()):
            tr.dma(eng, slot.sem, dst if dst is not None else slot.ap, src, reads=(), writes=(slot.b,) + tuple(extra_w))

        cp_sem = tr.new_sem("cpy")
        cp_pending = []
        import os as _os
        for g in range(3):
            W = WIN[g]
            rows_per = min(W - 8, 512)
            for n in range(NSEQ):
                r0 = 0
                while r0 < W - 8:
                    nr = min(rows_per, W - 8 - r0)
                    cp_pending.append((kvs_o[g][n, r0:r0 + nr, :], caches[g][n, 8 + r0:8 + r0 + nr, :]))
                    r0 += nr
        cp_pending.reverse()

        def copy_some(k, dep=None):
            for _ in range(k):
                if not cp_pending:
                    return
                o_, i_ = cp_pending.pop()
                tr.dma("sp", cp_sem, o_, i_, reads=(dep,) if dep is not None else ())
                if dep is not None:
                    dep.r.pop(cp_sem, None)

        csem = tr.new_sem("const")
        csem2 = tr.new_sem("const2")
        cb = tr.buf("const")
        identf = ar.alloc(F32, 128)
        identb = ar.alloc(BF16, 128)
        masks = ar.alloc(BF16, 384)
        smc = ar.alloc(BF16, 13, 32)
        smn = ar.alloc(BF16, NSEQ, 3, 32)
        maskd = ar.alloc(F32, 4, 64)
        bg = ar.alloc(F32, 16)
        cw = ar.alloc(F32, 12)
        epsb = ar.alloc(F32, 1)
        tr.dma("sp", csem, identf, identf_d)
        tr.dma("sp", csem, bg, bgt)
        tr.dma("sp", csem, cw, cwt)
        tr.dma("sp", csem, maskd, maskd_d.rearrange("p (a b) -> p a b", a=4))
        tr.dma("pool", csem2, identb, identf_d)
        tr.dma("pool", csem2, masks, masks_d)
        tr.dma("pool", csem2, smc, smc_d.rearrange("p (a b) -> p a b", a=13))
        tr.dma("pool", csem2, smn, smn_d.rearrange("p (a b c) -> p a b c", a=NSEQ, b=3))
        cb.w = {csem: tr.cnt[csem], csem2: tr.cnt[csem2]}
        ones64 = ar.alloc(BF16, 64)
        tr.op("dve", lambda e: e.memset(ones64, 1.0), writes=(cb,))
        cb.w = {csem: tr.cnt[csem], csem2: tr.cnt[csem2], "dve": tr.cnt["dve"]}
        epsb_b = tr.buf("epsb")
        tr.op("dve", lambda e: e.memset(epsb, LN_EPS), writes=(epsb_b,))

        xT = ar.alloc(BF16, KC, T)
        xTb = [[tr.buf("xT%d_%d" % (k, j)) for j in range(len(BLOCKS))] for k in range(1)]
        xT_b = tr.bufs("xTblk", T // 128)
        xpl2 = ar.alloc(BF16, KC, 2)
        xpl2_b = tr.buf("xpl2")
        yaT = ar.alloc(BF16, 2, T)
        yaT_b = tr.buf("yaT")
        qTs = [ar.alloc(BF16, 2, TS) for _ in range(3)]
        kTs = [ar.alloc(BF16, 2, TS) for _ in range(3)]
        Vs = [ar.alloc(BF16, 320) for _ in range(3)]
        st_b = [tr.buf("stash%d" % g) for g in range(3)]

        def xtiles(t0, n):
            return xT_b[t0 // 128:(t0 + n + 127) // 128]

        ar.push(); tr.push()
        kTp = [ar.alloc(BF16, 2, 128 * DIL[g]) for g in range(3)]
        Vp = [ar.alloc(BF16, DIL[g], 256) for g in range(3)]
        kTp_b = [tr.buf("kTp%d" % g) for g in range(3)]
        Vp_b = [tr.buf("Vp%d" % g) for g in range(3)]
        for g in range(3):
            tr.op("dve", lambda e, g=g: e.memset(Vs[g][:, 256:320], 1.0), writes=(st_b[g],))
        wqk_s = [Slot("wqk%d" % i, ar.alloc(BF16, 4, KC, 128)) for i in range(2)]
        wv_s = [Slot("wv%d" % i, ar.alloc(BF16, KC, 256)) for i in range(2)]
        rt = [ar.alloc(F32, 512) for _ in range(4)]
        rt_b = tr.bufs("rt", 4)
        ok = [ar.alloc(F32, 512) for _ in range(2)]
        ok_b = tr.bufs("ok", 2)

        ar.push(); tr.push()
        xpT = ar.alloc(BF16, KC, PF)
        xpT_b = tr.bufs("xpT", PF // 128)
        xin = [Slot("xin%d" % i, ar.alloc(BF16, D)) for i in range(6)]
        x_ti = [0]

        def x_tile(kind, i):
            ti = x_ti[0]
            x_ti[0] += 1
            sl = xin[ti % 6]
            src = (xm if kind == "m" else xp)[i * 128:(i + 1) * 128, :]
            load(sl, src)
            bk = next_bank()
            pbf = ps[bk][:].bitcast(BF16)
            tr.group("pe", [
                (lambda e, kc=kc: e.transpose(pbf[:, kc * 128:(kc + 1) * 128], sl.ap[:, kc * 128:(kc + 1) * 128], identb))
                for kc in range(KC)], reads=(sl.b, cb), writes=(psb[bk],))
            dstT, dstb = (xT, xT_b[i]) if kind == "m" else (xpT, xpT_b[i])
            dst = dstT[:, :, i * 128:(i + 1) * 128]
            srcp = pbf.rearrange("p (a b) -> p a b", a=KC)
            tr.op("act", lambda e: e.copy(dst, srcp), reads=(psb[bk],), writes=(dstb,))

        for i in range(PF // 128):
            x_tile("p", i)
        pending_main = [("m", i) for i in range(T // 128)]

        def x_main_step():
            if pending_main:
                x_tile(*pending_main.pop(0))

        tr.op("act", lambda e: e.copy(xpl2, xpT[:, :, PF - 2:PF]), reads=(xpT_b[-1],), writes=(xpl2_b,))
        if stop("a0"):
            tr.wait_all("sp")
            return nc

        def rope(bkA, bkB, n, cs, sn, tabb, dstA, dstB, dstbufs):
            zA = ps[bkA][:, 0:n]
            zB = ps[bkB][:, 0:n]
            tr.op("dve", lambda e: e.tensor_tensor(rt[0][:, 0:n], zA, cs, ALU.mult), reads=(psb[bkA], tabb), writes=(rt_b[0],))
            tr.op("dve", lambda e: e.tensor_tensor(rt[1][:, 0:n], zB, sn, ALU.mult), reads=(psb[bkB], tabb), writes=(rt_b[1],))
            tr.op("dve", lambda e: e.tensor_tensor(rt[2][:, 0:n], zB, cs, ALU.mult), reads=(psb[bkB], tabb), writes=(rt_b[2],))
            tr.op("dve", lambda e: e.tensor_tensor(rt[3][:, 0:n], zA, sn, ALU.mult), reads=(psb[bkA], tabb), writes=(rt_b[3],))
            tr.op("pool", lambda e: e.tensor_tensor(dstA, rt[0][:, 0:n], rt[1][:, 0:n], ALU.subtract), reads=(rt_b[0], rt_b[1]), writes=dstbufs)
            tr.op("pool", lambda e: e.tensor_tensor(dstB, rt[2][:, 0:n], rt[3][:, 0:n], ALU.add), reads=(rt_b[2], rt_b[3]), writes=dstbufs)

        pf_calls = [0]
        pf_every = [4]

        def proj_fm(wslot, j, xsrc, xbufs, t0, n):
            bk = next_bank()
            pf_calls[0] += 1
            if pf_calls[0] % pf_every[0] == 0:
                copy_some(1, dep=psb[bk])
            tr.group("pe", [
                (lambda e, kc=kc: e.matmul(ps[bk][:, 0:n], wslot.ap[:, j, kc, :], xsrc[:, kc, t0:t0 + n], start=(kc == 0), stop=(kc == KC - 1)))
                for kc in range(KC)], reads=(wslot.b,) + tuple(xbufs), writes=(psb[bk],))
            return bk

        tabp = Slot("tabp", ar.alloc(F32, 2, PF))
        tr.dma("sp", tabp.sem, tabp.ap[:, 0, :], cosp, writes=(tabp.b,))
        tr.dma("sp", tabp.sem, tabp.ap[:, 1, :], sinp, writes=(tabp.b,))
        for g in range(3):
            d = DIL[g]
            npg = 128 * d
            p0 = PF - npg
            load(wqk_s[g % 2], wqk[g].rearrange("j (kc p) c -> p j kc c", p=128))
            load(wv_s[g % 2], wv[g].rearrange("(kc p) c -> p kc c", p=128))
            wq = wqk_s[g % 2]
            t0 = p0
            while t0 < PF:
                n = min(512, PF - t0)
                xb = xpT_b[t0 // 128:(t0 + n) // 128]
                bA = proj_fm(wq, 2, xpT, xb, t0, n)
                bB = proj_fm(wq, 3, xpT, xb, t0, n)
                rope(bA, bB, n, tabp.ap[:, 0, t0:t0 + n], tabp.ap[:, 1, t0:t0 + n], tabp.b,
                     kTp[g][:, 0, t0 - p0:t0 - p0 + n], kTp[g][:, 1, t0 - p0:t0 - p0 + n], (kTp_b[g],))
                t0 += n
                x_main_step()
            for r in range(d):
                bk = next_bank()
                tr.group("pe", [
                    (lambda e, kc=kc, bk=bk, r=r: e.matmul(ps[bk][:, 0:256], xpT[:, kc, p0 + r:PF:d], wv_s[g % 2].ap[:, kc, :], start=(kc == 0), stop=(kc == KC - 1)))
                    for kc in range(KC)], reads=(wv_s[g % 2].b,) + tuple(xpT_b[p0 // 128:]), writes=(psb[bk],))
                tr.op("act", lambda e, bk=bk, r=r: e.copy(Vp[g][:, r, 0:256], ps[bk][:, 0:256]), reads=(psb[bk],), writes=(Vp_b[g],))
                x_main_step()
        while pending_main:
            x_main_step()
        tr.pop(); ar.pop()
        if stop("a1p"):
            tr.wait_all("sp")
            return nc

        ar.push(); tr.push()
        tabm = Slot("tabm", ar.alloc(F32, 2, T))
        tr.dma("sp", tabm.sem, tabm.ap[:, 0, :], cosm, writes=(tabm.b,))
        tr.dma("sp", tabm.sem, tabm.ap[:, 1, :], sinm, writes=(tabm.b,))
        UD = ar.alloc(F32, 4, TP)
        UD_b = tr.bufs("UD", TP // 128)
        qT = ar.alloc(BF16, 2, TP)
        kT = ar.alloc(BF16, 2, TP)
        qT_b = tr.bufs("qT", 4)
        kT_b = tr.bufs("kT", 4)
        Vm = ar.alloc(BF16, 16, 256)
        Vm_b = tr.bufs("Vm", 16)
        kst = [Slot("kst%d" % i, ar.alloc(F32, 4, 64)) for i in range(2)]
        vst = [Slot("vst%d" % i, ar.alloc(F32, 256)) for i in range(2)]
        Pt = [ar.alloc(BF16, 4, 256) for _ in range(2)]
        Pt_b = tr.bufs("Pt", 2)
        kst_i = [0]
        vst_i = [0]

        def k_out_rows(g, t0):
            W = WIN[g]
            if t0 >= TP:
                return kvs_o[g][:, W - 8:W, 0:256]
            lo = TP - W
            if t0 >= lo:
                return kvp_o[g][t0 - lo:t0 - lo + 128, 0:256]
            return None

        for g in range(3):
            d = DIL[g]
            J = 16 // d
            if g > 0:
                load(wqk_s[(g + 1) % 2], wqk[g].rearrange("j (kc p) c -> p j kc c", p=128))
                load(wv_s[(g + 1) % 2], wv[g].rearrange("(kc p) c -> p kc c", p=128))
                wq, wvs = wqk_s[(g + 1) % 2], wv_s[(g + 1) % 2]
            else:
                load(wqk_s[1], wqk[0].rearrange("j (kc p) c -> p j kc c", p=128))
                load(wv_s[1], wv[0].rearrange("(kc p) c -> p kc c", p=128))
                wq, wvs = wqk_s[1], wv_s[1]
            for bi, (t0, n) in enumerate(BLOCKS):
                xb = xtiles(t0, n)
                cs = tabm.ap[:, 0, t0:t0 + n]
                sn = tabm.ap[:, 1, t0:t0 + n]
                sample = t0 >= TP
                bA = proj_fm(wq, 0, xT, xb, t0, n)
                bB = proj_fm(wq, 1, xT, xb, t0, n)
                if sample:
                    rope(bA, bB, n, cs, sn, tabm.b, qTs[g][:, 0, :], qTs[g][:, 1, :], (st_b[g],))
                else:
                    rope(bA, bB, n, cs, sn, tabm.b, qT[:, 0, t0:t0 + n], qT[:, 1, t0:t0 + n], (qT_b[bi],))
                bA = proj_fm(wq, 2, xT, xb, t0, n)
                bB = proj_fm(wq, 3, xT, xb, t0, n)
                rope(bA, bB, n, cs, sn, tabm.b, ok[0][:, 0:n], ok[1][:, 0:n], (ok_b[0], ok_b[1]))
                if sample:
                    tr.op("act", lambda e: e.copy(kTs[g][:, 0, :], ok[0][:, 0:n]), reads=(ok_b[0],), writes=(st_b[g],))
                    tr.op("act", lambda e: e.copy(kTs[g][:, 1, :], ok[1][:, 0:n]), reads=(ok_b[1],), writes=(st_b[g],))
                else:
                    tr.op("act", lambda e: e.copy(kT[:, 0, t0:t0 + n], ok[0][:, 0:n]), reads=(ok_b[0],), writes=(kT_b[bi],))
                    tr.op("act", lambda e: e.copy(kT[:, 1, t0:t0 + n], ok[1][:, 0:n]), reads=(ok_b[1],), writes=(kT_b[bi],))
                for jt in range(n // 128):
                    dst = k_out_rows(g, t0 + jt * 128)
                    if dst is None or _os.environ.get("NO_KOUT"):
                        continue
                    bk = next_bank()
                    tr.group("pe", [
                        (lambda e, hf=hf: e.transpose(ps[bk][:, hf * 128:(hf + 1) * 128], ok[hf][:, jt * 128:(jt + 1) * 128], identf))
                        for hf in range(2)], reads=(ok_b[0], ok_b[1], cb), writes=(psb[bk],))
                    ks = kst[kst_i[0] % 2]
                    kst_i[0] += 1
                    for hf in range(2):
                        tr.op("act", lambda e, hf=hf: e.copy(ks.ap[:, :, hf * 32:(hf + 1) * 32],
                                                             ps[bk][:, hf * 128:(hf + 1) * 128].rearrange("p (h f) -> p h f", h=4)),
                              reads=(psb[bk],), writes=(ks.b,))
                    tr.dma("sp", ks.sem, dst, ks.ap.rearrange("p h f -> p (h f)"), reads=(ks.b,), writes=())
                    ks.b.r[ks.sem] = tr.cnt[ks.sem]
            vt = []
            for j in range(J):
                for r in range(d):
                    vt.append((j * d + r, 128 * d * j + r, d))
            vt.append((16, TP, 1))
            for (idx, tstart, step) in vt:
                bk = next_bank()
                tok = _sl(tstart, 128, step)
                if idx < 16:
                    xb = xT_b[(tstart // 128):(tstart + 128 * step + 127) // 128]
                else:
                    xb = xT_b[16:17]
                tr.group("pe", [
                    (lambda e, kc=kc: e.matmul(ps[bk][:, 0:256], xT[:, kc, tok], wvs.ap[:, kc, :], start=(kc == 0), stop=(kc == KC - 1)))
                    for kc in range(KC)], reads=(wvs.b,) + tuple(xb), writes=(psb[bk],))
                if idx < 16:
                    tr.op("act", lambda e: e.copy(Vm[:, idx, 0:256], ps[bk][:, 0:256]), reads=(psb[bk],), writes=(Vm_b[idx],))
                else:
                    tr.op("act", lambda e: e.copy(Vs[g][:, 0:256], ps[bk][:, 0:256]), reads=(psb[bk],), writes=(st_b[g],))
                W = WIN[g]
                dst = None
                if idx == 16:
                    dst = kvs_o[g][:, W - 8:W, 256:512]
                else:
                    lo = TP - W
                    if tstart >= lo:
                        rows = kvp_o[g].rearrange("(a s) c -> s a c", s=step) if step > 1 else None
                        if step == 1:
                            dst = kvp_o[g][tstart - lo:tstart - lo + 128, 256:512]
                        else:
                            base = tstart - lo
                            dst = rows[base % step, base // step:base // step + 128, 256:512]
                if dst is not None and not _os.environ.get("NO_VOUT"):
                    vs_ = vst[vst_i[0] % 2]
                    vst_i[0] += 1
                    tr.op("act", lambda e: e.copy(vs_.ap, ps[bk][:, 0:256]), reads=(psb[bk],), writes=(vs_.b,))
                    tr.dma("sp", vs_.sem, dst, vs_.ap, reads=(vs_.b,), writes=())
                    vs_.b.r[vs_.sem] = tr.cnt[vs_.sem]
            if stop("proj"):
                continue
            units = [(r, jj, hp) for r in range(d) for jj in range(-1, J) for hp in range(2)]

            def unit_info(r, jj):
                qbs = ([jj] if jj >= 0 else []) + ([jj + 1] if jj + 1 < J else [])
                ncols = 128 * len(qbs)
                qlo = 128 * d * qbs[0] + r
                return qbs, ncols, qlo

            def a_scores(k):
                r, jj, hp = units[k]
                qbs, ncols, qlo = unit_info(r, jj)
                qsl = _sl(qlo, ncols, d)
                qbufs = qT_b[(qlo // 512):((qlo + (ncols - 1) * d) // 512) + 1]
                if jj >= 0:
                    klo = 128 * d * jj + r
                    ksrc, ksl = kT, _sl(klo, 128, d)
                    kbufs = kT_b[(klo // 512):((klo + 127 * d) // 512) + 1]
                else:
                    ksrc, ksl = kTp[g], _sl(r, 128, d)
                    kbufs = [kTp_b[g]]
                for hh in range(2):
                    h = 2 * hp + hh
                    bk = (k % 2) * 2 + hh
                    tr.group("pe", [
                        (lambda e, hf=hf: e.matmul(ps[bk][:, 0:ncols], ksrc[32 * h:32 * h + 32, hf, ksl], qT[32 * h:32 * h + 32, hf, qsl],
                                                   start=(hf == 0), stop=(hf == 1), tile_position=(32 * h, 0)))
                        for hf in range(2)], reads=tuple(kbufs) + tuple(qbufs), writes=(psb[bk],))

            def a_rest(k):
                r, jj, hp = units[k]
                qbs, ncols, qlo = unit_info(r, jj)
                if jj >= 0:
                    vsrc, vbuf = Vm[:, jj * d + r, :], Vm_b[jj * d + r]
                else:
                    vsrc, vbuf = Vp[g][:, r, :], Vp_b[g]
                P, P_b = Pt[k % 2], Pt_b[k % 2]
                for hh in range(2):
                    bk = (k % 2) * 2 + hh
                    tr.op("act", lambda e: e.activation(P[:, hh, 0:ncols], ps[bk][:, 0:ncols], AF.Exp, scale=SCALE),
                          reads=(psb[bk],), writes=(P_b,))
                if jj == -1:
                    m = masks[:, 256:384]
                elif len(qbs) == 2:
                    m = masks[:, 0:256]
                else:
                    m = masks[:, 0:128]
                mb = m.unsqueeze(1).broadcast_to([128, 2, ncols])
                tr.op("dve", lambda e: e.tensor_tensor(P[:, 0:2, 0:ncols], P[:, 0:2, 0:ncols], mb, ALU.mult), reads=(P_b, cb), writes=(P_b,))
                for qi, qb in enumerate(qbs):
                    bk = 4 + (qb % 4)
                    is_prev = (qb != jj)
                    fns = []
                    for hh in range(2):
                        h = 2 * hp + hh
                        first = is_prev and h == 0
                        fns.append(lambda e, h=h, hh=hh, first=first: e.matmul(ps[bk][0:64, h * 128:(h + 1) * 128], vsrc[:, h * 64:(h + 1) * 64],
                                                                               P[:, hh, qi * 128:(qi + 1) * 128],
                                                                               start=first, stop=(not is_prev), skip_group_check=True, tile_position=(0, 0)))
                        fns.append(lambda e, h=h, hh=hh, first=first: e.matmul(ps[bk][64:128, h * 128:(h + 1) * 128], ones64,
                                                                               P[:, hh, qi * 128:(qi + 1) * 128],
                                                                               start=first, stop=(not is_prev), skip_group_check=True, tile_position=(0, 64)))
                    tr.group("pe", fns, reads=(vbuf, P_b, cb), writes=(psb[bk],))
                    if (not is_prev) and hp == 1:
                        tlo = 128 * d * qb + r
                        tsl = _sl(tlo, 128, d)
                        ub = UD_b[(tlo // 128):((tlo + 127 * d) // 128) + 1]
                        src = ps[bk][:].rearrange("p (h q) -> p h q", h=4)
                        if g == 0:
                            tr.op("act", lambda e: e.copy(UD[:, :, tsl], src), reads=(psb[bk],), writes=tuple(ub))
                        else:
                            tr.op("dve", lambda e: e.tensor_tensor(UD[:, :, tsl], src, UD[:, :, tsl], ALU.add), reads=(psb[bk],) + tuple(ub), writes=tuple(ub))

            a_scores(0)
            for k in range(len(units)):
                if k + 1 < len(units):
                    a_scores(k + 1)
                a_rest(k)
                if k % 8 == 7:
                    copy_some(1, dep=psb[(k % 2) * 2])
        if not stop("proj"):
            for c0 in range(0, TP, 512):
                ub = tuple(UD_b[c0 // 128:c0 // 128 + 4])
                for h in range(4):
                    tr.op("act", lambda e: e.activation(rt[h][0:64, :], UD[64:128, h, c0:c0 + 512], AF.Ln), reads=ub, writes=(rt_b[h],))
                for h in range(4):
                    tr.op("act", lambda e: e.activation(rt[h][0:64, :], rt[h][0:64, :], AF.Exp, scale=-1.0), reads=(rt_b[h],), writes=(rt_b[h],))
                for h in range(4):
                    ph = (h % 2) * 64
                    tr.op("dve", lambda e: e.tensor_tensor(yaT[ph:ph + 64, h // 2, c0:c0 + 512], UD[0:64, h, c0:c0 + 512], rt[h][0:64, :], ALU.mult),
                          reads=ub + (rt_b[h],), writes=(yaT_b,))
        tr.pop(); ar.pop()
        tr.pop(); ar.pop()
        if stop("attn"):
            dsem = tr.new_sem("dbg")
            tr.dma("pool", dsem, dbg_o[:, 0:2 * T].rearrange("p (a b) -> p a b", a=2)[:, :, 0:TP], yaT[:, :, 0:TP], reads=(yaT_b,))
            tr.wait_all("sp")
            return nc

        ar.push(); tr.push()
        NBK = (1, 4, 8)
        BOFF = (0, 1, 5)
        KV = [Slot("kvg%d" % i, ar.alloc(BF16, 13, 520)) for i in range(4)]
        for i in range(4):
            tr.op("dve", lambda e, i=i: e.memset(KV[i].ap[:, :, 512:513], 1.0), writes=(KV[i].b,))
        KTu2 = [ar.alloc(BF16, 26, 128) for _ in range(2)]
        KTu2_b = tr.bufs("KTu", 2)
        Qbd = ar.alloc(BF16, 3, 2, NSEQ, 32)
        Qbd_b = tr.buf("Qbd")
        kTn = ar.alloc(BF16, 3, 2, TS)
        tr.op("dve", lambda e: e.memset(Qbd, 0.0), writes=(Qbd_b,))
        for g in range(3):
            for hf in range(2):
                for h in range(4):
                    c_, hh = h // 2, h % 2
                    pd = hh * 64 + hf * 32
                    tr.op("act", lambda e: e.copy(Qbd[pd:pd + 32, g, c_, :, 8 * h:8 * h + 8],
                                                  qTs[g][32 * h:32 * h + 32, hf, :].rearrange("p (n t) -> p n t", t=8)),
                          reads=(st_b[g],), writes=(Qbd_b,))
                    tr.op("act", lambda e: e.copy(kTn[pd:pd + 32, g, c_, :], kTs[g][32 * h:32 * h + 32, hf, :]),
                          reads=(st_b[g],), writes=(Qbd_b,))
        Ps = [ar.alloc(BF16, 16, 32) for _ in range(2)]
        Ps_b = tr.bufs("Ps", 2)
        tmpd = ar.alloc(F32, 4, 64)
        usum = ar.alloc(F32, 64)
        recs = ar.alloc(F32, 1)
        yas = ar.alloc(F32, 64)
        ep_b = tr.buf("sep")
        yas_b = tr.buf("yas")
        rr5 = [0]

        def nb5():
            return next_bank()

        s_bank = {}

        def sample_D(n):
            kv = KV[n % 4]
            srcs = [caches[0][n].rearrange("(p b) c -> p b c", b=1),
                    caches[1][n].rearrange("(p b) c -> p b c", b=4),
                    caches[2][n].rearrange("(p b) c -> p b c", b=16)[:, 0:8, :]]
            for g in range(3):
                dstv = kv.ap[:, BOFF[g]:BOFF[g] + NBK[g], 0:512]
                if g == 0:
                    tr.dma("pool", kv.sem, dstv, srcs[g], writes=(kv.b,))
                else:
                    tr.dma("pool", kv.sem, dstv, srcs[g])
            kv.b.w = {kv.sem: tr.cnt[kv.sem]}
            kv.b.r = {}

        def sample_A(n):
            kv = KV[n % 4]
            KTu, KTu_b = KTu2[n % 2], KTu2_b[n % 2]
            for m in range(7):
                units = list(range(4 * m, min(4 * m + 4, 26)))
                bk = nb5()
                pbf = ps[bk][:].bitcast(BF16)
                fns = []
                for qi, u in enumerate(units):
                    b, hf = u // 2, u % 2
                    src = kv.ap[:, b, hf * 128:(hf + 1) * 128]
                    fns.append(lambda e, qi=qi, src=src: e.transpose(pbf[:, qi * 128:(qi + 1) * 128], src, identb))
                tr.group("pe", fns, reads=(kv.b, cb), writes=(psb[bk],))
                nu = len(units)
                dst = KTu[:, 4 * m:4 * m + nu, :]
                srcp = pbf[:, 0:nu * 128].rearrange("p (a b) -> p a b", a=nu)
                tr.op("act", lambda e: e.copy(dst, srcp), reads=(psb[bk],), writes=(KTu_b,))
        def sample_B(n):
            KTu, KTu_b = KTu2[n % 2], KTu2_b[n % 2]
            sb_ = nb5()
            fns = []
            for u in range(16):
                if u < 13:
                    g = 0 if u == 0 else (1 if u < 5 else 2)
                else:
                    g = u - 13
                for hf in range(2):
                    lhs = KTu[:, 2 * u + hf, :] if u < 13 else kTn[:, g, hf, :]
                    fns.append(lambda e, u=u, hf=hf, lhs=lhs, g=g: e.matmul(ps[sb_][:, u * 32:(u + 1) * 32], lhs, Qbd[:, g, hf, n, :],
                                                                         start=(hf == 0), stop=(hf == 1)))
            tr.group("pe", fns, reads=(KTu_b, Qbd_b, st_b[0], st_b[1], st_b[2]), writes=(psb[sb_],))
            P = Ps[n % 2]
            tr.op("act", lambda e: e.activation(P, ps[sb_][:].rearrange("p (u c) -> p u c", u=16), AF.Exp, scale=SCALE),
                  reads=(psb[sb_],), writes=(Ps_b[n % 2],))
            tr.op("dve", lambda e: e.tensor_tensor(P[:, 0:13, :], P[:, 0:13, :], smc, ALU.mult), reads=(Ps_b[n % 2], cb), writes=(Ps_b[n % 2],))
            tr.op("dve", lambda e: e.tensor_tensor(P[:, 13:16, :], P[:, 13:16, :], smn[:, n, :, :], ALU.mult), reads=(Ps_b[n % 2], cb), writes=(Ps_b[n % 2],))
        def sample_C(n):
            kv = KV[n % 4]
            nl = n % 4
            ob = 7
            P = Ps[n % 2]
            fns = []
            for u in range(16):
                if u < 13:
                    rhs = kv.ap[:, u, 256:513]
                else:
                    rhs = Vs[u - 13][:, 0:257]
                fns.append(lambda e, u=u, rhs=rhs: e.matmul(ps[ob][32 * nl:32 * nl + 32, 0:257], P[:, u, :], rhs,
                                                          start=(u == 0), stop=(u == 15), tile_position=(0, 32 * nl)))
            tr.group("pe", fns, reads=(Ps_b[n % 2], kv.b, st_b[0], st_b[1], st_b[2]), writes=(psb[ob],))
            if nl == 3:
                bt = n // 4
                tr.op("dve", lambda e: e.tensor_tensor(tmpd, ps[ob][:, 0:256].rearrange("p (h e) -> p h e", h=4), maskd, ALU.mult),
                      reads=(psb[ob], cb), writes=(ep_b,))
                tr.op("dve", lambda e: e.tensor_reduce(usum, tmpd.rearrange("p h e -> p e h"), AX.X, ALU.add), reads=(ep_b,), writes=(ep_b,))
                tr.op("dve", lambda e: e.reciprocal(recs, ps[ob][:, 256:257]), reads=(psb[ob],), writes=(ep_b,))
                tr.op("dve", lambda e: e.tensor_scalar(yas, usum, recs[:, 0:1], None, ALU.mult), reads=(ep_b,), writes=(yas_b,))
                tb_ = next_bank()
                tr.group("pe", [lambda e: e.transpose(ps[tb_][0:64, 0:128], yas, identf)], reads=(yas_b, cb), writes=(psb[tb_],))
                for h in range(4):
                    ph = (h % 2) * 64
                    c0 = TP + 32 * bt
                    tr.op("act", lambda e: e.copy(yaT[ph:ph + 64, h // 2, c0:c0 + 32].rearrange("p (a b) -> p a b", a=4),
                                                  ps[tb_][0:64, 0:128].rearrange("p (a h b) -> p a h b", a=4, h=4)[:, :, h, :]),
                          reads=(psb[tb_],), writes=(yaT_b,))
        def dbg_dump(src, nch, bufs):
            dsem = tr.new_sem("dbg")
            tr.dma("pool", dsem, dbg_o[:, 0:nch * T].rearrange("p (a b) -> p a b", a=nch), src, reads=tuple(bufs))
            tr.wait_all("sp")

        ycT = ar.alloc(BF16, 4, T, keep=True)
        ycT_b = tr.bufs("ycT", 4)
        ss_next = [0]

        def sample_step_pre():
            i = ss_next[0]
            if i < NSEQ:
                sample_A(i)

        def sample_step():
            i = ss_next[0]
            ss_next[0] += 1
            if 0 <= i - 1 < NSEQ:
                sample_B(i - 1)
            if 0 <= i - 2 < NSEQ:
                sample_C(i - 2)
            if i + 2 < NSEQ:
                sample_D(i + 2)

        sample_D(0)
        sample_D(1)
        pf_every[0] = 6
        bank_limit[0] = 7
        ar.push(); tr.push()
        wcv_s = [Slot("wcv%d" % i, ar.alloc(BF16, 3, KC, 128)) for i in range(2)]
        uext = ar.alloc(F32, 2 + TP)
        us = ar.alloc(F32, NSEQ, 10)
        u_b = tr.buf("u")
        hsb = [ar.alloc(F32, 512) for _ in range(2)]
        hsb_b = tr.bufs("hsb", 2)
        ct = [ar.alloc(F32, 512) for _ in range(2)]
        ct_b = tr.bufs("ct", 2)
        ust = ar.alloc(F32, 32)
        ust_b = tr.buf("ust")
        stsb = Slot("stsb", ar.alloc(F32, 512))
        tr.dma("sp", stsb.sem, stsb.ap[0:32, :], stc, writes=(stsb.b,))
        cstp = Slot("cstp", ar.alloc(F32, 512))
        csts = Slot("csts", ar.alloc(F32, 512))
        hk = [0]
        for i in range(4):
            w = wcv_s[i % 2]
            load(w, wcv[i].rearrange("s (kc p) c -> p s kc c", p=128))
            bC = next_bank()
            tr.group("pe", [(lambda e, kc=kc: e.matmul(ps[bC][:, 0:2], w.ap[:, 1, kc, :], xpl2[:, kc, :], start=(kc == 0), stop=(kc == KC - 1)))
                            for kc in range(KC)], reads=(w.b, xpl2_b), writes=(psb[bC],))
            bH = next_bank()
            tr.group("pe", [(lambda e, kc=kc: e.matmul(ps[bH][:, 0:2], w.ap[:, 2, kc, :], xpl2[:, kc, :], start=(kc == 0), stop=(kc == KC - 1)))
                            for kc in range(KC)], reads=(w.b, xpl2_b), writes=(psb[bH],))
            hs, hs_b = hsb[hk[0] % 2], hsb_b[hk[0] % 2]
            hk[0] += 1
            tr.op("act", lambda e: e.copy(hs[:, 0:2], ps[bH][:, 0:2]), reads=(psb[bH],), writes=(hs_b,))
            tr.op("dve", lambda e: e.tensor_tensor(uext[:, 0:2], ps[bC][:, 0:2], hs[:, 0:2], ALU.mult), reads=(psb[bC], hs_b), writes=(u_b,))
            bk = next_bank()
            tr.group("pe", [lambda e: e.transpose(ps[bk][:, 0:32], stsb.ap[0:32, i * 128:(i + 1) * 128], identf[0:32, 0:32])],
                     reads=(stsb.b, cb), writes=(psb[bk],))
            tr.op("act", lambda e: e.copy(us[:, :, 0:2], ps[bk][:, 0:32].rearrange("p (n j) -> p n j", j=2)), reads=(psb[bk],), writes=(u_b,))
            for bi, (t0, n) in enumerate(BLOCKS):
                sample_step_pre()
                xb = xtiles(t0, n)
                sample = t0 >= TP
                bC = proj_fm(w, 1, xT, xb, t0, n)
                bH = proj_fm(w, 2, xT, xb, t0, n)
                hs, hs_b = hsb[hk[0] % 2], hsb_b[hk[0] % 2]
                hk[0] += 1
                tr.op("act", lambda e: e.copy(hs[:, 0:n], ps[bH][:, 0:n]), reads=(psb[bH],), writes=(hs_b,))
                if sample:
                    v3 = lambda a: a.rearrange("p (n t) -> p n t", t=8)
                    tr.op("dve", lambda e: e.tensor_tensor(us[:, :, 2:10], v3(ps[bC][:, 0:n]), v3(hs[:, 0:n]), ALU.mult),
                          reads=(psb[bC], hs_b), writes=(u_b,))
                    s0, s1, s2 = us[:, :, 0:8], us[:, :, 1:9], us[:, :, 2:10]
                    c0, c1 = v3(ct[0][:, 0:n]), v3(ct[1][:, 0:n])
                else:
                    tr.op("dve", lambda e: e.tensor_tensor(uext[:, 2 + t0:2 + t0 + n], ps[bC][:, 0:n], hs[:, 0:n], ALU.mult),
                          reads=(psb[bC], hs_b), writes=(u_b,))
                    s0, s1, s2 = uext[:, t0:t0 + n], uext[:, t0 + 1:t0 + 1 + n], uext[:, t0 + 2:t0 + 2 + n]
                    c0, c1 = ct[0][:, 0:n], ct[1][:, 0:n]
                tr.op("act", lambda e: e.activation(c0, s0, AF.Copy, scale=cw[:, 3 * i:3 * i + 1]), reads=(u_b, cb), writes=(ct_b[0],))
                tr.op("dve", lambda e: e.scalar_tensor_tensor(c1, s1, cw[:, 3 * i + 1:3 * i + 2], c0, ALU.mult, ALU.add),
                      reads=(u_b, cb, ct_b[0]), writes=(ct_b[1],))
                tr.op("dve", lambda e: e.scalar_tensor_tensor(c0, s2, cw[:, 3 * i + 2:3 * i + 3], c1, ALU.mult, ALU.add),
                      reads=(u_b, cb, ct_b[1]), writes=(ct_b[0],))
                bB = proj_fm(w, 0, xT, xb, t0, n)
                tr.op("dve", lambda e: e.tensor_tensor(ycT[:, i, t0:t0 + n], ps[bB][:, 0:n], ct[0][:, 0:n], ALU.mult),
                      reads=(psb[bB], ct_b[0]), writes=(ycT_b[i],))
                sample_step()
            bk = next_bank()
            tr.group("pe", [lambda e: e.transpose(ps[bk][0:2, 0:128], uext[:, TP:TP + 2], identf)], reads=(u_b, cb), writes=(psb[bk],))
            tr.op("act", lambda e: e.copy(cstp.ap[0:2, i * 128:(i + 1) * 128], ps[bk][0:2, 0:128]), reads=(psb[bk],), writes=(cstp.b,))
            bk2 = next_bank()
            tr.op("act", lambda e: e.copy(ust.rearrange("p (n j) -> p n j", j=2), us[:, :, 8:10]), reads=(u_b,), writes=(ust_b,))
            tr.group("pe", [lambda e: e.transpose(ps[bk2][0:32, 0:128], ust, identf)], reads=(ust_b, cb), writes=(psb[bk2],))
            tr.op("act", lambda e: e.copy(csts.ap[0:32, i * 128:(i + 1) * 128], ps[bk2][0:32, 0:128]), reads=(psb[bk2],), writes=(csts.b,))
        tr.dma("sp", cstp.sem, convp_o, cstp.ap[0:2, :], reads=(cstp.b,))
        tr.dma("sp", csts.sem, convs_o, csts.ap[0:32, :], reads=(csts.b,))
        while ss_next[0] < NSEQ + 2:
            sample_step_pre()
            sample_step()
        tr.pop(); ar.pop()
        tr.pop(); ar.pop()
        bank_limit[0] = 8
        pf_every[0] = 8
        if stop("conv"):
            dbg_dump(ycT, 4, ycT_b)
            return nc

        mixT = ar.alloc(BF16, 8, T, keep=True, high=True)
        mixT_b = tr.bufs("mixT", len(BLOCKS))
        ar.push(); tr.push()
        Wc = Slot("wco", ar.alloc(BF16, 4, D))
        Wa = Slot("wao", ar.alloc(BF16, 2, D))
        load(Wc, wco.rearrange("(kc p) c -> p kc c", p=128))
        load(Wa, wao.rearrange("(kc p) c -> p kc c", p=128))
        wg_s = [Slot("wg%d" % i, ar.alloc(BF16, 2, KC, 128)) for i in range(2)]
        gt = [ar.alloc(F32, 512) for _ in range(4)]
        gt_b = tr.bufs("gt", 4)
        for f in range(8):
            wgs = wg_s[f % 2]
            load(wgs, wg[f].rearrange("s (kc p) c -> p s kc c", p=128))
            fs = slice(f * 128, (f + 1) * 128)
            for bi, (t0, n) in enumerate(BLOCKS):
                xb = xtiles(t0, n)
                b1 = next_bank()
                tr.group("pe", [(lambda e, c=c: e.matmul(ps[b1][:, 0:n], Wc.ap[:, c, fs], ycT[:, c, t0:t0 + n], start=(c == 0), stop=(c == 3)))
                                for c in range(4)], reads=(Wc.b,) + tuple(ycT_b), writes=(psb[b1],))
                b2 = next_bank()
                tr.group("pe", [(lambda e, c=c: e.matmul(ps[b2][:, 0:n], Wa.ap[:, c, fs], yaT[:, c, t0:t0 + n], start=(c == 0), stop=(c == 1)))
                                for c in range(2)], reads=(Wa.b, yaT_b), writes=(psb[b2],))
                b3 = proj_fm(wgs, 0, xT, xb, t0, n)
                b4 = proj_fm(wgs, 1, xT, xb, t0, n)
                tr.op("act", lambda e: e.activation(gt[0][:, 0:n], ps[b3][:, 0:n], AF.Sigmoid, bias=bg[:, f:f + 1]), reads=(psb[b3], cb), writes=(gt_b[0],))
                tr.op("act", lambda e: e.activation(gt[1][:, 0:n], ps[b4][:, 0:n], AF.Sigmoid, bias=bg[:, 8 + f:9 + f]), reads=(psb[b4], cb), writes=(gt_b[1],))
                tr.op("dve", lambda e: e.tensor_tensor(gt[2][:, 0:n], ps[b1][:, 0:n], gt[0][:, 0:n], ALU.mult), reads=(psb[b1], gt_b[0]), writes=(gt_b[2],))
                tr.op("dve", lambda e: e.tensor_tensor(gt[3][:, 0:n], ps[b2][:, 0:n], gt[1][:, 0:n], ALU.mult), reads=(psb[b2], gt_b[1]), writes=(gt_b[3],))
                tr.op("dve", lambda e: e.tensor_tensor(mixT[:, f, t0:t0 + n], gt[2][:, 0:n], gt[3][:, 0:n], ALU.add), reads=(gt_b[2], gt_b[3]), writes=(mixT_b[bi],))
        tr.pop(); ar.pop()
        if stop("mix"):
            dbg_dump(mixT, 8, mixT_b)
            return nc
        tr.retire(xT_b + [xpl2_b, yaT_b] + st_b + ycT_b)
        for a_ in [xT, xpl2, yaT, ycT] + qTs + kTs + Vs:
            ar.free(a_)

        X1 = ar.alloc(F32, T // 128, D, keep=True)
        X1_b = tr.bufs("X1", T // 128)
        x1T = ar.alloc(BF16, KC, T, keep=True)
        x1T_b = tr.bufs("x1T", T // 128)
        lnt = Slot("lnt", ar.alloc(F32, 2, D, keep=True))

        def load_ln(j0):
            for j in range(2):
                tr.dma("sp", lnt.sem, lnt.ap[:, j, :], lnp[j0 + j].partition_broadcast(128), writes=(lnt.b,) if j == 0 else ())
            lnt.b.w = {lnt.sem: tr.cnt[lnt.sem]}
            lnt.b.r = {}

        load_ln(0)
        lns3 = [ar.alloc(F32, 16, keep=True) for _ in range(3)]
        lns3_b = tr.bufs("lns", 3)

        def ln_stats(src, src_bufs, k):
            lns, lns_b = lns3[k % 3], lns3_b[k % 3]
            stv = lns[:, 0:12].rearrange("p (a b) -> p a b", a=2)
            tr.op("dve", lambda e: e.bn_stats(stv[:, 0, :], src[:, 0:512]), reads=tuple(src_bufs), writes=(lns_b,))
            tr.op("dve", lambda e: e.bn_stats(stv[:, 1, :], src[:, 512:1024]), reads=tuple(src_bufs), writes=(lns_b,))
            tr.op("dve", lambda e: e.bn_aggr(lns[:, 12:14], lns[:, 0:12]), reads=(lns_b,), writes=(lns_b,))
            tr.op("act", lambda e: e.activation(lns[:, 14:15], lns[:, 13:14], AF.Sqrt, bias=epsb[:, 0:1]), reads=(lns_b, epsb_b), writes=(lns_b,))
            tr.op("dve", lambda e: e.reciprocal(lns[:, 14:15], lns[:, 14:15]), reads=(lns_b,), writes=(lns_b,))
            tr.op("dve", lambda e: e.scalar_tensor_tensor(lns[:, 15:16], lns[:, 12:13], -1.0, lns[:, 14:15], ALU.mult, ALU.mult), reads=(lns_b,), writes=(lns_b,))

        def ln_apply(src, src_bufs, dst, dst_bufs, k):
            lns, lns_b = lns3[k % 3], lns3_b[k % 3]
            tr.op("act", lambda e: e.activation(dst, src, AF.Identity, bias=lns[:, 15:16], scale=lns[:, 14:15]),
                  reads=tuple(src_bufs) + (lns_b,), writes=tuple(dst_bufs))
            tr.op("pool", lambda e: e.tensor_tensor(dst, dst, lnt.ap[:, 0, :], ALU.mult), reads=tuple(dst_bufs) + (lnt.b,), writes=tuple(dst_bufs))
            tr.op("pool", lambda e: e.tensor_tensor(dst, dst, lnt.ap[:, 1, :], ALU.add), reads=tuple(dst_bufs) + (lnt.b,), writes=tuple(dst_bufs))

        ar.push(); tr.push()
        Wo = Slot("wo", ar.alloc(BF16, KC, D))
        load(Wo, wo.rearrange("(kc p) c -> p kc c", p=128))
        xf = [Slot("xf%d" % i, ar.alloc(F32, D)) for i in range(3)]
        pre3 = [ar.alloc(F32, D) for _ in range(3)]
        pre3_b = tr.bufs("pre", 3)
        x1b = [ar.alloc(BF16, D) for _ in range(3)]
        x1b_b = tr.bufs("x1b", 3)
        NT = T // 128

        def b_tr(tt):
            tsl = slice(tt * 128, (tt + 1) * 128)
            xb_, xbb = x1b[tt % 3], x1b_b[tt % 3]
            bk = next_bank()
            pbf = ps[bk][:].bitcast(BF16)
            tr.group("pe", [(lambda e, kc=kc: e.transpose(pbf[:, kc * 128:(kc + 1) * 128], xb_[:, kc * 128:(kc + 1) * 128], identb))
                            for kc in range(KC)], reads=(xbb, cb), writes=(psb[bk],))
            tr.op("act", lambda e: e.copy(x1T[:, :, tsl], pbf.rearrange("p (a b) -> p a b", a=KC)), reads=(psb[bk],), writes=(x1T_b[tt],))

        def b_gb(tt):
            dst, dst_bufs = X1[:, tt, :], (X1_b[tt],)
            tr.op("dve", lambda e: e.tensor_tensor(dst, dst, lnt.ap[:, 0, :], ALU.mult), reads=dst_bufs + (lnt.b,), writes=dst_bufs)
            tr.op("pool", lambda e: e.tensor_tensor(dst, dst, lnt.ap[:, 1, :], ALU.add), reads=dst_bufs + (lnt.b,), writes=dst_bufs)

        def b_mm(tt):
            if tt % 2 == 0:
                copy_some(1, dep=X1_b[max(tt - 3, 0)])
            xs_ = xf[tt % 3]
            pre, pre_b = pre3[tt % 3], pre3_b[tt % 3]
            tsl = slice(tt * 128, (tt + 1) * 128)
            tr.dma("sp", xs_.sem, xs_.ap, xm[tsl, :], writes=(xs_.b,))
            mb = mixT_b[min(tt // 4, 4)]
            for hf in range(2):
                cs_ = slice(hf * 512, (hf + 1) * 512)
                bk = next_bank()
                tr.group("pe", [(lambda e, f=f: e.matmul(ps[bk][:, :], mixT[:, f, tsl], Wo.ap[:, f, cs_], start=(f == 0), stop=(f == 7)))
                                for f in range(8)], reads=(mb, Wo.b), writes=(psb[bk],))
                tr.op("dve", lambda e: e.scalar_tensor_tensor(pre[:, cs_], xs_.ap[:, cs_], ALPHA, ps[bk][:, :], ALU.mult, ALU.add),
                      reads=(xs_.b, psb[bk]), writes=(pre_b,))
            ln_stats(pre, (pre_b,), tt)
            lns, lns_b = lns3[tt % 3], lns3_b[tt % 3]
            tr.op("act", lambda e: e.activation(X1[:, tt, :], pre, AF.Identity, bias=lns[:, 15:16], scale=lns[:, 14:15]),
                  reads=(pre_b, lns_b), writes=(X1_b[tt],))

        def b_cast(tt):
            xb_, xbb = x1b[tt % 3], x1b_b[tt % 3]
            tr.op("act", lambda e: e.copy(xb_, X1[:, tt, :]), reads=(X1_b[tt],), writes=(xbb,))

        for j in range(NT + 4):
            if 0 <= j - 4 < NT:
                b_tr(j - 4)
            if 0 <= j - 2 < NT:
                b_cast(j - 2)
            if 0 <= j - 1 < NT:
                b_gb(j - 1)
            if j < NT:
                b_mm(j)
        tr.pop(); ar.pop()
        tr.retire(mixT_b)
        ar.free(mixT)
        load_ln(2)

        ar.push(); tr.push()
        yst = [Slot("yst%d" % i, ar.alloc(F32, D)) for i in range(2)]
        hidT = ar.alloc(BF16, 4, T)
        hid_b = tr.bufs("hid", len(BLOCKS))
        hidT2 = ar.alloc(BF16, 4, T)
        hid2_b = tr.bufs("hid2", len(BLOCKS))
        wup_s = [Slot("wup%d" % i, ar.alloc(BF16, KC, 512)) for i in range(2)]
        wdn_s = [Slot("wdn%d" % i, ar.alloc(BF16, 4, D)) for i in range(2)]
        rl = [ar.alloc(F32, 512) for _ in range(2)]
        rl_b = tr.bufs("rl", 2)
        rk = [0]
        wup_v = wup.rearrange("(kc p) c -> p kc c", p=128)

        def c1(G, wu, fi, bi, t0, n, hidT=hidT, hid_b=hid_b):
            bk = next_bank()
            if rk[0] % 5 == 0:
                copy_some(1, dep=psb[bk])
            tr.group("pe", [(lambda e, kc=kc: e.matmul(ps[bk][:, 0:n], wu.ap[:, kc, fi * 128:(fi + 1) * 128], x1T[:, kc, t0:t0 + n],
                                                       start=(kc == 0), stop=(kc == KC - 1)))
                            for kc in range(KC)], reads=(wu.b,) + tuple(x1T_b[t0 // 128:(t0 + n) // 128]), writes=(psb[bk],))
            r_, r_b = rl[rk[0] % 2], rl_b[rk[0] % 2]
            rk[0] += 1
            tr.op("act", lambda e: e.activation(r_[:, 0:n], ps[bk][:, 0:n], AF.Relu), reads=(psb[bk],), writes=(r_b,))
            tr.op("dve", lambda e: e.tensor_tensor(hidT[:, fi, t0:t0 + n], r_[:, 0:n], r_[:, 0:n], ALU.mult), reads=(r_b,), writes=(hid_b[bi],))

        def c2(G, wd, tt, hidT=hidT, hid_b=hid_b):
            tsl = slice(tt * 128, (tt + 1) * 128)
            hb = hid_b[min(tt // 4, 4)]
            for hf in range(2):
                cs_ = slice(hf * 512, (hf + 1) * 512)
                bk = next_bank()
                tr.group("pe", [(lambda e, fi=fi: e.matmul(ps[bk][:, :], hidT[:, fi, tsl], wd.ap[:, fi, cs_], start=(fi == 0), stop=(fi == 3)))
                                for fi in range(4)], reads=(hb, wd.b), writes=(psb[bk],))
                if G == 0:
                    tr.op("dve", lambda e: e.scalar_tensor_tensor(X1[:, tt, cs_], X1[:, tt, cs_], ALPHA, ps[bk][:, :], ALU.mult, ALU.add),
                          reads=(X1_b[tt], psb[bk]), writes=(X1_b[tt],))
                else:
                    tr.op("dve", lambda e: e.tensor_tensor(X1[:, tt, cs_], ps[bk][:, :], X1[:, tt, cs_], ALU.add),
                          reads=(X1_b[tt], psb[bk]), writes=(X1_b[tt],))

        def y_out(tt):
            ys = yst[tt % 2]
            ln_apply(X1[:, tt, :], (X1_b[tt],), ys.ap, (ys.b,), tt)
            tr.dma("sp", ys.sem, y_o[tt * 128:(tt + 1) * 128, :], ys.ap, reads=(ys.b,))

        for G in range(8):
            wu, wd = wup_s[G % 2], wdn_s[G % 2]
            load(wu, wup_v[:, :, G * 512:(G + 1) * 512])
            load(wd, wdn[G * 512:(G + 1) * 512, :].rearrange("(fc p) c -> p fc c", p=128))
            if G < 6:
                for fi in range(4):
                    for bi, (t0, n) in enumerate(BLOCKS):
                        c1(G, wu, fi, bi, t0, n)
                for tt in range(NT):
                    c2(G, wd, tt)
            elif G == 6:
                continue
            else:
                wu6, wd6 = wup_s[0], wdn_s[0]

                def c1_both(bi):
                    t0, n = BLOCKS[bi]
                    for fi in range(4):
                        c1(6, wu6, fi, bi, t0, n)
                    for fi in range(4):
                        c1(7, wu, fi, bi, t0, n, hidT2, hid2_b)

                prev = None
                c1_both(0)
                for bi, (t0, n) in enumerate(BLOCKS):
                    if bi + 1 < len(BLOCKS):
                        c1_both(bi + 1)
                    for tt in range(t0 // 128, (t0 + n) // 128):
                        c2(6, wd6, tt)
                        c2(7, wd, tt, hidT2, hid2_b)
                        ln_stats(X1[:, tt, :], (X1_b[tt],), tt)
                        if prev is not None:
                            y_out(prev)
                        prev = tt
                y_out(prev)
        tr.pop(); ar.pop()

        copy_some(len(cp_pending))
        tr.wait_all("sp")
        print("build: waits=%d pe=%d act=%d dve=%d arena_peak=%d" % (tr.n_wait, tr.cnt["pe"], tr.cnt["act"], tr.cnt["dve"], ar.peak))
    return nc


def _rope_tables(pos):
    inv = 1.0 / (10000.0 ** (np.arange(0, 64, 2, dtype=np.float64) / 64.0))
    ang = np.asarray(pos, np.float64)[None, :] * inv[:, None]
    c = np.cos(ang).astype(np.float32)
    s = np.sin(ang).astype(np.float32)
    return np.ascontiguousarray(np.tile(c, (4, 1))), np.ascontiguousarray(np.tile(s, (4, 1)))


def _host_constants():
    ident = np.eye(128, dtype=np.float32)
    k = np.arange(128)[:, None]
    q = np.arange(128)[None, :]
    m_cur = (k <= q).astype(np.float32)
    m_prev = (k >= q).astype(np.float32)
    t = np.tile(np.arange(8), 4)[None, :]
    p = np.arange(128)[:, None]
    smc = np.zeros((128, 13, 32), np.float32)
    smc[:, 0, :] = (p >= t)
    for b in range(4):
        smc[:, 1 + b, :] = ((t % 4) == b) & ((4 * p + b) >= t)
    for b in range(8):
        smc[:, 5 + b, :] = (t == b)
    smn = np.zeros((128, NSEQ, 3, 32), np.float32)
    n_ = (np.arange(128) // 8)[:, None]
    t_ = (np.arange(128) % 8)[:, None]
    for n in range(NSEQ):
        same = (n_ == n)
        for g, dl in enumerate(DIL):
            smn[:, n, g, :] = same & (t_ <= t) & (((t - t_) % dl) == 0)
    hp = (np.arange(128) % 32) // 8
    maskd = np.zeros((128, 4, 64), np.float32)
    for h in range(4):
        maskd[hp == h, h, :] = 1.0
    return ident, m_cur, m_prev, smc.reshape(128, -1), smn.reshape(128, -1), maskd.reshape(128, -1)


def _prep_inputs(inp):
    f = np.float32
    w_in = np.asarray(inp["w_in"][0], f)
    wqk = np.empty((3, 4, D, 128), f)
    wv = np.empty((3, D, 256), f)
    for g in range(3):
        off = 1536 + g * 768
        for j in range(4):
            base = off + (256 if j >= 2 else 0) + (32 if j % 2 else 0)
            cols = (base + np.arange(4)[:, None] * 64 + np.arange(32)[None, :]).reshape(-1)
            wqk[g, j] = w_in[:, cols]
        wv[g] = w_in[:, off + 512:off + 768]
    wcv = np.empty((4, 3, D, 128), f)
    for i in range(4):
        for s in range(3):
            wcv[i, s] = w_in[:, s * 512 + i * 128:s * 512 + (i + 1) * 128]
    wg = np.empty((8, 2, D, 128), f)
    for i in range(8):
        for s in range(2):
            wg[i, s] = w_in[:, 3840 + s * 1024 + i * 128:3840 + s * 1024 + (i + 1) * 128]
    bgt = np.ascontiguousarray(np.asarray(inp["b_gate"][0], f).reshape(16, 128).T)
    cwt = np.ascontiguousarray(np.asarray(inp["conv_w"][0], f).reshape(3, 4, 128).transpose(2, 1, 0).reshape(128, 12))
    lnp = np.stack([np.asarray(inp[k][0], f) for k in ("ln1_g", "ln1_b", "ln2_g", "ln2_b")])
    ident, m_cur, m_prev, smc, smn, maskd = _host_constants()
    shared = dict(wqk=wqk, wv=wv, wcv=wcv, wg=wg, wco=np.asarray(inp["w_conv_out"][0], f), wao=np.asarray(inp["w_attn_out"][0], f),
                  wo=np.asarray(inp["w_o"][0], f), wup=np.asarray(inp["w_up"][0], f), wdn=np.asarray(inp["w_down"][0], f),
                  bgt=bgt, cwt=cwt, lnp=lnp, identf=ident, smc=smc, smn=smn, maskd=maskd)
    xpr = np.asarray(inp["x_prompt"], f)
    xs = np.asarray(inp["x_sample"], f)
    maps = []
    for c in range(NCORES):
        s, half = c // 2, c % 2
        p0 = half * TP
        m = dict(shared)
        m["xm"] = np.ascontiguousarray(np.concatenate([xpr[s, p0:p0 + TP], xs[c * NSEQ:(c + 1) * NSEQ].reshape(TS, D)], axis=0))
        m["xp"] = np.ascontiguousarray(xpr[s, 0:PF]) if half == 1 else np.zeros((PF, D), f)
        pos_m = np.concatenate([p0 + np.arange(TP), np.tile(2048 + np.arange(8), NSEQ)])
        pos_p = (np.arange(PF) if half == 1 else np.zeros(PF))
        m["cosm"], m["sinm"] = _rope_tables(pos_m)
        m["cosp"], m["sinp"] = _rope_tables(pos_p)
        m["masks"] = np.ascontiguousarray(np.concatenate([m_cur, m_prev, m_prev * float(half)], axis=1))
        m["stc"] = np.ascontiguousarray(np.asarray(inp["state_conv"][0, c * NSEQ:(c + 1) * NSEQ], f).reshape(32, 512))
        m["c128"] = np.ascontiguousarray(np.asarray(inp["cache_kv_w128"][0, c * NSEQ:(c + 1) * NSEQ], f).reshape(NSEQ, 128, 512))
        m["c512"] = np.ascontiguousarray(np.asarray(inp["cache_kv_w512"][0, c * NSEQ:(c + 1) * NSEQ], f).reshape(NSEQ, 512, 512))
        m["c2048"] = np.ascontiguousarray(np.asarray(inp["cache_kv_w2048"][0, c * NSEQ:(c + 1) * NSEQ], f).reshape(NSEQ, 2048, 512))
        maps.append(m)
    return maps


def _assemble(res):
    f = np.float32
    y_p = np.empty((4, 4096, D), f)
    y_s = np.empty((128, 8, D), f)
    conv_p = np.empty((1, 4, 2, 512), f)
    kvp = [np.empty((1, 4, w, 2, 4, 64), f) for w in WIN]
    conv_s = np.empty((1, 128, 2, 512), f)
    kvs = [np.empty((1, 128, w, 2, 4, 64), f) for w in WIN]
    names_p = ("kv128p", "kv512p", "kv2048p")
    names_s = ("kv128s", "kv512s", "kv2048s")
    for c in range(NCORES):
        r = res[c]
        s, half = c // 2, c % 2
        y = np.asarray(r["y"])
        y_p[s, half * TP:(half + 1) * TP] = y[:TP]
        y_s[c * NSEQ:(c + 1) * NSEQ] = y[TP:].reshape(NSEQ, 8, D)
        conv_s[0, c * NSEQ:(c + 1) * NSEQ] = np.asarray(r["convs"]).reshape(NSEQ, 2, 512)
        for g in range(3):
            kvs[g][0, c * NSEQ:(c + 1) * NSEQ] = np.asarray(r[names_s[g]]).reshape(NSEQ, WIN[g], 2, 4, 64)
        if half == 1:
            conv_p[0, s] = np.asarray(r["convp"])
            for g in range(3):
                kvp[g][0, s] = np.asarray(r[names_p[g]]).reshape(WIN[g], 2, 4, 64)
    return (y_p, y_s, conv_p, kvp[0], kvp[1], kvp[2], conv_s, kvs[0], kvs[1], kvs[2])


_PROGRAM = {}


DEV_CORES = None


def kernel(**inputs):
    maps = _prep_inputs(inputs)
    key = STOP_AFTER
    if key not in _PROGRAM:
        _PROGRAM[key] = build_program(STOP_AFTER)
    nc = _PROGRAM[key]
    if DEV_CORES:
        res = run_bass_kernel_spmd(nc, maps[:DEV_CORES], core_ids=list(range(DEV_CORES)))
        results = list(res.results) + [res.results[0]] * (NCORES - DEV_CORES)
        return _assemble(results)
    res = run_bass_kernel_spmd(nc, maps, core_ids=list(range(NCORES)))
    return _assemble(res.results)
```
